# Optimizing a Trainium2 kernel written in Bass

```python
import jax, jax.numpy as jnp
from jax import lax
import numpy as np

D_MODEL = 1024
BATCH = 4
SEQ = 8192
DEPTH = 2

HEAD_DIM = 64
ATTN_GROUPS = ((128, 1), (512, 4), (2048, 16))
N_GROUPS = len(ATTN_GROUPS)
HEADS_PER_GROUP = 6
N_ATTN_HEADS = N_GROUPS * HEADS_PER_GROUP
ATTN_WIDTH = N_ATTN_HEADS * HEAD_DIM
ATTN_OUT_WIDTH = HEADS_PER_GROUP * HEAD_DIM
NUM_BUCKETS = 32
MAX_DISTANCE = 2048
RET_HEADS = 4
RET_QK_DIM = 256
RET_V_DIM = 2 * RET_QK_DIM
RET_QK_WIDTH = RET_HEADS * RET_QK_DIM
RET_V_WIDTH = RET_HEADS * RET_V_DIM
RET_CHUNK = 128
ROPE_BASE = 10000.0
D_FF = -(-8 * D_MODEL // (3 * 256)) * 256
ALPHA = (2 * DEPTH) ** 0.25
BETA = (8 * DEPTH) ** -0.25
LN_EPS = 1e-5
GN_EPS = 1e-5
SPLIT_SIZES = (ATTN_WIDTH, ATTN_WIDTH, ATTN_WIDTH,
               RET_QK_WIDTH, RET_QK_WIDTH, RET_V_WIDTH, RET_V_WIDTH,
               D_MODEL, D_MODEL)
IN_COLS = sum(SPLIT_SIZES)
SPLIT_POINTS = tuple(int(v) for v in np.cumsum(SPLIT_SIZES)[:-1])

kernel_name = "hybrid_dilated_attn_retention_deepnorm"


def _t5_bucket(dist):
    max_exact = NUM_BUCKETS // 2
    large = max_exact + (np.log(np.maximum(dist, max_exact) / max_exact)
                         / np.log(MAX_DISTANCE / max_exact)
                         * (NUM_BUCKETS - max_exact)).astype(np.int32)
    large = np.minimum(large, NUM_BUCKETS - 1)
    return np.where(dist < max_exact, dist, large).astype(np.int32)


def _layer_norm(x, g, b):
    xf = x.astype(jnp.float32)
    mu = jnp.mean(xf, axis=-1, keepdims=True)
    var = jnp.mean(jnp.square(xf - mu), axis=-1, keepdims=True)
    return ((xf - mu) * lax.rsqrt(var + LN_EPS) * g + b).astype(x.dtype)


def _dilated_window_attention(q, k, v, bias_table, window, dilation):
    B, S, H, Dh = q.shape
    W = window // dilation
    L = S // dilation
    nb = -(-L // W)
    Lp = nb * W

    def to_sub(t):
        t = t.reshape(B, L, dilation, H, Dh).transpose(0, 2, 3, 1, 4)
        return jnp.pad(t, ((0, 0), (0, 0), (0, 0), (0, Lp - L), (0, 0)))

    def band(t):
        t = jnp.pad(to_sub(t), ((0, 0), (0, 0), (0, 0), (W, 0), (0, 0)))
        prev = t[:, :, :, :Lp].reshape(B, dilation, H, nb, W, Dh)
        cur = t[:, :, :, W:].reshape(B, dilation, H, nb, W, Dh)
        return jnp.concatenate([prev, cur], axis=-2)

    qs = to_sub(q).reshape(B, dilation, H, nb, W, Dh)
    kb, vb = band(k), band(v)

    qi = np.arange(W)[:, None]
    kj = np.arange(2 * W)[None, :]
    rel = qi + W - kj
    in_win = (rel >= 0) & (rel <= W)
    key_idx = np.arange(nb)[:, None, None] * W + kj[None] - W
    mask = in_win[None] & (key_idx >= 0)
    buckets = _t5_bucket(np.clip(rel, 0, W) * dilation)
    bias = jnp.moveaxis(jnp.take(bias_table, buckets, axis=0), -1, 0).astype(jnp.float32)

    s = jnp.einsum('bghnqe,bghnke->bghnqk', qs, kb).astype(jnp.float32) * (Dh ** -0.5)
    s = s + bias[None, None, :, None]
    s = jnp.where(mask[None, None, None], s, -jnp.inf)
    m = jnp.max(s, axis=-1, keepdims=True)
    p = jnp.exp(s - m)
    l = jnp.sum(p, axis=-1, keepdims=True)
    o = jnp.einsum('bghnqk,bghnke->bghnqe', (p / l).astype(v.dtype), vb)
    lse = (m + jnp.log(l))[..., 0]
    o = o.reshape(B, dilation, H, Lp, Dh)[:, :, :, :L].transpose(0, 3, 1, 2, 4).reshape(B, S, H, Dh)
    lse = lse.reshape(B, dilation, H, Lp)[..., :L].transpose(0, 3, 1, 2).reshape(B, S, H)
    return o, lse


def _retention(q, k, v):
    B, S, H, dk = q.shape
    dv = v.shape[-1]
    half = dk // 2
    pos = jnp.arange(S, dtype=jnp.float32)
    inv_freq = ROPE_BASE ** (-jnp.arange(half, dtype=jnp.float32) / half)
    ang = pos[:, None] * inv_freq[None]
    cos = jnp.cos(ang)[None, :, None]
    sin = jnp.sin(ang)[None, :, None]

    def rot(t):
        t1, t2 = t[..., :half], t[..., half:]
        return jnp.concatenate([t1 * cos - t2 * sin, t1 * sin + t2 * cos], axis=-1).astype(t.dtype)

    q = rot(q)
    k = rot(k) * (dk ** -0.5)
    log_g = jnp.log(1.0 - 2.0 ** (-5.0 - jnp.arange(H, dtype=jnp.float32)))
    C = RET_CHUNK
    nC = S // C
    n = jnp.arange(C, dtype=jnp.float32)
    diff = n[:, None] - n[None, :]
    decay_mask = jnp.where(diff >= 0, jnp.exp(log_g[:, None, None] * jnp.maximum(diff, 0.0)), 0.0)
    q_dec = jnp.exp(log_g[:, None] * (n + 1.0))
    k_dec = jnp.exp(log_g[:, None] * (C - 1.0 - n))
    chunk_dec = jnp.exp(log_g * C)

    def chunks(t):
        return t.reshape(B, nC, C, H, t.shape[-1]).transpose(1, 0, 3, 2, 4)

    def step(state, xs):
        qc, kc, vc = xs
        sc = jnp.einsum('bhnd,bhmd->bhnm', qc, kc) * decay_mask
        o = (jnp.einsum('bhnm,bhmv->bhnv', sc, vc)
             + jnp.einsum('bhnd,bhdv->bhnv', qc * q_dec[..., None], state))
        state = (state * chunk_dec[:, None, None]
                 + jnp.einsum('bhmd,bhmv->bhdv', kc * k_dec[..., None], vc))
        return state.astype(jnp.float32), o.astype(jnp.float32)

    state0 = jnp.zeros((B, H, dk, dv), jnp.float32)
    _, ys = lax.scan(step, state0, (chunks(q), chunks(k), chunks(v)))
    o = ys.transpose(1, 0, 3, 2, 4).reshape(B, S, H, dv)
    mu = jnp.mean(o, axis=-1, keepdims=True)
    var = jnp.mean(jnp.square(o - mu), axis=-1, keepdims=True)
    return (o - mu) * lax.rsqrt(var + GN_EPS)


def _hybrid_mixer(x, rel_bias, w_in, b_in, w_attn_proj, w_ret_proj, w_out):
    B, S, _ = x.shape
    z = jnp.einsum('bsd,dc->bsc', x, w_in) + b_in
    q_a, k_a, v_a, q_r, k_r, v_r, g_r, gate_a, gate_b = jnp.split(z, SPLIT_POINTS, axis=-1)
    q_a = q_a.reshape(B, S, N_GROUPS, HEADS_PER_GROUP, HEAD_DIM)
    k_a = k_a.reshape(B, S, N_GROUPS, HEADS_PER_GROUP, HEAD_DIM)
    v_a = v_a.reshape(B, S, N_GROUPS, HEADS_PER_GROUP, HEAD_DIM)
    outs, lses = [], []
    for gi, (window, dilation) in enumerate(ATTN_GROUPS):
        o, lse = _dilated_window_attention(
            q_a[:, :, gi], k_a[:, :, gi], v_a[:, :, gi],
            rel_bias[:, gi * HEADS_PER_GROUP:(gi + 1) * HEADS_PER_GROUP], window, dilation)
        outs.append(o.astype(jnp.float32))
        lses.append(lse)
    wts = jax.nn.softmax(jnp.stack(lses, axis=0), axis=0)
    y_a = jnp.sum(wts[..., None] * jnp.stack(outs, axis=0), axis=0)
    y_a = y_a.astype(x.dtype).reshape(B, S, ATTN_OUT_WIDTH)
    y_r = _retention(q_r.reshape(B, S, RET_HEADS, RET_QK_DIM),
                     k_r.reshape(B, S, RET_HEADS, RET_QK_DIM),
                     v_r.reshape(B, S, RET_HEADS, RET_V_DIM))
    y_b = (jax.nn.silu(g_r) * y_r.reshape(B, S, RET_V_WIDTH)).astype(x.dtype)
    merged = (jax.nn.sigmoid(gate_a) * jnp.einsum('bsc,cd->bsd', y_a, w_attn_proj)
              + jax.nn.sigmoid(gate_b) * jnp.einsum('bsc,cd->bsd', y_b, w_ret_proj))
    return jnp.einsum('bsd,de->bse', merged, w_out)


def _swiglu(x, w_gate, w_up, w_down):
    h = jax.nn.silu(jnp.einsum('bsd,df->bsf', x, w_gate)) * jnp.einsum('bsd,df->bsf', x, w_up)
    return jnp.einsum('bsf,fd->bsd', h, w_down)


def setup_inputs(seed: int = 0) -> dict:
    key = jax.random.key(seed)
    ks = jax.random.split(key, 14)
    f32 = jnp.float32

    def nrm(k, shape, scale):
        return jax.random.normal(k, shape, f32) * scale

    return {
        "x": nrm(ks[0], (BATCH, SEQ, D_MODEL), 1.0),
        "rel_bias": nrm(ks[1], (NUM_BUCKETS, N_ATTN_HEADS), 0.2),
        "w_in": nrm(ks[2], (DEPTH, D_MODEL, IN_COLS), D_MODEL ** -0.5),
        "b_in": nrm(ks[3], (DEPTH, IN_COLS), 0.02),
        "w_attn_proj": nrm(ks[4], (DEPTH, ATTN_OUT_WIDTH, D_MODEL), BETA * ATTN_OUT_WIDTH ** -0.5),
        "w_ret_proj": nrm(ks[5], (DEPTH, RET_V_WIDTH, D_MODEL), BETA * RET_V_WIDTH ** -0.5),
        "w_out": nrm(ks[6], (DEPTH, D_MODEL, D_MODEL), BETA * D_MODEL ** -0.5),
        "ln1_g": 1.0 + nrm(ks[7], (DEPTH, D_MODEL), 0.02),
        "ln1_b": nrm(ks[8], (DEPTH, D_MODEL), 0.02),
        "w_ffn_gate": nrm(ks[9], (DEPTH, D_MODEL, D_FF), D_MODEL ** -0.5),
        "w_ffn_up": nrm(ks[10], (DEPTH, D_MODEL, D_FF), D_MODEL ** -0.5),
        "w_ffn_down": nrm(ks[11], (DEPTH, D_FF, D_MODEL), BETA * D_FF ** -0.5),
        "ln2_g": 1.0 + nrm(ks[12], (DEPTH, D_MODEL), 0.02),
        "ln2_b": nrm(ks[13], (DEPTH, D_MODEL), 0.02),
    }


def reference(x, rel_bias, w_in, b_in, w_attn_proj, w_ret_proj, w_out, ln1_g, ln1_b,
              w_ffn_gate, w_ffn_up, w_ffn_down, ln2_g, ln2_b):
    for l in range(DEPTH):
        mix = _hybrid_mixer(x, rel_bias, w_in[l], b_in[l], w_attn_proj[l], w_ret_proj[l], w_out[l])
        x = _layer_norm(ALPHA * x + mix, ln1_g[l], ln1_b[l])
        ffn = _swiglu(x, w_ffn_gate[l], w_ffn_up[l], w_ffn_down[l])
        x = _layer_norm(ALPHA * x + ffn, ln2_g[l], ln2_b[l])
    return x
```

```python
import contextlib
import numpy as np
import concourse.bass as bass
import concourse.mybir as mybir
from concourse.bass_utils import run_bass_kernel_spmd

F32 = mybir.dt.float32
BF16 = mybir.dt.bfloat16
AF = mybir.ActivationFunctionType
ALU = mybir.AluOpType

D = 1024
HD = 64
GROUPS = ((128, 1), (512, 4), (2048, 16))
NBLK = (2, 5, 17)
RH = 4
RDK = 256
RDV = 512
DFF = 2816
IN_COLS = 11648
NUM_BUCKETS = 32
MAX_DISTANCE = 2048
LF = 2304
COL = dict(qa=0, ka=1152, va=2304, qr=3456, kr=4480, vr=5504, gr=7552, ga=9600, gb=10624)
SLOT_ELEMS = 2304
NSLOT = 7
LN_EPS = 1e-5
GN_EPS = 1e-5


def _t5_bucket(dist):
    max_exact = NUM_BUCKETS // 2
    large = max_exact + (np.log(np.maximum(dist, max_exact) / max_exact)
                         / np.log(MAX_DISTANCE / max_exact)
                         * (NUM_BUCKETS - max_exact)).astype(np.int32)
    large = np.minimum(large, NUM_BUCKETS - 1)
    return np.where(dist < max_exact, dist, large).astype(np.int32)


def make_consts(S):
    half = RDK // 2
    pos = np.arange(S, dtype=np.float32)
    inv_freq = (10000.0 ** (-np.arange(half, dtype=np.float32) / half)).astype(np.float32)
    ang = (pos[None, :] * inv_freq[:, None]).astype(np.float32)
    cos_t = np.cos(ang).astype(np.float32)
    sin_t = np.sin(ang).astype(np.float32)
    log_g = np.log(1.0 - 2.0 ** (-5.0 - np.arange(RH, dtype=np.float64)))
    n = np.arange(128, dtype=np.float64)
    gq = np.exp(log_g[:, None] * (n[None, :] + 1.0))
    GQ = np.broadcast_to(gq.reshape(1, 4 * 128), (128, 512)).astype(np.float32).copy()
    diff = n[None, :] - n[:, None]
    DM = np.zeros((128, 4, 128), np.float64)
    for h in range(RH):
        DM[:, h, :] = np.where(diff >= 0, np.exp(log_g[h] * np.maximum(diff, 0.0)), 0.0) / 16.0
    DM = DM.reshape(128, 512).astype(np.float32)
    GK = (np.exp(log_g[None, :] * (127.0 - n[:, None])) / 16.0).astype(np.float32)
    chunk_dec = [float(np.exp(log_g[h] * 128.0)) for h in range(RH)]
    ident = np.eye(128, dtype=np.float32)
    J = ident[::-1].copy()
    delta = np.arange(LF) - 127
    bk = _t5_bucket(np.clip(delta, 0, MAX_DISTANCE))
    OH = np.zeros((NUM_BUCKETS, LF), np.float32)
    OH[bk, np.arange(LF)] = 1.0
    VALID = np.zeros((18, LF), np.float32)
    for g, (win, dil) in enumerate(GROUPS):
        v = ((delta >= 0) & (delta % dil == 0) & (delta <= win)).astype(np.float32)
        VALID[6 * g:6 * g + 6, :] = v[None, :]
    return dict(cos_t=cos_t, sin_t=sin_t, GQ=GQ, DM=DM, GK=GK, ident=ident, J=J, OH=OH,
                VALID=VALID, zeros=np.zeros((128, 256), np.float32)), chunk_dec


class Buf:
    __slots__ = ("w", "rs", "name")

    def __init__(self, name=""):
        self.w = None
        self.rs = {}
        self.name = name


class _Src:
    def __init__(self, name, sem, step):
        self.name, self.sem, self.step, self.cnt = name, sem, step, 0


class Tracker:
    def __init__(self, nc, es):
        self.nc = nc
        self.es = es
        self.src = {}
        self.handles = {"pe": nc.tensor, "act": nc.scalar, "dve": nc.vector,
                        "pool": nc.gpsimd, "sp": nc.sync}
        for e in ("pe", "act", "dve", "pool"):
            self.src[e] = _Src(e, es.enter_context(nc.semaphore("s_" + e)), 1)
        self.seen = {e: {} for e in self.handles}
        self.dry = False

    def dsem(self, name):
        s = _Src(name, self.es.enter_context(self.nc.semaphore(name)), 16)
        self.src[name] = s
        return s

    def _waits(self, eng, R, W):
        need = {}

        def req(ev, same_ok):
            if ev is None:
                return
            s, c = ev
            if s == eng and not same_ok:
                return
            if c > need.get(s, 0):
                need[s] = c
        for b in R:
            req(b.w, True)
        for b in W:
            req(b.w, False)
            for s, c in b.rs.items():
                req((s, c), False)
        h = self.handles[eng]
        seen = self.seen[eng]
        for s, c in need.items():
            if seen.get(s, 0) >= c:
                continue
            src = self.src[s]
            assert c <= src.cnt, (eng, s, c, src.cnt)
            h.wait_ge(src.sem, c)
            seen[s] = c

    def op(self, eng, fn, R=(), W=(), inc=True):
        if self.dry:
            return None
        self._waits(eng, R, W)
        src = self.src[eng]
        ins = fn(self.handles[eng])
        ticket = src.cnt + 1
        if inc:
            ins.then_inc(src.sem, 1)
            src.cnt += 1
        else:
            assert eng == "pe"
        for b in R:
            b.rs[eng] = ticket
        for b in W:
            b.w = (eng, ticket)
            b.rs = {}
        return ins

    def dma(self, eng, ds, out, in_, R=(), W=()):
        if self.dry:
            return
        self._waits(eng, R, W)
        ins = self.handles[eng].dma_start(out=out, in_=in_)
        ins.then_inc(ds.sem, 16)
        ds.cnt += 16
        for b in R:
            b.rs[ds.name] = ds.cnt
        for b in W:
            b.w = (ds.name, ds.cnt)
            b.rs = {}

    def drain(self, eng, ds):
        if self.dry:
            return
        if self.seen[eng].get(ds.name, 0) < ds.cnt:
            self.handles[eng].wait_ge(ds.sem, ds.cnt)
            self.seen[eng][ds.name] = ds.cnt


def chunk_plan():
    P = []

    def win(tag, col0, width, cw):
        c = 0
        j = 0
        while c < width:
            w = min(cw, width - c)
            P.append(dict(name=f"{tag}{j}", src="w_in", kc=8, cw=w, row0=0, col0=col0 + c, bias=True))
            c += w
            j += 1
    win("qa", COL["qa"], 1152, 256)
    win("ka", COL["ka"], 1152, 256)
    win("va", COL["va"], 1152, 192)
    win("qr", COL["qr"], 1024, 256)
    win("kr", COL["kr"], 1024, 256)
    win("vr", COL["vr"], 2048, 256)
    win("gr", COL["gr"], 2048, 256)
    for half in range(2):
        for j in range(2):
            P.append(dict(name=f"ga{half}{j}", src="w_in", kc=8, cw=256, row0=0,
                          col0=COL["ga"] + half * 512 + j * 256, bias=True))
        for j in range(2):
            P.append(dict(name=f"gb{half}{j}", src="w_in", kc=8, cw=256, row0=0,
                          col0=COL["gb"] + half * 512 + j * 256, bias=True))
        P.append(dict(name=f"ap{half}", src="w_attn_proj", kc=3, cw=512, row0=0, col0=half * 512, bias=False))
        for j in range(4):
            P.append(dict(name=f"rp{half}{j}", src="w_ret_proj", kc=16, cw=128, row0=0,
                          col0=half * 512 + j * 128, bias=False))
    for j in range(4):
        P.append(dict(name=f"wo{j}", src="w_out", kc=8, cw=256, row0=0, col0=j * 256, bias=False))
    for j in range(11):
        P.append(dict(name=f"fg{j}", src="w_ffn_gate", kc=8, cw=256, row0=0, col0=j * 256, bias=False))
        P.append(dict(name=f"fu{j}", src="w_ffn_up", kc=8, cw=256, row0=0, col0=j * 256, bias=False))
        P.append(dict(name=f"fd{j}", src="w_ffn_down", kc=2, cw=1024, row0=j * 256, col0=0, bias=False))
    for c in P:
        c["kk"] = c["kc"]
        assert c["kk"] * c["cw"] <= 2048 and c["cw"] <= 256 or not c["bias"], c
    return P


def build_program(S, depth, alpha, chunk_dec):
    NT = S // 128
    nc = bass.Bass("TRN2", target_bir_lowering=False)
    es = contextlib.ExitStack()

    def din(name, shape, dt=F32):
        return nc.dram_tensor(name, list(shape), dt, kind="ExternalInput").ap()

    x_in = din("x", [S, D])
    rel_bias = din("rel_bias", [NUM_BUCKETS, 18])
    Wd = dict(w_in=din("w_in", [depth, D, IN_COLS]), w_attn_proj=din("w_attn_proj", [depth, 384, D]),
              w_ret_proj=din("w_ret_proj", [depth, 2048, D]), w_out=din("w_out", [depth, D, D]),
              w_ffn_gate=din("w_ffn_gate", [depth, D, DFF]), w_ffn_up=din("w_ffn_up", [depth, D, DFF]),
              w_ffn_down=din("w_ffn_down", [depth, DFF, D]))
    b_in = din("b_in", [depth, IN_COLS])
    ln_in = dict(ln1_g=din("ln1_g", [depth, D]), ln1_b=din("ln1_b", [depth, D]),
                 ln2_g=din("ln2_g", [depth, D]), ln2_b=din("ln2_b", [depth, D]))
    c_cos = din("cos_t", [128, S])
    c_sin = din("sin_t", [128, S])
    c_GQ = din("GQ", [128, 512])
    c_DM = din("DM", [128, 512])
    c_GK = din("GK", [128, 4])
    c_ident = din("ident", [128, 128])
    c_J = din("J", [128, 128])
    c_OH = din("OH", [NUM_BUCKETS, LF])
    c_VALID = din("VALID", [18, LF])
    c_zeros = din("zeros", [128, 256])
    y_out = nc.dram_tensor("y", [S, D], F32, kind="ExternalOutput").ap()
    xmid_d = [nc.dram_tensor(f"xl{l}", [S, D], F32, kind="Internal").ap() for l in range(depth - 1)]
    Fd = nc.dram_tensor("Fd", [18, LF], BF16, kind="Internal")
    plan = chunk_plan()
    wsc = [[nc.dram_tensor(f"ws{l}_{c['name']}", [128, c["kk"] * c["cw"]], BF16, kind="Internal").ap()
            for c in plan] for l in range(depth)]
    wsb = [[(nc.dram_tensor(f"wb{l}_{c['name']}", [1, c["cw"]], BF16, kind="Internal").ap() if c["bias"] else None)
            for c in plan] for l in range(depth)]

    with es:
        tk = Tracker(nc, es)

        def sb(name, shape, dt):
            return es.enter_context(nc.sbuf_tensor("sb_" + name, list(shape), dt))

        slots = [sb(f"slot{i}", [128, SLOT_ELEMS], BF16) for i in range(NSLOT)]
        slot_b = [Buf(f"slot{i}") for i in range(NSLOT)]
        slot_sem = [tk.dsem(f"d_slot{i}") for i in range(NSLOT)]
        Et = [sb(f"Et{g}", [128, 6, NBLK[g], 128], BF16) for g in range(3)]
        Et_b = Buf("Et")
        Kr = [sb(f"Kr{g}", [128, 3, NBLK[g], 128], BF16) for g in range(3)]
        Vr = [sb(f"Vr{g}", [128, NBLK[g], 6, 65], BF16) for g in range(3)]
        Kr_b = [[Buf(f"Kr{g}_{s}") for s in range(NBLK[g])] for g in range(3)]
        Vr_b = [[Buf(f"Vr{g}_{s}") for s in range(NBLK[g])] for g in range(3)]
        St_f = sb("St_f", [128, 8, 512], F32)
        St_b = sb("St_b", [128, 8, 512], BF16)
        St_fb = [Buf(f"Stf{i}") for i in range(8)]
        St_bb = [Buf(f"Stb{i}") for i in range(8)]
        X = sb("X", [128, D], F32); X_b = Buf("X")
        M = sb("M", [128, D], F32); M_b = Buf("M")
        O = sb("O", [128, D], F32); O_b = Buf("O")
        xb = sb("xb", [128, D], BF16); xb_b = Buf("xb")
        xb2 = sb("xb2", [128, D], BF16); xb2_b = Buf("xb2")
        xT = sb("xT", [128, 8, 128], BF16); xT_b = Buf("xT")
        xmT = sb("xmT", [128, 8, 128], BF16); xmT_b = Buf("xmT")
        qaT = sb("qaT", [128, 9, 128], BF16); qaT_b = Buf("qaT")
        ta = sb("ta", [128, 512], F32); ta_b = Buf("ta")
        tb = sb("tb", [128, 512], F32); tb_b = Buf("tb")
        qrT = sb("qrT", [128, 8, 128], BF16); qrT_b = [Buf("qrT0"), Buf("qrT1")]
        krT = sb("krT", [128, 8, 128], BF16); krT_b = [Buf("krT0"), Buf("krT1")]
        qtT = sb("qtT", [128, 8, 128], BF16); qtT_b = [Buf("qtT0"), Buf("qtT1")]
        ktm = sb("ktm", [128, 1024], BF16); ktm_b = [Buf("ktm0"), Buf("ktm1")]
        vr = sb("vr", [128, 2048], BF16); vr_b = [Buf(f"vr{i}") for i in range(8)]
        sg = [sb(f"sg{i}", [128, 512], F32) for i in range(2)]; sg_b = [Buf("sg0"), Buf("sg1")]
        th = sb("th", [128, 512], F32); th_b = Buf("th")
        pexp2 = [sb(f"pexp{i}", [128, 512], BF16) for i in range(2)]; pexp2_b = [Buf("pexp0"), Buf("pexp1")]
        pexp = pexp2[0]; pexp_b = pexp2_b[0]
        PT = [sb(f"PT{i}", [128, 512], BF16) for i in range(2)]; PT_b = [Buf("PT0"), Buf("PT1")]
        scT = sb("scT", [128, 512], BF16); scT_b = Buf("scT")
        gn = sb("gn", [128, 512], F32); gn_b = Buf("gn")
        yb = [sb(f"yb{i}", [128, 512], BF16) for i in range(2)]; yb_b = [Buf("yb0"), Buf("yb1")]
        ybT = sb("ybT", [128, 16, 128], BF16); ybT_b = [Buf(f"ybT{i}") for i in range(4)]
        ya = sb("ya", [128, 6, 64], BF16); ya_b = Buf("ya")
        yaT = sb("yaT", [128, 3, 128], BF16); yaT_b = Buf("yaT")
        rL = sb("rL", [128, 8], F32); rL_b = Buf("rL")
        sgA = sb("sgA", [128, 512], F32); sgA_b = Buf("sgA")
        sgB = sb("sgB", [128, 512], F32); sgB_b = Buf("sgB")
        mT = sb("mT", [128, 8, 128], BF16); mT_b = [Buf("mT0"), Buf("mT1")]
        hT = [sb(f"hT{i}", [128, 2, 128], BF16) for i in range(2)]; hT_b = [Buf("hT0"), Buf("hT1")]
        cs = sb("cs", [128, 2, 128], F32); cs_b = Buf("cs")
        lnbc = sb("lnbc", [128, 2, D], F32); lnbc_b = Buf("lnbc")
        ident = sb("ident", [128, 128], BF16)
        Jm = sb("Jm", [128, 128], BF16)
        GQ = sb("GQ", [128, 512], F32)
        DM = sb("DM", [128, 512], F32)
        GK = sb("GK", [128, 4], F32)
        ones = sb("ones", [128, 128], BF16)
        nhalf = sb("nhalf", [128, 1], F32)
        stats = sb("stats", [128, 2, 6], F32); stats_b = Buf("stats")
        mv = sb("mv", [128, 2], F32); mv_b = Buf("mv")
        rstd = sb("rstd", [128, 1], F32); rstd_b = Buf("rstd")
        nb = sb("nb", [128, 1], F32); nb_b = Buf("nb")
        const_b = Buf("const")
        PS = [es.enter_context(nc.psum_tensor(f"ps{i}", [128, 512], F32)) for i in range(8)]
        PS_b = [Buf(f"ps{i}") for i in range(8)]
        rot_state = [0]
        rot_mode = [None]
        rot_sub = {"A": 0, "B": 0}

        def rot():
            m = rot_mode[0]
            if m is None:
                i = rot_state[0]
                rot_state[0] = (i + 1) % 5
                return i
            if m == "A":
                i = rot_sub[m]
                rot_sub[m] = (i + 1) % 3
                return i
            i = rot_sub[m]
            rot_sub[m] = (i + 1) % 2
            return 3 + i
        L0, L1, L2 = 5, 6, 7

        d_misc = tk.dsem("d_misc")
        d_x = tk.dsem("d_x")
        d_cs = tk.dsem("d_cs")
        d_ln = tk.dsem("d_ln")
        d_st = tk.dsem("d_st")
        d_F = tk.dsem("d_F")
        d_H = tk.dsem("d_H")
        NCG = 6
        d_cv = [[tk.dsem(f"d_cv{l}_{k}") for k in range(NCG)] for l in range(depth)]

        rb = sb("rb", [NUM_BUCKETS, 18], F32)
        d_misc2 = tk.dsem("d_misc2")
        tk.dma("pool", d_misc2, ident[:], c_ident, W=[const_b])
        tk.dma("pool", d_misc2, Jm[:], c_J, W=[const_b])
        tk.dma("sp", d_misc, GQ[:], c_GQ, W=[const_b])
        tk.dma("sp", d_misc, DM[:], c_DM, W=[const_b])
        tk.dma("sp", d_misc, GK[:], c_GK, W=[const_b])
        tk.dma("sp", d_misc, rb[:], rel_bias, W=[const_b])
        for e_ in ("pe", "act", "dve", "pool", "sp"):
            tk.drain(e_, d_misc)
            tk.drain(e_, d_misc2)
        for i_ in range(NSLOT):
            tk.op("pool", lambda e, i_=i_: e.memset(slots[i_][:, 2048:SLOT_ELEMS], 0.0), W=[slot_b[i_]])
        tk.op("pool", lambda e: e.memset(ones[:], 1.0), W=[const_b])
        tk.op("pool", lambda e: e.memset(nhalf[:], -0.5), W=[const_b])
        for g in range(3):
            tk.op("pool", lambda e, g=g: e.memset(Vr[g][:], 1.0), W=Vr_b[g])

        def cgroup(ci):
            return min(NCG - 1, ci * NCG // len(plan))
        conv_todo = {l: list(range(len(plan))) for l in range(depth)}

        def emit_conv(l, n):
            if globals().get("SKIP_CONV"):
                conv_todo[l] = []
            for _ in range(n):
                if not conv_todo[l]:
                    return
                ci = conv_todo[l].pop(0)
                c = plan[ci]
                ds = d_cv[l][cgroup(ci)]
                kc, cw = c["kc"], c["cw"]
                src = Wd[c["src"]][l, c["row0"]:c["row0"] + kc * 128, c["col0"]:c["col0"] + cw]
                src = src.rearrange("(k p) c -> p k c", p=128)
                dst = wsc[l][ci][:, 0:kc * cw].rearrange("p (k c) -> p k c", k=kc)
                if tk.dry:
                    continue
                ins = nc.gpsimd.dma_start(out=dst, in_=src)
                ins.then_inc(ds.sem, 16)
                ds.cnt += 16
                if c["bias"]:
                    bsrc = b_in[l:l + 1, c["col0"]:c["col0"] + cw]
                    ins = nc.gpsimd.dma_start(out=wsb[l][ci], in_=bsrc)
                    ins.then_inc(ds.sem, 16)
                    ds.cnt += 16
        emit_conv(0, len(plan))

        d_oh = tk.dsem("d_oh")
        d_vl = tk.dsem("d_vl")
        Fd_b = Buf("Fd")
        nchk = (LF + 511) // 512
        for j in range(nchk):
            c0 = j * 512
            w = min(512, LF - c0)
            tk.dma("sp", d_oh, ta[0:NUM_BUCKETS, 0:w], c_OH[:, c0:c0 + w], W=[ta_b])
            tk.dma("sp", d_vl, tb[0:18, 0:w], c_VALID[:, c0:c0 + w], W=[tb_b])
            p = rot()
            tk.op("pe", lambda e: e.matmul(PS[p][0:18, 0:w], lhsT=rb[:, :], rhs=ta[0:NUM_BUCKETS, 0:w],
                                           start=True, stop=True), R=[ta_b, const_b], W=[PS_b[p]])
            tk.op("act", lambda e: e.activation(out=gn[0:18, 0:w], in_=PS[p][0:18, 0:w], func=AF.Exp),
                  R=[PS_b[p]], W=[gn_b])
            tk.op("dve", lambda e: e.tensor_tensor(out=pexp[0:18, 0:w], in0=gn[0:18, 0:w], in1=tb[0:18, 0:w], op=ALU.mult),
                  R=[gn_b, tb_b], W=[pexp_b])
            tk.dma("sp", d_F, Fd.ap()[:, c0:c0 + w], pexp[0:18, 0:w], R=[pexp_b], W=[Fd_b])
        Hs = St_b[:].rearrange("p a (b c) -> p (a b) c", c=128)
        for g in range(3):
            no = NBLK[g]
            for h in range(6):
                hh = 6 * g + h
                srcap = bass.AP(Fd, hh * LF, [[1, 128], [128, no], [1, 128]])
                tk.dma("sp", d_H, Hs[:, 0:no, :], srcap, R=[Fd_b], W=St_bb)
                o0 = 0
                while o0 < no:
                    n_ = min(4, no - o0)
                    p = rot()
                    tk.op("pe", lambda e: e.matmul(PS[p][:, 0:n_ * 128], lhsT=Jm[:], rhs=Hs[:, o0:o0 + n_, :],
                                                   start=True, stop=True), R=St_bb + [const_b], W=[PS_b[p]])
                    tk.op("dve", lambda e: e.tensor_copy(
                        Et[g][:, h, o0:o0 + n_, :], PS[p][:, 0:n_ * 128].rearrange("p (a b) -> p a b", b=128)),
                        R=[PS_b[p]], W=[Et_b])
                    o0 += n_

        stream = []
        name2ci = {c["name"]: i for i, c in enumerate(plan)}
        issue_ptr = [0]
        use_ptr = [0]
        cv_drained = set()

        def issue_next():
            k = issue_ptr[0]
            if k >= len(stream):
                return
            l, t, ci = stream[k]
            c = plan[ci]
            s = k % NSLOT
            key = (l, cgroup(ci))
            if key not in cv_drained:
                tk.drain("sp", d_cv[l][cgroup(ci)])
                cv_drained.add(key)
            n = c["kk"] * c["cw"]
            tk.dma("sp", slot_sem[s], slots[s][:, 0:n], wsc[l][ci][:, 0:n], W=[slot_b[s]])
            if c["bias"]:
                tk.dma("sp", slot_sem[s], slots[s][0:1, 2048:2048 + c["cw"]], wsb[l][ci], W=[slot_b[s]])
            issue_ptr[0] += 1

        def next_chunk(l, t, name):
            if tk.dry:
                k = len(stream)
                stream.append((l, t, name2ci[name]))
            else:
                k = use_ptr[0]
                assert stream[k] == (l, t, name2ci[name]), (stream[k], l, t, name)
                use_ptr[0] += 1
                while issue_ptr[0] < min(len(stream), k + NSLOT):
                    issue_next()
            s = k % NSLOT
            c = plan[stream[k][2]]
            view = slots[s][:, 0:c["kk"] * c["cw"]].rearrange("p (k c) -> p k c", k=c["kk"])
            c = dict(c)
            c["bview"] = slots[s][:, 2048:SLOT_ELEMS]
            return view, slot_b[s], c

        def mm(out, lhsT, rhs, start, stop, R, W, inc=None):
            tk.op("pe", lambda e: e.matmul(out, lhsT=lhsT, rhs=rhs, start=start, stop=stop), R=R, W=W,
                  inc=stop if inc is None else inc)

        def fm_group(pbank, col, wv, wb, c, cb, act, act_b, bias=True):
            kc = c["kc"]
            for k in range(kc):
                mm(PS[pbank][:, col:col + 128], wv[:, k, cb * 128:(cb + 1) * 128], act[:, k, :],
                   k == 0, (k == kc - 1) and not bias, R=[wb] + act_b, W=[PS_b[pbank]])
            if bias:
                mm(PS[pbank][:, col:col + 128], c["bview"][:, cb * 128:(cb + 1) * 128], ones[:, :],
                   False, True, R=[wb, const_b], W=[PS_b[pbank]])

        def tm_group(pbank, col, wv, wb, c, act, act_b, bias=True):
            kc, cw = c["kc"], c["cw"]
            for k in range(kc):
                mm(PS[pbank][:, col:col + cw], act[:, k, :], wv[:, k, :], k == 0, (k == kc - 1) and not bias,
                   R=[wb] + act_b, W=[PS_b[pbank]])
            if bias:
                mm(PS[pbank][:, col:col + cw], ones[:, :], c["bview"][:, 0:cw], False, True,
                   R=[wb, const_b], W=[PS_b[pbank]])

        def transpose_to(dst, dst_b, src, src_b, nblk, evac_eng):
            j0 = 0
            while j0 < nblk:
                n_ = min(4, nblk - j0)
                p = rot()
                pv = PS[p][:].bitcast(BF16)
                for j in range(n_):
                    tk.op("pe", lambda e, j=j: e.transpose(pv[:, j * 128:(j + 1) * 128],
                                                           src[:, (j0 + j) * 128:(j0 + j + 1) * 128], ident[:]),
                          R=src_b + [const_b], W=[PS_b[p]])
                if evac_eng == "act":
                    tk.op("act", lambda e: e.copy(dst[:, j0:j0 + n_, :],
                                                  pv[:, 0:n_ * 128].rearrange("p (a b) -> p a b", b=128)),
                          R=[PS_b[p]], W=dst_b)
                else:
                    tk.op("dve", lambda e: e.tensor_copy(dst[:, j0:j0 + n_, :],
                                                         pv[:, 0:n_ * 128].rearrange("p (a b) -> p a b", b=128)),
                          R=[PS_b[p]], W=dst_b)
                j0 += n_

        def layer_norm(buf, buf_b, eps):
            for j in range(2):
                tk.op("dve", lambda e, j=j: e.bn_stats(stats[:, j, :], buf[:, j * 512:(j + 1) * 512]),
                      R=[buf_b], W=[stats_b])
            tk.op("dve", lambda e: e.bn_aggr(mv[:], stats[:].rearrange("p a b -> p (a b)")), R=[stats_b], W=[mv_b])
            tk.op("pool", lambda e: e.tensor_scalar(rstd[:], mv[:, 1:2], eps, None, ALU.add), R=[mv_b], W=[rstd_b])
            tk.op("pool", lambda e: e.tensor_tensor(out=rstd[:], in0=rstd[:], in1=nhalf[:], op=ALU.pow),
                  R=[rstd_b, const_b], W=[rstd_b])
            tk.op("dve", lambda e: e.tensor_scalar(buf[:], buf[:], mv[:, 0:1], rstd[:], ALU.subtract, ALU.mult),
                  R=[buf_b, mv_b, rstd_b], W=[buf_b])
            tk.op("dve", lambda e: e.tensor_tensor(out=buf[:], in0=buf[:], in1=lnbc[:, 0, :], op=ALU.mult),
                  R=[buf_b, lnbc_b], W=[buf_b])
            tk.op("dve", lambda e: e.tensor_tensor(out=buf[:], in0=buf[:], in1=lnbc[:, 1, :], op=ALU.add),
                  R=[buf_b, lnbc_b], W=[buf_b])

        def emit_all():
            for l in range(depth):
                src_x = x_in if l == 0 else xmid_d[l - 1]
                dst_x = y_out if l == depth - 1 else xmid_d[l]
                if l > 0:
                    tk.drain("sp", d_st)
                for i in range(8):
                    tk.op("pool", lambda e, i=i: e.memset(St_f[:, i, :], 0.0), W=[St_fb[i]])
                    tk.op("pool", lambda e, i=i: e.memset(St_b[:, i, :], 0.0), W=[St_bb[i]])
                tk.dma("sp", d_x, X[:], src_x[0:128, :], W=[X_b])
                def front(t):
                    t0 = t * 128
                    tk.op("act", lambda e: e.copy(xb2[:], X[:]), R=[X_b], W=[xb2_b])
                    transpose_to(xT, [xT_b], xb2, [xb2_b], 8, "act")
                    tk.dma("sp", d_cs, cs[:, 0, :], c_cos[:, t0:t0 + 128], W=[cs_b])
                    tk.dma("sp", d_cs, cs[:, 1, :], c_sin[:, t0:t0 + 128], W=[cs_b])

                    blk = 0
                    for j in range(5):
                        wv, wb, c = next_chunk(l, t, f"qa{j}")
                        nb_ = c["cw"] // 128
                        p = rot()
                        for cb in range(nb_):
                            fm_group(p, cb * 128, wv, wb, c, cb, xT, [xT_b])
                        tk.op("act", lambda e, p=p, nb_=nb_, blk=blk: e.copy(
                            qaT[:, blk:blk + nb_, :], PS[p][:, 0:nb_ * 128].rearrange("p (a b) -> p a b", b=128)),
                            R=[PS_b[p]], W=[qaT_b])
                        blk += nb_
                        yield
                    blk = 0
                    for j in range(5):
                        wv, wb, c = next_chunk(l, t, f"ka{j}")
                        nb_ = c["cw"] // 128
                        p = rot()
                        for cb in range(nb_):
                            fm_group(p, cb * 128, wv, wb, c, cb, xT, [xT_b])
                        for cb in range(nb_):
                            g, hp = divmod(blk + cb, 3)
                            s = t % NBLK[g]
                            tk.op("dve", lambda e, p=p, cb=cb, g=g, hp=hp, s=s: e.tensor_copy(
                                Kr[g][:, hp, s, :], PS[p][:, cb * 128:(cb + 1) * 128]),
                                R=[PS_b[p]], W=[Kr_b[g][s]])
                        blk += nb_
                        yield
                    for j in range(6):
                        wv, wb, c = next_chunk(l, t, f"va{j}")
                        g, hh = divmod(j, 2)
                        s = t % NBLK[g]
                        p = rot()
                        tm_group(p, 0, wv, wb, c, xT, [xT_b])
                        tk.op("act", lambda e, p=p, g=g, hh=hh, s=s: e.copy(
                            Vr[g][:, s, 3 * hh:3 * hh + 3, 0:64], PS[p][:, 0:192].rearrange("p (a b) -> p a b", b=64)),
                            R=[PS_b[p]], W=[Vr_b[g][s]])
                        yield
                def attn_core(t):
                    jobs = []
                    for h in range(6):
                        for g in range(3):
                            nvis = min(t + 1, NBLK[g])
                            o0 = 0
                            while o0 < nvis:
                                n_ = min(4, nvis - o0)
                                jobs.append((h, g, o0, n_))
                                o0 += n_
                    first_of_head = {}
                    last_of_head = {}
                    for idx, (h, g, o0, n_) in enumerate(jobs):
                        first_of_head.setdefault(h, idx)
                        last_of_head[h] = idx
                    Uv = PS[L2][:, 0:390].rearrange("p (a b) -> p a b", b=65)

                    def emit_scores(idx):
                        h, g, o0, n_ = jobs[idx]
                        hp, hs = divmod(h, 2)
                        p = rot()
                        for j in range(n_):
                            s = (t - (o0 + j)) % NBLK[g]
                            mm(PS[p][:, j * 128:(j + 1) * 128],
                               Kr[g][hs * 64:(hs + 1) * 64, hp, s, :], qaT[hs * 64:(hs + 1) * 64, 3 * g + hp, :],
                               True, True, R=[Kr_b[g][s], qaT_b], W=[PS_b[p]])
                        return p

                    def emit_soft(idx, p):
                        h, g, o0, n_ = jobs[idx]
                        w = n_ * 128
                        q = idx % 2
                        tk.op("act", lambda e: e.activation(out=pexp2[q][:, 0:w], in_=PS[p][:, 0:w], func=AF.Exp, scale=0.125),
                              R=[PS_b[p]], W=[pexp2_b[q]])
                        tk.op("dve", lambda e: e.tensor_tensor(
                            out=PT[q][:, 0:w], in0=pexp2[q][:, 0:w],
                            in1=Et[g][:, h, o0:o0 + n_, :].rearrange("p a b -> p (a b)"), op=ALU.mult),
                            R=[pexp2_b[q], Et_b], W=[PT_b[q]])

                    def emit_pv(idx):
                        h, g, o0, n_ = jobs[idx]
                        q = idx % 2
                        for j in range(n_):
                            s = (t - (o0 + j)) % NBLK[g]
                            first = (idx == first_of_head[h]) and j == 0
                            last = (idx == last_of_head[h]) and j == n_ - 1
                            mm(Uv[:, h, :], PT[q][:, j * 128:(j + 1) * 128], Vr[g][:, s, h, :], first, last,
                               R=[PT_b[q], Vr_b[g][s]], W=[PS_b[L2]], inc=last or (j == n_ - 1))
                            if not last and j == n_ - 1:
                                pass
                    pbank = {0: emit_scores(0)}
                    if len(jobs) > 1:
                        pbank[1] = emit_scores(1)
                    for idx in range(len(jobs)):
                        emit_soft(idx, pbank.pop(idx))
                        if idx + 2 < len(jobs):
                            pbank[idx + 2] = emit_scores(idx + 2)
                        emit_pv(idx)
                        yield
                    tk.op("dve", lambda e: e.reciprocal(rL[:, 0:6], Uv[:, :, 64:65].rearrange("p a b -> p (a b)")),
                          R=[PS_b[L2]], W=[rL_b])
                    tk.op("dve", lambda e: e.tensor_tensor(out=ya[:], in0=Uv[:, :, 0:64],
                                                           in1=rL[:, 0:6].unsqueeze(2).to_broadcast([128, 6, 64]), op=ALU.mult),
                          R=[PS_b[L2], rL_b], W=[ya_b])
                    transpose_to(yaT, [yaT_b], ya[:].rearrange("p a b -> p (a b)"), [ya_b], 3, "act")

                deferred = []

                def ret_proj(t):
                    def rotary(tag, dstT, dstT_b):
                        for bank in range(2):
                            p = rot()
                            for hh in range(2):
                                wv, wb, c = next_chunk(l, t, f"{tag}{bank * 2 + hh}")
                                for cb in range(2):
                                    fm_group(p, (hh * 2 + cb) * 128, wv, wb, c, cb, xT, [xT_b])
                                yield
                            pv = PS[p][:].rearrange("p (h f t) -> p h f t", h=2, f=2)
                            q1 = pv[:, :, 0, :]
                            q2 = pv[:, :, 1, :]
                            cosb = cs[:, 0:1, :].to_broadcast([128, 2, 128])
                            sinb = cs[:, 1:2, :].to_broadcast([128, 2, 128])
                            tav = ta[:, 0:256].rearrange("p (h t) -> p h t", h=2)
                            tbv = tb[:, 0:256].rearrange("p (h t) -> p h t", h=2)
                            dv = dstT[:, bank * 4:(bank + 1) * 4, :].rearrange("p (h f) t -> p h f t", f=2)
                            tk.op("dve", lambda e: e.tensor_tensor(out=tav, in0=q1, in1=cosb, op=ALU.mult),
                                  R=[PS_b[p], cs_b], W=[ta_b])
                            tk.op("dve", lambda e: e.tensor_tensor(out=tbv, in0=q2, in1=sinb, op=ALU.mult),
                                  R=[PS_b[p], cs_b], W=[tb_b])
                            tk.op("pool", lambda e: e.tensor_tensor(out=dv[:, :, 0, :], in0=tav, in1=tbv, op=ALU.subtract),
                                  R=[ta_b, tb_b], W=[dstT_b[bank]])
                            tk.op("dve", lambda e: e.tensor_tensor(out=tav, in0=q1, in1=sinb, op=ALU.mult),
                                  R=[PS_b[p], cs_b], W=[ta_b])
                            tk.op("dve", lambda e: e.tensor_tensor(out=tbv, in0=q2, in1=cosb, op=ALU.mult),
                                  R=[PS_b[p], cs_b], W=[tb_b])
                            tk.op("pool", lambda e: e.tensor_tensor(out=dv[:, :, 1, :], in0=tav, in1=tbv, op=ALU.add),
                                  R=[ta_b, tb_b], W=[dstT_b[bank]])
                    yield from rotary("qr", qrT, qrT_b)
                    for bank in range(2):
                        tk.op("pool", lambda e, bank=bank: e.tensor_tensor(
                            out=qtT[:, bank * 4:(bank + 1) * 4, :].rearrange("p (h f) t -> p h f t", f=2),
                            in0=qrT[:, bank * 4:(bank + 1) * 4, :].rearrange("p (h f) t -> p h f t", f=2),
                            in1=GQ[:, bank * 256:(bank + 1) * 256].rearrange("p (h t) -> p h t", h=2).unsqueeze(2).to_broadcast([128, 2, 2, 128]),
                            op=ALU.mult), R=[qrT_b[bank], const_b], W=[qtT_b[bank]])
                    yield from rotary("kr", krT, krT_b)
                    while deferred:
                        deferred.pop(0)()
                    for j in range(8):
                        wv, wb, c = next_chunk(l, t, f"vr{j}")
                        p = rot()
                        tm_group(p, 0, wv, wb, c, xT, [xT_b])
                        tk.op("act", lambda e, p=p, j=j: e.copy(vr[:, j * 256:(j + 1) * 256], PS[p][:, 0:256]),
                              R=[PS_b[p]], W=[vr_b[j]])
                        yield

                def ffn_steps(t):
                    def gu(j):
                        p = rot()
                        wv, wb, c = next_chunk(l, t, f"fg{j}")
                        for cb in range(2):
                            fm_group(p, cb * 128, wv, wb, c, cb, xmT, [xmT_b], bias=False)
                        wv, wb, c = next_chunk(l, t, f"fu{j}")
                        for cb in range(2):
                            fm_group(p, (2 + cb) * 128, wv, wb, c, cb, xmT, [xmT_b], bias=False)
                        tk.op("act", lambda e: e.activation(out=th[:, 0:256], in_=PS[p][:, 0:256], func=AF.Tanh, scale=0.5),
                              R=[PS_b[p]], W=[th_b])
                        tk.op("dve", lambda e: e.scalar_tensor_tensor(
                            out=th[:, 256:512], in0=th[:, 0:256], scalar=1.0, in1=PS[p][:, 0:256], op0=ALU.add, op1=ALU.mult),
                            R=[th_b, PS_b[p]], W=[th_b])
                        q = j % 2
                        tk.op("dve", lambda e: e.scalar_tensor_tensor(
                            out=hT[q][:].rearrange("p a b -> p (a b)"), in0=th[:, 256:512], scalar=0.5, in1=PS[p][:, 256:512],
                            op0=ALU.mult, op1=ALU.mult), R=[th_b, PS_b[p]], W=[hT_b[q]])

                    def down(j):
                        q = j % 2
                        wv, wb, c = next_chunk(l, t, f"fd{j}")
                        for hb, bank in enumerate((L0, L1)):
                            for k in range(2):
                                mm(PS[bank][:], hT[q][:, k, :], wv[:, k, hb * 512:(hb + 1) * 512],
                                   j == 0 and k == 0, j == 10 and k == 1, R=[hT_b[q], wb], W=[PS_b[bank]],
                                   inc=(k == 1 and hb == 1) or (j == 10 and k == 1))
                    gu(0)
                    yield
                    for j in range(11):
                        if j + 1 < 11:
                            gu(j + 1)
                        down(j)
                        yield

                def run(gen, mode=None):
                    rot_mode[0] = mode
                    try:
                        next(gen)
                        ok = True
                    except StopIteration:
                        ok = False
                    rot_mode[0] = None
                    return ok

                def chain(*gens):
                    for g_ in gens:
                        yield from g_

                for t in range(NT):
                    t0 = t * 128
                    if l + 1 < depth:
                        emit_conv(l + 1, len(plan) if t == NT - 1 else -(-len(plan) // NT))
                    if t == 0:
                        for _ in front(0):
                            pass
                        for _ in attn_core(0):
                            pass
                    tk.op("pool", lambda e: e.tensor_scalar(M[:], X[:], alpha, 0.0, ALU.mult, ALU.add), R=[X_b], W=[M_b])
                    if t + 1 < NT:
                        tk.dma("sp", d_x, X[:], src_x[t0 + 128:t0 + 256, :], W=[X_b])
                    for _ in ret_proj(t):
                        pass
                    for bank in range(2):
                        p = rot()
                        pv = PS[p][:].bitcast(BF16)
                        for j in range(4):
                            blk_ = bank * 4 + j
                            tk.op("pe", lambda e, j=j, blk_=blk_: e.transpose(pv[:, j * 128:(j + 1) * 128], krT[:, blk_, :], ident[:]),
                                  R=[krT_b[bank], const_b], W=[PS_b[p]])
                        tk.op("dve", lambda e, bank=bank, pv=pv: e.tensor_tensor(
                            out=ktm[:, bank * 512:(bank + 1) * 512].rearrange("p (h d) -> p h d", h=2),
                            in0=pv[:, 0:512].rearrange("p (h d) -> p h d", h=2),
                            in1=GK[:, bank * 2:bank * 2 + 2].unsqueeze(2).to_broadcast([128, 2, 256]), op=ALU.mult),
                            R=[PS_b[p], const_b], W=[ktm_b[bank]])
                    p = rot()
                    for h in range(4):
                        for f in range(2):
                            mm(PS[p][:, h * 128:(h + 1) * 128], krT[:, h * 2 + f, :], qrT[:, h * 2 + f, :], f == 0, f == 1,
                               R=[krT_b[h // 2], qrT_b[h // 2]], W=[PS_b[p]])
                    tk.op("dve", lambda e, p=p: e.tensor_tensor(out=scT[:], in0=PS[p][:], in1=DM[:], op=ALU.mult),
                          R=[PS_b[p], const_b], W=[scT_b])
                    def ret_stage1(h):
                        pg = rot()
                        for j in range(2):
                            wv, wb, c = next_chunk(l, t, f"gr{h * 2 + j}")
                            tm_group(pg, j * 256, wv, wb, c, xT, [xT_b])
                        tk.op("act", lambda e, pg=pg: e.activation(out=th[:], in_=PS[pg][:], func=AF.Tanh, scale=0.5),
                              R=[PS_b[pg]], W=[th_b])
                        sq = h % 2
                        tk.op("dve", lambda e, pg=pg, sq=sq: e.scalar_tensor_tensor(
                            out=sg[sq][:], in0=th[:], scalar=1.0, in1=PS[pg][:], op0=ALU.add, op1=ALU.mult),
                            R=[th_b, PS_b[pg]], W=[sg_b[sq]])
                        po = rot()
                        mm(PS[po][:], scT[:, h * 128:(h + 1) * 128], vr[:, h * 512:(h + 1) * 512], True, False,
                           R=[scT_b, vr_b[2 * h], vr_b[2 * h + 1]], W=[PS_b[po]])
                        for f in range(2):
                            mm(PS[po][:], qtT[:, h * 2 + f, :], St_b[:, h * 2 + f, :], False, f == 1,
                               R=[qtT_b[h // 2], St_bb[h * 2 + f]], W=[PS_b[po]])
                        tk.op("dve", lambda e, po=po: e.bn_stats(stats[:, 0, :], PS[po][:]), R=[PS_b[po]], W=[stats_b])
                        tk.op("dve", lambda e: e.bn_aggr(mv[:], stats[:, 0, :]), R=[stats_b], W=[mv_b])
                        tk.op("pool", lambda e: e.tensor_scalar(rstd[:], mv[:, 1:2], GN_EPS, None, ALU.add), R=[mv_b], W=[rstd_b])
                        tk.op("pool", lambda e: e.tensor_tensor(out=rstd[:], in0=rstd[:], in1=nhalf[:], op=ALU.pow),
                              R=[rstd_b, const_b], W=[rstd_b])
                        tk.op("dve", lambda e, po=po: e.tensor_scalar(gn[:], PS[po][:], mv[:, 0:1], rstd[:], ALU.subtract, ALU.mult),
                              R=[PS_b[po], mv_b, rstd_b], W=[gn_b])
                        tk.op("dve", lambda e, sq=sq: e.scalar_tensor_tensor(
                            out=yb[sq][:], in0=gn[:], scalar=0.5, in1=sg[sq][:], op0=ALU.mult, op1=ALU.mult),
                            R=[gn_b, sg_b[sq]], W=[yb_b[sq]])

                    def ret_stage2(h):
                        sq = h % 2
                        p = rot()
                        pv = PS[p][:].bitcast(BF16)
                        for j in range(4):
                            tk.op("pe", lambda e, j=j, sq=sq, pv=pv: e.transpose(pv[:, j * 128:(j + 1) * 128],
                                                                                 yb[sq][:, j * 128:(j + 1) * 128], ident[:]),
                                  R=[yb_b[sq], const_b], W=[PS_b[p]])
                        tk.op("act", lambda e, h=h, pv=pv: e.copy(ybT[:, 4 * h:4 * h + 4, :],
                                                                  pv[:, 0:512].rearrange("p (a b) -> p a b", b=128)),
                              R=[PS_b[p]], W=[ybT_b[h]])
                    ret_stage1(0)
                    for h in range(4):
                        if h + 1 < 4:
                            ret_stage1(h + 1)
                        ret_stage2(h)
                    for h in range(4):
                        for f in range(2):
                            i8 = h * 2 + f
                            p = rot()
                            mm(PS[p][:], ktm[:, i8 * 128:(i8 + 1) * 128], vr[:, h * 512:(h + 1) * 512], True, True,
                               R=[ktm_b[h // 2], vr_b[2 * h], vr_b[2 * h + 1]], W=[PS_b[p]])
                            tk.op("dve", lambda e, p=p, i8=i8, h=h: e.scalar_tensor_tensor(
                                out=St_f[:, i8, :], in0=St_f[:, i8, :], scalar=chunk_dec[h], in1=PS[p][:],
                                op0=ALU.mult, op1=ALU.add), R=[St_fb[i8], PS_b[p]], W=[St_fb[i8]])
                            tk.op("act", lambda e, i8=i8: e.copy(St_b[:, i8, :], St_f[:, i8, :]),
                                  R=[St_fb[i8]], W=[St_bb[i8]])

                    for half in range(2):
                        pA = rot()
                        for j in range(2):
                            wv, wb, c = next_chunk(l, t, f"ga{half}{j}")
                            for cb in range(2):
                                fm_group(pA, (j * 2 + cb) * 128, wv, wb, c, cb, xT, [xT_b])
                        tk.op("act", lambda e, pA=pA: e.activation(out=sgA[:], in_=PS[pA][:], func=AF.Tanh, scale=0.5),
                              R=[PS_b[pA]], W=[sgA_b])
                        tk.op("pool", lambda e: e.tensor_scalar(sgA[:], sgA[:], 0.5, 0.5, ALU.mult, ALU.add),
                              R=[sgA_b], W=[sgA_b])
                        pB = rot()
                        for j in range(2):
                            wv, wb, c = next_chunk(l, t, f"gb{half}{j}")
                            for cb in range(2):
                                fm_group(pB, (j * 2 + cb) * 128, wv, wb, c, cb, xT, [xT_b])
                        tk.op("act", lambda e, pB=pB: e.activation(out=sgB[:], in_=PS[pB][:], func=AF.Tanh, scale=0.5),
                              R=[PS_b[pB]], W=[sgB_b])
                        tk.op("pool", lambda e: e.tensor_scalar(sgB[:], sgB[:], 0.5, 0.5, ALU.mult, ALU.add),
                              R=[sgB_b], W=[sgB_b])
                        wv, wb, c = next_chunk(l, t, f"ap{half}")
                        pa = rot()
                        for cb in range(4):
                            fm_group(pa, cb * 128, wv, wb, c, cb, yaT, [yaT_b], bias=False)
                        tk.op("dve", lambda e, pa=pa: e.tensor_tensor(out=ta[:], in0=PS[pa][:], in1=sgA[:], op=ALU.mult),
                              R=[PS_b[pa], sgA_b], W=[ta_b])
                        pb_ = rot()
                        for j in range(4):
                            wv, wb, c = next_chunk(l, t, f"rp{half}{j}")
                            fm_group(pb_, j * 128, wv, wb, c, 0, ybT, ybT_b, bias=False)
                        tk.op("dve", lambda e, pb_=pb_: e.tensor_tensor(out=tb[:], in0=PS[pb_][:], in1=sgB[:], op=ALU.mult),
                              R=[PS_b[pb_], sgB_b], W=[tb_b])
                        tk.op("pool", lambda e, half=half: e.tensor_tensor(
                            out=mT[:, half * 4:(half + 1) * 4, :].rearrange("p a b -> p (a b)"), in0=ta[:], in1=tb[:], op=ALU.add),
                            R=[ta_b, tb_b], W=[mT_b[half]])

                    tk.dma("sp", d_ln, lnbc[:, 0, :], ln_in["ln1_g"][l:l + 1, :].to_broadcast([128, D]), W=[lnbc_b])
                    tk.dma("sp", d_ln, lnbc[:, 1, :], ln_in["ln1_b"][l:l + 1, :].to_broadcast([128, D]), W=[lnbc_b])
                    for j in range(4):
                        wv, wb, c = next_chunk(l, t, f"wo{j}")
                        bank = L0 if j < 2 else L1
                        tm_group(bank, (j % 2) * 256, wv, wb, c, mT, mT_b, bias=False)
                    for j, bank in enumerate((L0, L1)):
                        tk.op("dve", lambda e, j=j, bank=bank: e.tensor_tensor(
                            out=M[:, j * 512:(j + 1) * 512], in0=M[:, j * 512:(j + 1) * 512], in1=PS[bank][:], op=ALU.add),
                            R=[M_b, PS_b[bank]], W=[M_b])
                    layer_norm(M, M_b, LN_EPS)
                    gA = chain(front(t + 1), attn_core(t + 1)) if t + 1 < NT else iter(())
                    a_live = True
                    for _ in range(16):
                        a_live = a_live and run(gA, "A")
                    tk.op("act", lambda e: e.copy(xb[:], M[:]), R=[M_b], W=[xb_b])
                    transpose_to(xmT, [xmT_b], xb, [xb_b], 8, "dve")

                    tk.dma("sp", d_ln, lnbc[:, 0, :], ln_in["ln2_g"][l:l + 1, :].to_broadcast([128, D]), R=[M_b], W=[lnbc_b])
                    tk.dma("sp", d_ln, lnbc[:, 1, :], ln_in["ln2_b"][l:l + 1, :].to_broadcast([128, D]), R=[M_b], W=[lnbc_b])
                    gE = ffn_steps(t)
                    e_live = True
                    while e_live:
                        e_live = run(gE, "B")
                        for _ in range(6):
                            a_live = a_live and run(gA, "A")
                    while a_live:
                        a_live = run(gA, "A")
                    for j, bank in enumerate((L0, L1)):
                        tk.op("dve", lambda e, j=j, bank=bank: e.scalar_tensor_tensor(
                            out=O[:, j * 512:(j + 1) * 512], in0=M[:, j * 512:(j + 1) * 512], scalar=alpha, in1=PS[bank][:],
                            op0=ALU.mult, op1=ALU.add), R=[M_b, PS_b[bank]], W=[O_b])
                    def _ln2_store(t0=t0):
                        layer_norm(O, O_b, LN_EPS)
                        tk.dma("sp", d_st, dst_x[t0:t0 + 128, :], O[:], R=[O_b])
                    deferred.append(_ln2_store)
                    if t + 1 >= NT:
                        while deferred:
                            deferred.pop(0)()

        tk.dry = True
        emit_all()
        tk.dry = False
        conv_todo.update({l: ([] if l == 0 else list(range(len(plan)))) for l in range(depth)})
        rot_state[0] = 0
        rot_sub.update({"A": 0, "B": 0})
        for _ in range(NSLOT - 1):
            issue_next()
        emit_all()
        tk.drain("sp", d_st)
    return nc


_PROG_CACHE = {}


def _run(inputs, S, depth, n_cores):
    alpha = float((2 * depth) ** 0.25)
    consts, chunk_dec = make_consts(S)
    key = (S, depth)
    if key not in _PROG_CACHE:
        _PROG_CACHE[key] = build_program(S, depth, alpha, chunk_dec)
    nc = _PROG_CACHE[key]
    x = np.ascontiguousarray(np.asarray(inputs["x"], dtype=np.float32))
    shared = {k: np.ascontiguousarray(np.asarray(v, dtype=np.float32)) for k, v in inputs.items() if k != "x"}
    shared.update(consts)
    in_maps = []
    for c in range(n_cores):
        m = dict(shared)
        m["x"] = x[c]
        in_maps.append(m)
    res = run_bass_kernel_spmd(nc, in_maps, core_ids=list(range(n_cores)))
    return np.stack([np.asarray(r["y"], dtype=np.float32) for r in res.results], axis=0)


def kernel(**inputs):
    x = np.asarray(inputs["x"])
    B, S, _ = x.shape
    depth = int(np.asarray(inputs["w_in"]).shape[0])
    return _run(inputs, S, depth, B)
```

```python
import contextlib
import numpy as np
import concourse.bass as bass
import concourse.mybir as mybir
from concourse.bass_utils import run_bass_kernel_spmd

F32 = mybir.dt.float32
BF16 = mybir.dt.bfloat16
AF = mybir.ActivationFunctionType
ALU = mybir.AluOpType

D = 1024
HD = 64
GROUPS = ((128, 1), (512, 4), (2048, 16))
NBLK = (2, 5, 17)
RH = 4
RDK = 256
RDV = 512
DFF = 2816
IN_COLS = 11648
NUM_BUCKETS = 32
MAX_DISTANCE = 2048
LF = 2304
COL = dict(qa=0, ka=1152, va=2304, qr=3456, kr=4480, vr=5504, gr=7552, ga=9600, gb=10624)
SLOT_ELEMS = 2304
NSLOT = 7
LN_EPS = 1e-5
GN_EPS = 1e-5


def _t5_bucket(dist):
    max_exact = NUM_BUCKETS // 2
    large = max_exact + (np.log(np.maximum(dist, max_exact) / max_exact)
                         / np.log(MAX_DISTANCE / max_exact)
                         * (NUM_BUCKETS - max_exact)).astype(np.int32)
    large = np.minimum(large, NUM_BUCKETS - 1)
    return np.where(dist < max_exact, dist, large).astype(np.int32)


def make_consts(S):
    half = RDK // 2
    pos = np.arange(S, dtype=np.float32)
    inv_freq = (10000.0 ** (-np.arange(half, dtype=np.float32) / half)).astype(np.float32)
    ang = (pos[None, :] * inv_freq[:, None]).astype(np.float32)
    cos_t = np.cos(ang).astype(np.float32)
    sin_t = np.sin(ang).astype(np.float32)
    log_g = np.log(1.0 - 2.0 ** (-5.0 - np.arange(RH, dtype=np.float64)))
    n = np.arange(128, dtype=np.float64)
    gq = np.exp(log_g[:, None] * (n[None, :] + 1.0))
    GQ = np.broadcast_to(gq.reshape(1, 4 * 128), (128, 512)).astype(np.float32).copy()
    diff = n[None, :] - n[:, None]
    DM = np.zeros((128, 4, 128), np.float64)
    for h in range(RH):
        DM[:, h, :] = np.where(diff >= 0, np.exp(log_g[h] * np.maximum(diff, 0.0)), 0.0) / 16.0
    DM = DM.reshape(128, 512).astype(np.float32)
    GK = (np.exp(log_g[None, :] * (127.0 - n[:, None])) / 16.0).astype(np.float32)
    chunk_dec = [float(np.exp(log_g[h] * 128.0)) for h in range(RH)]
    ident = np.eye(128, dtype=np.float32)
    J = ident[::-1].copy()
    delta = np.arange(LF) - 127
    bk = _t5_bucket(np.clip(delta, 0, MAX_DISTANCE))
    OH = np.zeros((NUM_BUCKETS, LF), np.float32)
    OH[bk, np.arange(LF)] = 1.0
    VALID = np.zeros((18, LF), np.float32)
    for g, (win, dil) in enumerate(GROUPS):
        v = ((delta >= 0) & (delta % dil == 0) & (delta <= win)).astype(np.float32)
        VALID[6 * g:6 * g + 6, :] = v[None, :]
    return dict(cos_t=cos_t, sin_t=sin_t, GQ=GQ, DM=DM, GK=GK, ident=ident, J=J, OH=OH,
                VALID=VALID, zeros=np.zeros((128, 256), np.float32)), chunk_dec


class Buf:
    __slots__ = ("w", "rs", "name")

    def __init__(self, name=""):
        self.w = None
        self.rs = {}
        self.name = name


class _Src:
    def __init__(self, name, sem, step):
        self.name, self.sem, self.step, self.cnt = name, sem, step, 0


class Tracker:
    def __init__(self, nc, es):
        self.nc = nc
        self.es = es
        self.src = {}
        self.handles = {"pe": nc.tensor, "act": nc.scalar, "dve": nc.vector,
                        "pool": nc.gpsimd, "sp": nc.sync}
        for e in ("pe", "act", "dve", "pool"):
            self.src[e] = _Src(e, es.enter_context(nc.semaphore("s_" + e)), 1)
        self.seen = {e: {} for e in self.handles}
        self.dry = False

    def dsem(self, name):
        s = _Src(name, self.es.enter_context(self.nc.semaphore(name)), 16)
        self.src[name] = s
        return s

    def _waits(self, eng, R, W):
        need = {}

        def req(ev, same_ok):
            if ev is None:
                return
            s, c = ev
            if s == eng and not same_ok:
                return
            if c > need.get(s, 0):
                need[s] = c
        for b in R:
            req(b.w, True)
        for b in W:
            req(b.w, False)
            for s, c in b.rs.items():
                req((s, c), False)
        h = self.handles[eng]
        seen = self.seen[eng]
        for s, c in need.items():
            if seen.get(s, 0) >= c:
                continue
            src = self.src[s]
            assert c <= src.cnt, (eng, s, c, src.cnt)
            h.wait_ge(src.sem, c)
            seen[s] = c

    def op(self, eng, fn, R=(), W=(), inc=True):
        if self.dry:
            return None
        self._waits(eng, R, W)
        src = self.src[eng]
        ins = fn(self.handles[eng])
        ticket = src.cnt + 1
        if inc:
            ins.then_inc(src.sem, 1)
            src.cnt += 1
        else:
            assert eng == "pe"
        for b in R:
            b.rs[eng] = ticket
        for b in W:
            b.w = (eng, ticket)
            b.rs = {}
        return ins

    def dma(self, eng, ds, out, in_, R=(), W=()):
        if self.dry:
            return
        self._waits(eng, R, W)
        ins = self.handles[eng].dma_start(out=out, in_=in_)
        ins.then_inc(ds.sem, 16)
        ds.cnt += 16
        for b in R:
            b.rs[ds.name] = ds.cnt
        for b in W:
            b.w = (ds.name, ds.cnt)
            b.rs = {}

    def drain(self, eng, ds):
        if self.dry:
            return
        if self.seen[eng].get(ds.name, 0) < ds.cnt:
            self.handles[eng].wait_ge(ds.sem, ds.cnt)
            self.seen[eng][ds.name] = ds.cnt


def chunk_plan():
    P = []

    def win(tag, col0, width, cw):
        c = 0
        j = 0
        while c < width:
            w = min(cw, width - c)
            P.append(dict(name=f"{tag}{j}", src="w_in", kc=8, cw=w, row0=0, col0=col0 + c, bias=True))
            c += w
            j += 1
    win("qa", COL["qa"], 1152, 256)
    win("ka", COL["ka"], 1152, 256)
    win("va", COL["va"], 1152, 192)
    win("qr", COL["qr"], 1024, 256)
    win("kr", COL["kr"], 1024, 256)
    win("vr", COL["vr"], 2048, 256)
    win("gr", COL["gr"], 2048, 256)
    for half in range(2):
        for j in range(2):
            P.append(dict(name=f"ga{half}{j}", src="w_in", kc=8, cw=256, row0=0,
                          col0=COL["ga"] + half * 512 + j * 256, bias=True))
        for j in range(2):
            P.append(dict(name=f"gb{half}{j}", src="w_in", kc=8, cw=256, row0=0,
                          col0=COL["gb"] + half * 512 + j * 256, bias=True))
        P.append(dict(name=f"ap{half}", src="w_attn_proj", kc=3, cw=512, row0=0, col0=half * 512, bias=False))
        for j in range(4):
            P.append(dict(name=f"rp{half}{j}", src="w_ret_proj", kc=16, cw=128, row0=0,
                          col0=half * 512 + j * 128, bias=False))
    for j in range(4):
        P.append(dict(name=f"wo{j}", src="w_out", kc=8, cw=256, row0=0, col0=j * 256, bias=False))
    for j in range(11):
        P.append(dict(name=f"fg{j}", src="w_ffn_gate", kc=8, cw=256, row0=0, col0=j * 256, bias=False))
        P.append(dict(name=f"fu{j}", src="w_ffn_up", kc=8, cw=256, row0=0, col0=j * 256, bias=False))
        P.append(dict(name=f"fd{j}", src="w_ffn_down", kc=2, cw=1024, row0=j * 256, col0=0, bias=False))
    for c in P:
        c["kk"] = c["kc"] + (1 if c["bias"] else 0)
        assert c["kk"] * c["cw"] <= SLOT_ELEMS, c
    return P


def build_program(S, depth, alpha, chunk_dec):
    NT = S // 128
    nc = bass.Bass("TRN2", target_bir_lowering=False)
    es = contextlib.ExitStack()

    def din(name, shape, dt=F32):
        return nc.dram_tensor(name, list(shape), dt, kind="ExternalInput").ap()

    x_in = din("x", [S, D])
    rel_bias = din("rel_bias", [NUM_BUCKETS, 18])
    Wd = dict(w_in=din("w_in", [depth, D, IN_COLS]), w_attn_proj=din("w_attn_proj", [depth, 384, D]),
              w_ret_proj=din("w_ret_proj", [depth, 2048, D]), w_out=din("w_out", [depth, D, D]),
              w_ffn_gate=din("w_ffn_gate", [depth, D, DFF]), w_ffn_up=din("w_ffn_up", [depth, D, DFF]),
              w_ffn_down=din("w_ffn_down", [depth, DFF, D]))
    b_in = din("b_in", [depth, IN_COLS])
    ln_in = dict(ln1_g=din("ln1_g", [depth, D]), ln1_b=din("ln1_b", [depth, D]),
                 ln2_g=din("ln2_g", [depth, D]), ln2_b=din("ln2_b", [depth, D]))
    c_cos = din("cos_t", [128, S])
    c_sin = din("sin_t", [128, S])
    c_GQ = din("GQ", [128, 512])
    c_DM = din("DM", [128, 512])
    c_GK = din("GK", [128, 4])
    c_ident = din("ident", [128, 128])
    c_J = din("J", [128, 128])
    c_OH = din("OH", [NUM_BUCKETS, LF])
    c_VALID = din("VALID", [18, LF])
    c_zeros = din("zeros", [128, 256])
    y_out = nc.dram_tensor("y", [S, D], F32, kind="ExternalOutput").ap()
    xmid_d = [nc.dram_tensor(f"xl{l}", [S, D], F32, kind="Internal").ap() for l in range(depth - 1)]
    Fd = nc.dram_tensor("Fd", [18, LF], BF16, kind="Internal")
    plan = chunk_plan()
    wsc = [[nc.dram_tensor(f"ws{l}_{c['name']}", [128, c["kk"] * c["cw"]], BF16, kind="Internal").ap()
            for c in plan] for l in range(depth)]

    with es:
        tk = Tracker(nc, es)

        def sb(name, shape, dt):
            return es.enter_context(nc.sbuf_tensor("sb_" + name, list(shape), dt))

        slots = [sb(f"slot{i}", [128, SLOT_ELEMS], BF16) for i in range(NSLOT)]
        slot_b = [Buf(f"slot{i}") for i in range(NSLOT)]
        slot_sem = [tk.dsem(f"d_slot{i}") for i in range(NSLOT)]
        Et = [sb(f"Et{g}", [128, 6, NBLK[g], 128], BF16) for g in range(3)]
        Et_b = Buf("Et")
        Kr = [sb(f"Kr{g}", [128, 3, NBLK[g], 128], BF16) for g in range(3)]
        Vr = [sb(f"Vr{g}", [128, NBLK[g], 6, 65], BF16) for g in range(3)]
        Kr_b = [[Buf(f"Kr{g}_{s}") for s in range(NBLK[g])] for g in range(3)]
        Vr_b = [[Buf(f"Vr{g}_{s}") for s in range(NBLK[g])] for g in range(3)]
        St_f = sb("St_f", [128, 8, 512], F32)
        St_b = sb("St_b", [128, 8, 512], BF16)
        St_fb = [Buf(f"Stf{i}") for i in range(8)]
        St_bb = [Buf(f"Stb{i}") for i in range(8)]
        X = sb("X", [128, D], F32); X_b = Buf("X")
        M = sb("M", [128, D], F32); M_b = Buf("M")
        O = sb("O", [128, D], F32); O_b = Buf("O")
        xb = sb("xb", [128, D], BF16); xb_b = Buf("xb")
        xb2 = sb("xb2", [128, D], BF16); xb2_b = Buf("xb2")
        xT = sb("xT", [128, 8, 128], BF16); xT_b = Buf("xT")
        xmT = sb("xmT", [128, 8, 128], BF16); xmT_b = Buf("xmT")
        qaT = sb("qaT", [128, 9, 128], BF16); qaT_b = Buf("qaT")
        ta = sb("ta", [128, 512], F32); ta_b = Buf("ta")
        tb = sb("tb", [128, 512], F32); tb_b = Buf("tb")
        qrT = sb("qrT", [128, 8, 128], BF16); qrT_b = [Buf("qrT0"), Buf("qrT1")]
        krT = sb("krT", [128, 8, 128], BF16); krT_b = [Buf("krT0"), Buf("krT1")]
        qtT = sb("qtT", [128, 8, 128], BF16); qtT_b = [Buf("qtT0"), Buf("qtT1")]
        ktm = sb("ktm", [128, 1024], BF16); ktm_b = [Buf("ktm0"), Buf("ktm1")]
        vr = sb("vr", [128, 2048], BF16); vr_b = [Buf(f"vr{i}") for i in range(8)]
        sg = [sb(f"sg{i}", [128, 512], F32) for i in range(2)]; sg_b = [Buf("sg0"), Buf("sg1")]
        th = sb("th", [128, 512], F32); th_b = Buf("th")
        pexp2 = [sb(f"pexp{i}", [128, 512], BF16) for i in range(2)]; pexp2_b = [Buf("pexp0"), Buf("pexp1")]
        pexp = pexp2[0]; pexp_b = pexp2_b[0]
        PT = [sb(f"PT{i}", [128, 512], BF16) for i in range(2)]; PT_b = [Buf("PT0"), Buf("PT1")]
        scT = sb("scT", [128, 512], BF16); scT_b = Buf("scT")
        gn = sb("gn", [128, 512], F32); gn_b = Buf("gn")
        yb = [sb(f"yb{i}", [128, 512], BF16) for i in range(2)]; yb_b = [Buf("yb0"), Buf("yb1")]
        ybT = sb("ybT", [128, 16, 128], BF16); ybT_b = [Buf(f"ybT{i}") for i in range(4)]
        ya = sb("ya", [128, 6, 64], BF16); ya_b = Buf("ya")
        yaT = sb("yaT", [128, 3, 128], BF16); yaT_b = Buf("yaT")
        rL = sb("rL", [128, 8], F32); rL_b = Buf("rL")
        sgA = sb("sgA", [128, 512], F32); sgA_b = Buf("sgA")
        sgB = sb("sgB", [128, 512], F32); sgB_b = Buf("sgB")
        mT = sb("mT", [128, 8, 128], BF16); mT_b = [Buf("mT0"), Buf("mT1")]
        hT = [sb(f"hT{i}", [128, 2, 128], BF16) for i in range(2)]; hT_b = [Buf("hT0"), Buf("hT1")]
        cs = sb("cs", [128, 2, 128], F32); cs_b = Buf("cs")
        lnbc = sb("lnbc", [128, 2, D], F32); lnbc_b = Buf("lnbc")
        ident = sb("ident", [128, 128], BF16)
        Jm = sb("Jm", [128, 128], BF16)
        GQ = sb("GQ", [128, 512], F32)
        DM = sb("DM", [128, 512], F32)
        GK = sb("GK", [128, 4], F32)
        ones = sb("ones", [128, 128], BF16)
        nhalf = sb("nhalf", [128, 1], F32)
        stats = sb("stats", [128, 2, 6], F32); stats_b = Buf("stats")
        mv = sb("mv", [128, 2], F32); mv_b = Buf("mv")
        rstd = sb("rstd", [128, 1], F32); rstd_b = Buf("rstd")
        nb = sb("nb", [128, 1], F32); nb_b = Buf("nb")
        const_b = Buf("const")
        PS = [es.enter_context(nc.psum_tensor(f"ps{i}", [128, 512], F32)) for i in range(8)]
        PS_b = [Buf(f"ps{i}") for i in range(8)]
        rot_state = [0]
        rot_mode = [None]
        rot_sub = {"A": 0, "B": 0}

        def rot():
            m = rot_mode[0]
            if m is None:
                i = rot_state[0]
                rot_state[0] = (i + 1) % 5
                return i
            if m == "A":
                i = rot_sub[m]
                rot_sub[m] = (i + 1) % 3
                return i
            i = rot_sub[m]
            rot_sub[m] = (i + 1) % 2
            return 3 + i
        L0, L1, L2 = 5, 6, 7

        d_misc = tk.dsem("d_misc")
        d_x = tk.dsem("d_x")
        d_cs = tk.dsem("d_cs")
        d_ln = tk.dsem("d_ln")
        d_st = tk.dsem("d_st")
        d_F = tk.dsem("d_F")
        d_H = tk.dsem("d_H")
        NCG = 6
        d_cv = [[tk.dsem(f"d_cv{l}_{k}") for k in range(NCG)] for l in range(depth)]

        rb = sb("rb", [NUM_BUCKETS, 18], F32)
        d_misc2 = tk.dsem("d_misc2")
        tk.dma("pool", d_misc2, ident[:], c_ident, W=[const_b])
        tk.dma("pool", d_misc2, Jm[:], c_J, W=[const_b])
        tk.dma("sp", d_misc, GQ[:], c_GQ, W=[const_b])
        tk.dma("sp", d_misc, DM[:], c_DM, W=[const_b])
        tk.dma("sp", d_misc, GK[:], c_GK, W=[const_b])
        tk.dma("sp", d_misc, rb[:], rel_bias, W=[const_b])
        for e_ in ("pe", "act", "dve", "pool", "sp"):
            tk.drain(e_, d_misc)
            tk.drain(e_, d_misc2)
        tk.op("pool", lambda e: e.memset(ones[:], 1.0), W=[const_b])
        tk.op("pool", lambda e: e.memset(nhalf[:], -0.5), W=[const_b])
        for g in range(3):
            tk.op("pool", lambda e, g=g: e.memset(Vr[g][:], 1.0), W=Vr_b[g])

        def cgroup(ci):
            return min(NCG - 1, ci * NCG // len(plan))
        conv_todo = {l: list(range(len(plan))) for l in range(depth)}

        def emit_conv(l, n):
            if globals().get("SKIP_CONV"):
                conv_todo[l] = []
            for _ in range(n):
                if not conv_todo[l]:
                    return
                ci = conv_todo[l].pop(0)
                c = plan[ci]
                ds = d_cv[l][cgroup(ci)]
                kc, cw = c["kc"], c["cw"]
                src = Wd[c["src"]][l, c["row0"]:c["row0"] + kc * 128, c["col0"]:c["col0"] + cw]
                src = src.rearrange("(k p) c -> p k c", p=128)
                dst = wsc[l][ci][:, 0:kc * cw].rearrange("p (k c) -> p k c", k=kc)
                if tk.dry:
                    continue
                ins = nc.gpsimd.dma_start(out=dst, in_=src)
                ins.then_inc(ds.sem, 16)
                ds.cnt += 16
                if c["bias"]:
                    bsrc = b_in[l:l + 1, c["col0"]:c["col0"] + cw]
                    ins = nc.gpsimd.dma_start(out=wsc[l][ci][0:1, kc * cw:(kc + 1) * cw], in_=bsrc)
                    ins.then_inc(ds.sem, 16)
                    ds.cnt += 16
                    ins = nc.gpsimd.dma_start(out=wsc[l][ci][1:128, kc * cw:(kc + 1) * cw], in_=c_zeros[0:127, 0:cw])
                    ins.then_inc(ds.sem, 16)
                    ds.cnt += 16
        emit_conv(0, len(plan))

        d_oh = tk.dsem("d_oh")
        d_vl = tk.dsem("d_vl")
        Fd_b = Buf("Fd")
        nchk = (LF + 511) // 512
        for j in range(nchk):
            c0 = j * 512
            w = min(512, LF - c0)
            tk.dma("sp", d_oh, ta[0:NUM_BUCKETS, 0:w], c_OH[:, c0:c0 + w], W=[ta_b])
            tk.dma("sp", d_vl, tb[0:18, 0:w], c_VALID[:, c0:c0 + w], W=[tb_b])
            p = rot()
            tk.op("pe", lambda e: e.matmul(PS[p][0:18, 0:w], lhsT=rb[:, :], rhs=ta[0:NUM_BUCKETS, 0:w],
                                           start=True, stop=True), R=[ta_b, const_b], W=[PS_b[p]])
            tk.op("act", lambda e: e.activation(out=gn[0:18, 0:w], in_=PS[p][0:18, 0:w], func=AF.Exp),
                  R=[PS_b[p]], W=[gn_b])
            tk.op("dve", lambda e: e.tensor_tensor(out=pexp[0:18, 0:w], in0=gn[0:18, 0:w], in1=tb[0:18, 0:w], op=ALU.mult),
                  R=[gn_b, tb_b], W=[pexp_b])
            tk.dma("sp", d_F, Fd.ap()[:, c0:c0 + w], pexp[0:18, 0:w], R=[pexp_b], W=[Fd_b])
        Hs = St_b[:].rearrange("p a (b c) -> p (a b) c", c=128)
        for g in range(3):
            no = NBLK[g]
            for h in range(6):
                hh = 6 * g + h
                srcap = bass.AP(Fd, hh * LF, [[1, 128], [128, no], [1, 128]])
                tk.dma("sp", d_H, Hs[:, 0:no, :], srcap, R=[Fd_b], W=St_bb)
                o0 = 0
                while o0 < no:
                    n_ = min(4, no - o0)
                    p = rot()
                    tk.op("pe", lambda e: e.matmul(PS[p][:, 0:n_ * 128], lhsT=Jm[:], rhs=Hs[:, o0:o0 + n_, :],
                                                   start=True, stop=True), R=St_bb + [const_b], W=[PS_b[p]])
                    tk.op("dve", lambda e: e.tensor_copy(
                        Et[g][:, h, o0:o0 + n_, :], PS[p][:, 0:n_ * 128].rearrange("p (a b) -> p a b", b=128)),
                        R=[PS_b[p]], W=[Et_b])
                    o0 += n_

        stream = []
        name2ci = {c["name"]: i for i, c in enumerate(plan)}
        issue_ptr = [0]
        use_ptr = [0]
        cv_drained = set()

        def issue_next():
            k = issue_ptr[0]
            if k >= len(stream):
                return
            l, t, ci = stream[k]
            c = plan[ci]
            s = k % NSLOT
            key = (l, cgroup(ci))
            if key not in cv_drained:
                tk.drain("sp", d_cv[l][cgroup(ci)])
                cv_drained.add(key)
            n = c["kk"] * c["cw"]
            tk.dma("sp", slot_sem[s], slots[s][:, 0:n], wsc[l][ci][:, 0:n], W=[slot_b[s]])
            issue_ptr[0] += 1

        def next_chunk(l, t, name):
            if tk.dry:
                k = len(stream)
                stream.append((l, t, name2ci[name]))
            else:
                k = use_ptr[0]
                assert stream[k] == (l, t, name2ci[name]), (stream[k], l, t, name)
                use_ptr[0] += 1
                while issue_ptr[0] < min(len(stream), k + NSLOT):
                    issue_next()
            s = k % NSLOT
            c = plan[stream[k][2]]
            view = slots[s][:, 0:c["kk"] * c["cw"]].rearrange("p (k c) -> p k c", k=c["kk"])
            return view, slot_b[s], c

        def mm(out, lhsT, rhs, start, stop, R, W, inc=None):
            tk.op("pe", lambda e: e.matmul(out, lhsT=lhsT, rhs=rhs, start=start, stop=stop), R=R, W=W,
                  inc=stop if inc is None else inc)

        def fm_group(pbank, col, wv, wb, c, cb, act, act_b, bias=True):
            kc = c["kc"]
            for k in range(kc):
                mm(PS[pbank][:, col:col + 128], wv[:, k, cb * 128:(cb + 1) * 128], act[:, k, :],
                   k == 0, (k == kc - 1) and not bias, R=[wb] + act_b, W=[PS_b[pbank]])
            if bias:
                mm(PS[pbank][:, col:col + 128], wv[:, kc, cb * 128:(cb + 1) * 128], ones[:, :],
                   False, True, R=[wb, const_b], W=[PS_b[pbank]])

        def tm_group(pbank, col, wv, wb, c, act, act_b, bias=True):
            kc, cw = c["kc"], c["cw"]
            for k in range(kc):
                mm(PS[pbank][:, col:col + cw], act[:, k, :], wv[:, k, :], k == 0, (k == kc - 1) and not bias,
                   R=[wb] + act_b, W=[PS_b[pbank]])
            if bias:
                mm(PS[pbank][:, col:col + cw], ones[:, :], wv[:, kc, :], False, True,
                   R=[wb, const_b], W=[PS_b[pbank]])

        def transpose_to(dst, dst_b, src, src_b, nblk, evac_eng):
            j0 = 0
            while j0 < nblk:
                n_ = min(4, nblk - j0)
                p = rot()
                pv = PS[p][:].bitcast(BF16)
                for j in range(n_):
                    tk.op("pe", lambda e, j=j: e.transpose(pv[:, j * 128:(j + 1) * 128],
                                                           src[:, (j0 + j) * 128:(j0 + j + 1) * 128], ident[:]),
                          R=src_b + [const_b], W=[PS_b[p]])
                if evac_eng == "act":
                    tk.op("act", lambda e: e.copy(dst[:, j0:j0 + n_, :],
                                                  pv[:, 0:n_ * 128].rearrange("p (a b) -> p a b", b=128)),
                          R=[PS_b[p]], W=dst_b)
                else:
                    tk.op("dve", lambda e: e.tensor_copy(dst[:, j0:j0 + n_, :],
                                                         pv[:, 0:n_ * 128].rearrange("p (a b) -> p a b", b=128)),
                          R=[PS_b[p]], W=dst_b)
                j0 += n_

        def layer_norm(buf, buf_b, eps):
            for j in range(2):
                tk.op("dve", lambda e, j=j: e.bn_stats(stats[:, j, :], buf[:, j * 512:(j + 1) * 512]),
                      R=[buf_b], W=[stats_b])
            tk.op("dve", lambda e: e.bn_aggr(mv[:], stats[:].rearrange("p a b -> p (a b)")), R=[stats_b], W=[mv_b])
            tk.op("pool", lambda e: e.tensor_scalar(rstd[:], mv[:, 1:2], eps, None, ALU.add), R=[mv_b], W=[rstd_b])
            tk.op("pool", lambda e: e.tensor_tensor(out=rstd[:], in0=rstd[:], in1=nhalf[:], op=ALU.pow),
                  R=[rstd_b, const_b], W=[rstd_b])
            tk.op("dve", lambda e: e.tensor_scalar(buf[:], buf[:], mv[:, 0:1], rstd[:], ALU.subtract, ALU.mult),
                  R=[buf_b, mv_b, rstd_b], W=[buf_b])
            tk.op("dve", lambda e: e.tensor_tensor(out=buf[:], in0=buf[:], in1=lnbc[:, 0, :], op=ALU.mult),
                  R=[buf_b, lnbc_b], W=[buf_b])
            tk.op("dve", lambda e: e.tensor_tensor(out=buf[:], in0=buf[:], in1=lnbc[:, 1, :], op=ALU.add),
                  R=[buf_b, lnbc_b], W=[buf_b])

        def emit_all():
            for l in range(depth):
                src_x = x_in if l == 0 else xmid_d[l - 1]
                dst_x = y_out if l == depth - 1 else xmid_d[l]
                if l > 0:
                    tk.drain("sp", d_st)
                for i in range(8):
                    tk.op("pool", lambda e, i=i: e.memset(St_f[:, i, :], 0.0), W=[St_fb[i]])
                    tk.op("pool", lambda e, i=i: e.memset(St_b[:, i, :], 0.0), W=[St_bb[i]])
                tk.dma("sp", d_x, X[:], src_x[0:128, :], W=[X_b])
                def front(t):
                    t0 = t * 128
                    tk.op("act", lambda e: e.copy(xb2[:], X[:]), R=[X_b], W=[xb2_b])
                    transpose_to(xT, [xT_b], xb2, [xb2_b], 8, "act")
                    tk.dma("sp", d_cs, cs[:, 0, :], c_cos[:, t0:t0 + 128], W=[cs_b])
                    tk.dma("sp", d_cs, cs[:, 1, :], c_sin[:, t0:t0 + 128], W=[cs_b])

                    blk = 0
                    for j in range(5):
                        wv, wb, c = next_chunk(l, t, f"qa{j}")
                        nb_ = c["cw"] // 128
                        p = rot()
                        for cb in range(nb_):
                            fm_group(p, cb * 128, wv, wb, c, cb, xT, [xT_b])
                        tk.op("act", lambda e, p=p, nb_=nb_, blk=blk: e.copy(
                            qaT[:, blk:blk + nb_, :], PS[p][:, 0:nb_ * 128].rearrange("p (a b) -> p a b", b=128)),
                            R=[PS_b[p]], W=[qaT_b])
                        blk += nb_
                        yield
                    blk = 0
                    for j in range(5):
                        wv, wb, c = next_chunk(l, t, f"ka{j}")
                        nb_ = c["cw"] // 128
                        p = rot()
                        for cb in range(nb_):
                            fm_group(p, cb * 128, wv, wb, c, cb, xT, [xT_b])
                        for cb in range(nb_):
                            g, hp = divmod(blk + cb, 3)
                            s = t % NBLK[g]
                            tk.op("dve", lambda e, p=p, cb=cb, g=g, hp=hp, s=s: e.tensor_copy(
                                Kr[g][:, hp, s, :], PS[p][:, cb * 128:(cb + 1) * 128]),
                                R=[PS_b[p]], W=[Kr_b[g][s]])
                        blk += nb_
                        yield
                    for j in range(6):
                        wv, wb, c = next_chunk(l, t, f"va{j}")
                        g, hh = divmod(j, 2)
                        s = t % NBLK[g]
                        p = rot()
                        tm_group(p, 0, wv, wb, c, xT, [xT_b])
                        tk.op("act", lambda e, p=p, g=g, hh=hh, s=s: e.copy(
                            Vr[g][:, s, 3 * hh:3 * hh + 3, 0:64], PS[p][:, 0:192].rearrange("p (a b) -> p a b", b=64)),
                            R=[PS_b[p]], W=[Vr_b[g][s]])
                        yield
                def attn_core(t):
                    jobs = []
                    for h in range(6):
                        for g in range(3):
                            nvis = min(t + 1, NBLK[g])
                            o0 = 0
                            while o0 < nvis:
                                n_ = min(4, nvis - o0)
                                jobs.append((h, g, o0, n_))
                                o0 += n_
                    first_of_head = {}
                    last_of_head = {}
                    for idx, (h, g, o0, n_) in enumerate(jobs):
                        first_of_head.setdefault(h, idx)
                        last_of_head[h] = idx
                    Uv = PS[L2][:, 0:390].rearrange("p (a b) -> p a b", b=65)

                    def emit_scores(idx):
                        h, g, o0, n_ = jobs[idx]
                        hp, hs = divmod(h, 2)
                        p = rot()
                        for j in range(n_):
                            s = (t - (o0 + j)) % NBLK[g]
                            mm(PS[p][:, j * 128:(j + 1) * 128],
                               Kr[g][hs * 64:(hs + 1) * 64, hp, s, :], qaT[hs * 64:(hs + 1) * 64, 3 * g + hp, :],
                               True, True, R=[Kr_b[g][s], qaT_b], W=[PS_b[p]])
                        return p

                    def emit_soft(idx, p):
                        h, g, o0, n_ = jobs[idx]
                        w = n_ * 128
                        q = idx % 2
                        tk.op("act", lambda e: e.activation(out=pexp2[q][:, 0:w], in_=PS[p][:, 0:w], func=AF.Exp, scale=0.125),
                              R=[PS_b[p]], W=[pexp2_b[q]])
                        tk.op("dve", lambda e: e.tensor_tensor(
                            out=PT[q][:, 0:w], in0=pexp2[q][:, 0:w],
                            in1=Et[g][:, h, o0:o0 + n_, :].rearrange("p a b -> p (a b)"), op=ALU.mult),
                            R=[pexp2_b[q], Et_b], W=[PT_b[q]])

                    def emit_pv(idx):
                        h, g, o0, n_ = jobs[idx]
                        q = idx % 2
                        for j in range(n_):
                            s = (t - (o0 + j)) % NBLK[g]
                            first = (idx == first_of_head[h]) and j == 0
                            last = (idx == last_of_head[h]) and j == n_ - 1
                            mm(Uv[:, h, :], PT[q][:, j * 128:(j + 1) * 128], Vr[g][:, s, h, :], first, last,
                               R=[PT_b[q], Vr_b[g][s]], W=[PS_b[L2]], inc=last or (j == n_ - 1))
                            if not last and j == n_ - 1:
                                pass
                    pbank = {0: emit_scores(0)}
                    if len(jobs) > 1:
                        pbank[1] = emit_scores(1)
                    for idx in range(len(jobs)):
                        emit_soft(idx, pbank.pop(idx))
                        if idx + 2 < len(jobs):
                            pbank[idx + 2] = emit_scores(idx + 2)
                        emit_pv(idx)
                        yield
                    tk.op("dve", lambda e: e.reciprocal(rL[:, 0:6], Uv[:, :, 64:65].rearrange("p a b -> p (a b)")),
                          R=[PS_b[L2]], W=[rL_b])
                    tk.op("dve", lambda e: e.tensor_tensor(out=ya[:], in0=Uv[:, :, 0:64],
                                                           in1=rL[:, 0:6].unsqueeze(2).to_broadcast([128, 6, 64]), op=ALU.mult),
                          R=[PS_b[L2], rL_b], W=[ya_b])
                    transpose_to(yaT, [yaT_b], ya[:].rearrange("p a b -> p (a b)"), [ya_b], 3, "act")

                deferred = []

                def ret_proj(t):
                    def rotary(tag, dstT, dstT_b):
                        for bank in range(2):
                            p = rot()
                            for hh in range(2):
                                wv, wb, c = next_chunk(l, t, f"{tag}{bank * 2 + hh}")
                                for cb in range(2):
                                    fm_group(p, (hh * 2 + cb) * 128, wv, wb, c, cb, xT, [xT_b])
                                yield
                            pv = PS[p][:].rearrange("p (h f t) -> p h f t", h=2, f=2)
                            q1 = pv[:, :, 0, :]
                            q2 = pv[:, :, 1, :]
                            cosb = cs[:, 0:1, :].to_broadcast([128, 2, 128])
                            sinb = cs[:, 1:2, :].to_broadcast([128, 2, 128])
                            tav = ta[:, 0:256].rearrange("p (h t) -> p h t", h=2)
                            tbv = tb[:, 0:256].rearrange("p (h t) -> p h t", h=2)
                            dv = dstT[:, bank * 4:(bank + 1) * 4, :].rearrange("p (h f) t -> p h f t", f=2)
                            tk.op("dve", lambda e: e.tensor_tensor(out=tav, in0=q1, in1=cosb, op=ALU.mult),
                                  R=[PS_b[p], cs_b], W=[ta_b])
                            tk.op("dve", lambda e: e.tensor_tensor(out=tbv, in0=q2, in1=sinb, op=ALU.mult),
                                  R=[PS_b[p], cs_b], W=[tb_b])
                            tk.op("pool", lambda e: e.tensor_tensor(out=dv[:, :, 0, :], in0=tav, in1=tbv, op=ALU.subtract),
                                  R=[ta_b, tb_b], W=[dstT_b[bank]])
                            tk.op("dve", lambda e: e.tensor_tensor(out=tav, in0=q1, in1=sinb, op=ALU.mult),
                                  R=[PS_b[p], cs_b], W=[ta_b])
                            tk.op("dve", lambda e: e.tensor_tensor(out=tbv, in0=q2, in1=cosb, op=ALU.mult),
                                  R=[PS_b[p], cs_b], W=[tb_b])
                            tk.op("pool", lambda e: e.tensor_tensor(out=dv[:, :, 1, :], in0=tav, in1=tbv, op=ALU.add),
                                  R=[ta_b, tb_b], W=[dstT_b[bank]])
                    yield from rotary("qr", qrT, qrT_b)
                    for bank in range(2):
                        tk.op("pool", lambda e, bank=bank: e.tensor_tensor(
                            out=qtT[:, bank * 4:(bank + 1) * 4, :].rearrange("p (h f) t -> p h f t", f=2),
                            in0=qrT[:, bank * 4:(bank + 1) * 4, :].rearrange("p (h f) t -> p h f t", f=2),
                            in1=GQ[:, bank * 256:(bank + 1) * 256].rearrange("p (h t) -> p h t", h=2).unsqueeze(2).to_broadcast([128, 2, 2, 128]),
                            op=ALU.mult), R=[qrT_b[bank], const_b], W=[qtT_b[bank]])
                    yield from rotary("kr", krT, krT_b)
                    while deferred:
                        deferred.pop(0)()
                    for j in range(8):
                        wv, wb, c = next_chunk(l, t, f"vr{j}")
                        p = rot()
                        tm_group(p, 0, wv, wb, c, xT, [xT_b])
                        tk.op("act", lambda e, p=p, j=j: e.copy(vr[:, j * 256:(j + 1) * 256], PS[p][:, 0:256]),
                              R=[PS_b[p]], W=[vr_b[j]])
                        yield

                def ffn_steps(t):
                    def gu(j):
                        p = rot()
                        wv, wb, c = next_chunk(l, t, f"fg{j}")
                        for cb in range(2):
                            fm_group(p, cb * 128, wv, wb, c, cb, xmT, [xmT_b], bias=False)
                        wv, wb, c = next_chunk(l, t, f"fu{j}")
                        for cb in range(2):
                            fm_group(p, (2 + cb) * 128, wv, wb, c, cb, xmT, [xmT_b], bias=False)
                        tk.op("act", lambda e: e.activation(out=th[:, 0:256], in_=PS[p][:, 0:256], func=AF.Tanh, scale=0.5),
                              R=[PS_b[p]], W=[th_b])
                        tk.op("dve", lambda e: e.scalar_tensor_tensor(
                            out=th[:, 256:512], in0=th[:, 0:256], scalar=1.0, in1=PS[p][:, 0:256], op0=ALU.add, op1=ALU.mult),
                            R=[th_b, PS_b[p]], W=[th_b])
                        q = j % 2
                        tk.op("dve", lambda e: e.scalar_tensor_tensor(
                            out=hT[q][:].rearrange("p a b -> p (a b)"), in0=th[:, 256:512], scalar=0.5, in1=PS[p][:, 256:512],
                            op0=ALU.mult, op1=ALU.mult), R=[th_b, PS_b[p]], W=[hT_b[q]])

                    def down(j):
                        q = j % 2
                        wv, wb, c = next_chunk(l, t, f"fd{j}")
                        for hb, bank in enumerate((L0, L1)):
                            for k in range(2):
                                mm(PS[bank][:], hT[q][:, k, :], wv[:, k, hb * 512:(hb + 1) * 512],
                                   j == 0 and k == 0, j == 10 and k == 1, R=[hT_b[q], wb], W=[PS_b[bank]],
                                   inc=(k == 1 and hb == 1) or (j == 10 and k == 1))
                    gu(0)
                    yield
                    for j in range(11):
                        if j + 1 < 11:
                            gu(j + 1)
                        down(j)
                        yield

                def run(gen, mode=None):
                    rot_mode[0] = mode
                    try:
                        next(gen)
                        ok = True
                    except StopIteration:
                        ok = False
                    rot_mode[0] = None
                    return ok

                def chain(*gens):
                    for g_ in gens:
                        yield from g_

                for t in range(NT):
                    t0 = t * 128
                    if l + 1 < depth:
                        emit_conv(l + 1, len(plan) if t == NT - 1 else -(-len(plan) // NT))
                    if t == 0:
                        for _ in front(0):
                            pass
                        for _ in attn_core(0):
                            pass
                    tk.op("pool", lambda e: e.tensor_scalar(M[:], X[:], alpha, 0.0, ALU.mult, ALU.add), R=[X_b], W=[M_b])
                    if t + 1 < NT:
                        tk.dma("sp", d_x, X[:], src_x[t0 + 128:t0 + 256, :], W=[X_b])
                    for _ in ret_proj(t):
                        pass
                    for bank in range(2):
                        p = rot()
                        pv = PS[p][:].bitcast(BF16)
                        for j in range(4):
                            blk_ = bank * 4 + j
                            tk.op("pe", lambda e, j=j, blk_=blk_: e.transpose(pv[:, j * 128:(j + 1) * 128], krT[:, blk_, :], ident[:]),
                                  R=[krT_b[bank], const_b], W=[PS_b[p]])
                        tk.op("dve", lambda e, bank=bank, pv=pv: e.tensor_tensor(
                            out=ktm[:, bank * 512:(bank + 1) * 512].rearrange("p (h d) -> p h d", h=2),
                            in0=pv[:, 0:512].rearrange("p (h d) -> p h d", h=2),
                            in1=GK[:, bank * 2:bank * 2 + 2].unsqueeze(2).to_broadcast([128, 2, 256]), op=ALU.mult),
                            R=[PS_b[p], const_b], W=[ktm_b[bank]])
                    p = rot()
                    for h in range(4):
                        for f in range(2):
                            mm(PS[p][:, h * 128:(h + 1) * 128], krT[:, h * 2 + f, :], qrT[:, h * 2 + f, :], f == 0, f == 1,
                               R=[krT_b[h // 2], qrT_b[h // 2]], W=[PS_b[p]])
                    tk.op("dve", lambda e, p=p: e.tensor_tensor(out=scT[:], in0=PS[p][:], in1=DM[:], op=ALU.mult),
                          R=[PS_b[p], const_b], W=[scT_b])
                    def ret_stage1(h):
                        pg = rot()
                        for j in range(2):
                            wv, wb, c = next_chunk(l, t, f"gr{h * 2 + j}")
                            tm_group(pg, j * 256, wv, wb, c, xT, [xT_b])
                        tk.op("act", lambda e, pg=pg: e.activation(out=th[:], in_=PS[pg][:], func=AF.Tanh, scale=0.5),
                              R=[PS_b[pg]], W=[th_b])
                        sq = h % 2
                        tk.op("dve", lambda e, pg=pg, sq=sq: e.scalar_tensor_tensor(
                            out=sg[sq][:], in0=th[:], scalar=1.0, in1=PS[pg][:], op0=ALU.add, op1=ALU.mult),
                            R=[th_b, PS_b[pg]], W=[sg_b[sq]])
                        po = rot()
                        mm(PS[po][:], scT[:, h * 128:(h + 1) * 128], vr[:, h * 512:(h + 1) * 512], True, False,
                           R=[scT_b, vr_b[2 * h], vr_b[2 * h + 1]], W=[PS_b[po]])
                        for f in range(2):
                            mm(PS[po][:], qtT[:, h * 2 + f, :], St_b[:, h * 2 + f, :], False, f == 1,
                               R=[qtT_b[h // 2], St_bb[h * 2 + f]], W=[PS_b[po]])
                        tk.op("dve", lambda e, po=po: e.bn_stats(stats[:, 0, :], PS[po][:]), R=[PS_b[po]], W=[stats_b])
                        tk.op("dve", lambda e: e.bn_aggr(mv[:], stats[:, 0, :]), R=[stats_b], W=[mv_b])
                        tk.op("pool", lambda e: e.tensor_scalar(rstd[:], mv[:, 1:2], GN_EPS, None, ALU.add), R=[mv_b], W=[rstd_b])
                        tk.op("pool", lambda e: e.tensor_tensor(out=rstd[:], in0=rstd[:], in1=nhalf[:], op=ALU.pow),
                              R=[rstd_b, const_b], W=[rstd_b])
                        tk.op("dve", lambda e, po=po: e.tensor_scalar(gn[:], PS[po][:], mv[:, 0:1], rstd[:], ALU.subtract, ALU.mult),
                              R=[PS_b[po], mv_b, rstd_b], W=[gn_b])
                        tk.op("dve", lambda e, sq=sq: e.scalar_tensor_tensor(
                            out=yb[sq][:], in0=gn[:], scalar=0.5, in1=sg[sq][:], op0=ALU.mult, op1=ALU.mult),
                            R=[gn_b, sg_b[sq]], W=[yb_b[sq]])

                    def ret_stage2(h):
                        sq = h % 2
                        p = rot()
                        pv = PS[p][:].bitcast(BF16)
                        for j in range(4):
                            tk.op("pe", lambda e, j=j, sq=sq, pv=pv: e.transpose(pv[:, j * 128:(j + 1) * 128],
                                                                                 yb[sq][:, j * 128:(j + 1) * 128], ident[:]),
                                  R=[yb_b[sq], const_b], W=[PS_b[p]])
                        tk.op("act", lambda e, h=h, pv=pv: e.copy(ybT[:, 4 * h:4 * h + 4, :],
                                                                  pv[:, 0:512].rearrange("p (a b) -> p a b", b=128)),
                              R=[PS_b[p]], W=[ybT_b[h]])
                    ret_stage1(0)
                    for h in range(4):
                        if h + 1 < 4:
                            ret_stage1(h + 1)
                        ret_stage2(h)
                    for h in range(4):
                        for f in range(2):
                            i8 = h * 2 + f
                            p = rot()
                            mm(PS[p][:], ktm[:, i8 * 128:(i8 + 1) * 128], vr[:, h * 512:(h + 1) * 512], True, True,
                               R=[ktm_b[h // 2], vr_b[2 * h], vr_b[2 * h + 1]], W=[PS_b[p]])
                            tk.op("dve", lambda e, p=p, i8=i8, h=h: e.scalar_tensor_tensor(
                                out=St_f[:, i8, :], in0=St_f[:, i8, :], scalar=chunk_dec[h], in1=PS[p][:],
                                op0=ALU.mult, op1=ALU.add), R=[St_fb[i8], PS_b[p]], W=[St_fb[i8]])
                            tk.op("act", lambda e, i8=i8: e.copy(St_b[:, i8, :], St_f[:, i8, :]),
                                  R=[St_fb[i8]], W=[St_bb[i8]])

                    for half in range(2):
                        pA = rot()
                        for j in range(2):
                            wv, wb, c = next_chunk(l, t, f"ga{half}{j}")
                            for cb in range(2):
                                fm_group(pA, (j * 2 + cb) * 128, wv, wb, c, cb, xT, [xT_b])
                        tk.op("act", lambda e, pA=pA: e.activation(out=sgA[:], in_=PS[pA][:], func=AF.Tanh, scale=0.5),
                              R=[PS_b[pA]], W=[sgA_b])
                        tk.op("pool", lambda e: e.tensor_scalar(sgA[:], sgA[:], 0.5, 0.5, ALU.mult, ALU.add),
                              R=[sgA_b], W=[sgA_b])
                        pB = rot()
                        for j in range(2):
                            wv, wb, c = next_chunk(l, t, f"gb{half}{j}")
                            for cb in range(2):
                                fm_group(pB, (j * 2 + cb) * 128, wv, wb, c, cb, xT, [xT_b])
                        tk.op("act", lambda e, pB=pB: e.activation(out=sgB[:], in_=PS[pB][:], func=AF.Tanh, scale=0.5),
                              R=[PS_b[pB]], W=[sgB_b])
                        tk.op("pool", lambda e: e.tensor_scalar(sgB[:], sgB[:], 0.5, 0.5, ALU.mult, ALU.add),
                              R=[sgB_b], W=[sgB_b])
                        wv, wb, c = next_chunk(l, t, f"ap{half}")
                        pa = rot()
                        for cb in range(4):
                            fm_group(pa, cb * 128, wv, wb, c, cb, yaT, [yaT_b], bias=False)
                        tk.op("dve", lambda e, pa=pa: e.tensor_tensor(out=ta[:], in0=PS[pa][:], in1=sgA[:], op=ALU.mult),
                              R=[PS_b[pa], sgA_b], W=[ta_b])
                        pb_ = rot()
                        for j in range(4):
                            wv, wb, c = next_chunk(l, t, f"rp{half}{j}")
                            fm_group(pb_, j * 128, wv, wb, c, 0, ybT, ybT_b, bias=False)
                        tk.op("dve", lambda e, pb_=pb_: e.tensor_tensor(out=tb[:], in0=PS[pb_][:], in1=sgB[:], op=ALU.mult),
                              R=[PS_b[pb_], sgB_b], W=[tb_b])
                        tk.op("pool", lambda e, half=half: e.tensor_tensor(
                            out=mT[:, half * 4:(half + 1) * 4, :].rearrange("p a b -> p (a b)"), in0=ta[:], in1=tb[:], op=ALU.add),
                            R=[ta_b, tb_b], W=[mT_b[half]])

                    tk.dma("sp", d_ln, lnbc[:, 0, :], ln_in["ln1_g"][l:l + 1, :].to_broadcast([128, D]), W=[lnbc_b])
                    tk.dma("sp", d_ln, lnbc[:, 1, :], ln_in["ln1_b"][l:l + 1, :].to_broadcast([128, D]), W=[lnbc_b])
                    for j in range(4):
                        wv, wb, c = next_chunk(l, t, f"wo{j}")
                        bank = L0 if j < 2 else L1
                        tm_group(bank, (j % 2) * 256, wv, wb, c, mT, mT_b, bias=False)
                    for j, bank in enumerate((L0, L1)):
                        tk.op("dve", lambda e, j=j, bank=bank: e.tensor_tensor(
                            out=M[:, j * 512:(j + 1) * 512], in0=M[:, j * 512:(j + 1) * 512], in1=PS[bank][:], op=ALU.add),
                            R=[M_b, PS_b[bank]], W=[M_b])
                    layer_norm(M, M_b, LN_EPS)
                    gA = chain(front(t + 1), attn_core(t + 1)) if t + 1 < NT else iter(())
                    a_live = True
                    for _ in range(16):
                        a_live = a_live and run(gA, "A")
                    tk.op("act", lambda e: e.copy(xb[:], M[:]), R=[M_b], W=[xb_b])
                    transpose_to(xmT, [xmT_b], xb, [xb_b], 8, "dve")

                    tk.dma("sp", d_ln, lnbc[:, 0, :], ln_in["ln2_g"][l:l + 1, :].to_broadcast([128, D]), R=[M_b], W=[lnbc_b])
                    tk.dma("sp", d_ln, lnbc[:, 1, :], ln_in["ln2_b"][l:l + 1, :].to_broadcast([128, D]), R=[M_b], W=[lnbc_b])
                    gE = ffn_steps(t)
                    e_live = True
                    while e_live:
                        e_live = run(gE, "B")
                        for _ in range(6):
                            a_live = a_live and run(gA, "A")
                    while a_live:
                        a_live = run(gA, "A")
                    for j, bank in enumerate((L0, L1)):
                        tk.op("dve", lambda e, j=j, bank=bank: e.scalar_tensor_tensor(
                            out=O[:, j * 512:(j + 1) * 512], in0=M[:, j * 512:(j + 1) * 512], scalar=alpha, in1=PS[bank][:],
                            op0=ALU.mult, op1=ALU.add), R=[M_b, PS_b[bank]], W=[O_b])
                    def _ln2_store(t0=t0):
                        layer_norm(O, O_b, LN_EPS)
                        tk.dma("sp", d_st, dst_x[t0:t0 + 128, :], O[:], R=[O_b])
                    deferred.append(_ln2_store)
                    if t + 1 >= NT:
                        while deferred:
                            deferred.pop(0)()

        tk.dry = True
        emit_all()
        tk.dry = False
        conv_todo.update({l: ([] if l == 0 else list(range(len(plan)))) for l in range(depth)})
        rot_state[0] = 0
        rot_sub.update({"A": 0, "B": 0})
        for _ in range(NSLOT - 1):
            issue_next()
        emit_all()
        tk.drain("sp", d_st)
    return nc


_PROG_CACHE = {}


def _run(inputs, S, depth, n_cores):
    alpha = float((2 * depth) ** 0.25)
    consts, chunk_dec = make_consts(S)
    key = (S, depth)
    if key not in _PROG_CACHE:
        _PROG_CACHE[key] = build_program(S, depth, alpha, chunk_dec)
    nc = _PROG_CACHE[key]
    x = np.ascontiguousarray(np.asarray(inputs["x"], dtype=np.float32))
    shared = {k: np.ascontiguousarray(np.asarray(v, dtype=np.float32)) for k, v in inputs.items() if k != "x"}
    shared.update(consts)
    in_maps = []
    for c in range(n_cores):
        m = dict(shared)
        m["x"] = x[c]
        in_maps.append(m)
    res = run_bass_kernel_spmd(nc, in_maps, core_ids=list(range(n_cores)))
    return np.stack([np.asarray(r["y"], dtype=np.float32) for r in res.results], axis=0)


def kernel(**inputs):
    x = np.asarray(inputs["x"])
    B, S, _ = x.shape
    depth = int(np.asarray(inputs["w_in"]).shape[0])
    return _run(inputs, S, depth, B)
```

```python
import contextlib
import numpy as np
import concourse.bass as bass
import concourse.mybir as mybir
from concourse.bass_utils import run_bass_kernel_spmd

F32 = mybir.dt.float32
BF16 = mybir.dt.bfloat16
AF = mybir.ActivationFunctionType
ALU = mybir.AluOpType

D = 1024
HD = 64
GROUPS = ((128, 1), (512, 4), (2048, 16))
NBLK = (2, 5, 17)
RH = 4
RDK = 256
RDV = 512
DFF = 2816
IN_COLS = 11648
NUM_BUCKETS = 32
MAX_DISTANCE = 2048
LF = 2304
COL = dict(qa=0, ka=1152, va=2304, qr=3456, kr=4480, vr=5504, gr=7552, ga=9600, gb=10624)
SLOT_ELEMS = 2304
NSLOT = 7
LN_EPS = 1e-5
GN_EPS = 1e-5


def _t5_bucket(dist):
    max_exact = NUM_BUCKETS // 2
    large = max_exact + (np.log(np.maximum(dist, max_exact) / max_exact)
                         / np.log(MAX_DISTANCE / max_exact)
                         * (NUM_BUCKETS - max_exact)).astype(np.int32)
    large = np.minimum(large, NUM_BUCKETS - 1)
    return np.where(dist < max_exact, dist, large).astype(np.int32)


def make_consts(S):
    half = RDK // 2
    pos = np.arange(S, dtype=np.float32)
    inv_freq = (10000.0 ** (-np.arange(half, dtype=np.float32) / half)).astype(np.float32)
    ang = (pos[None, :] * inv_freq[:, None]).astype(np.float32)
    cos_t = np.cos(ang).astype(np.float32)
    sin_t = np.sin(ang).astype(np.float32)
    log_g = np.log(1.0 - 2.0 ** (-5.0 - np.arange(RH, dtype=np.float64)))
    n = np.arange(128, dtype=np.float64)
    gq = np.exp(log_g[:, None] * (n[None, :] + 1.0))
    GQ = np.broadcast_to(gq.reshape(1, 4 * 128), (128, 512)).astype(np.float32).copy()
    diff = n[None, :] - n[:, None]
    DM = np.zeros((128, 4, 128), np.float64)
    for h in range(RH):
        DM[:, h, :] = np.where(diff >= 0, np.exp(log_g[h] * np.maximum(diff, 0.0)), 0.0) / 16.0
    DM = DM.reshape(128, 512).astype(np.float32)
    GK = (np.exp(log_g[None, :] * (127.0 - n[:, None])) / 16.0).astype(np.float32)
    chunk_dec = [float(np.exp(log_g[h] * 128.0)) for h in range(RH)]
    ident = np.eye(128, dtype=np.float32)
    J = ident[::-1].copy()
    delta = np.arange(LF) - 127
    bk = _t5_bucket(np.clip(delta, 0, MAX_DISTANCE))
    OH = np.zeros((NUM_BUCKETS, LF), np.float32)
    OH[bk, np.arange(LF)] = 1.0
    VALID = np.zeros((18, LF), np.float32)
    for g, (win, dil) in enumerate(GROUPS):
        v = ((delta >= 0) & (delta % dil == 0) & (delta <= win)).astype(np.float32)
        VALID[6 * g:6 * g + 6, :] = v[None, :]
    return dict(cos_t=cos_t, sin_t=sin_t, GQ=GQ, DM=DM, GK=GK, ident=ident, J=J, OH=OH,
                VALID=VALID, zeros=np.zeros((128, 256), np.float32)), chunk_dec


class Buf:
    __slots__ = ("w", "rs", "name")

    def __init__(self, name=""):
        self.w = None
        self.rs = {}
        self.name = name


class _Src:
    def __init__(self, name, sem, step):
        self.name, self.sem, self.step, self.cnt = name, sem, step, 0


class Tracker:
    def __init__(self, nc, es):
        self.nc = nc
        self.es = es
        self.src = {}
        self.handles = {"pe": nc.tensor, "act": nc.scalar, "dve": nc.vector,
                        "pool": nc.gpsimd, "sp": nc.sync}
        for e in ("pe", "act", "dve", "pool"):
            self.src[e] = _Src(e, es.enter_context(nc.semaphore("s_" + e)), 1)
        self.seen = {e: {} for e in self.handles}
        self.dry = False

    def dsem(self, name):
        s = _Src(name, self.es.enter_context(self.nc.semaphore(name)), 16)
        self.src[name] = s
        return s

    def _waits(self, eng, R, W):
        need = {}

        def req(ev, same_ok):
            if ev is None:
                return
            s, c = ev
            if s == eng and not same_ok:
                return
            if c > need.get(s, 0):
                need[s] = c
        for b in R:
            req(b.w, True)
        for b in W:
            req(b.w, False)
            for s, c in b.rs.items():
                req((s, c), False)
        h = self.handles[eng]
        seen = self.seen[eng]
        for s, c in need.items():
            if seen.get(s, 0) >= c:
                continue
            src = self.src[s]
            assert c <= src.cnt, (eng, s, c, src.cnt)
            h.wait_ge(src.sem, c)
            seen[s] = c

    def op(self, eng, fn, R=(), W=(), inc=True):
        if self.dry:
            return None
        self._waits(eng, R, W)
        src = self.src[eng]
        ins = fn(self.handles[eng])
        ticket = src.cnt + 1
        if inc:
            ins.then_inc(src.sem, 1)
            src.cnt += 1
        else:
            assert eng == "pe"
        for b in R:
            b.rs[eng] = ticket
        for b in W:
            b.w = (eng, ticket)
            b.rs = {}
        return ins

    def dma(self, eng, ds, out, in_, R=(), W=()):
        if self.dry:
            return
        self._waits(eng, R, W)
        ins = self.handles[eng].dma_start(out=out, in_=in_)
        ins.then_inc(ds.sem, 16)
        ds.cnt += 16
        for b in R:
            b.rs[ds.name] = ds.cnt
        for b in W:
            b.w = (ds.name, ds.cnt)
            b.rs = {}

    def drain(self, eng, ds):
        if self.dry:
            return
        if self.seen[eng].get(ds.name, 0) < ds.cnt:
            self.handles[eng].wait_ge(ds.sem, ds.cnt)
            self.seen[eng][ds.name] = ds.cnt


def chunk_plan():
    P = []

    def win(tag, col0, width, cw):
        c = 0
        j = 0
        while c < width:
            w = min(cw, width - c)
            P.append(dict(name=f"{tag}{j}", src="w_in", kc=8, cw=w, row0=0, col0=col0 + c, bias=True))
            c += w
            j += 1
    win("qa", COL["qa"], 1152, 256)
    win("ka", COL["ka"], 1152, 256)
    win("va", COL["va"], 1152, 192)
    win("qr", COL["qr"], 1024, 256)
    win("kr", COL["kr"], 1024, 256)
    win("vr", COL["vr"], 2048, 256)
    win("gr", COL["gr"], 2048, 256)
    for half in range(2):
        for j in range(2):
            P.append(dict(name=f"ga{half}{j}", src="w_in", kc=8, cw=256, row0=0,
                          col0=COL["ga"] + half * 512 + j * 256, bias=True))
        for j in range(2):
            P.append(dict(name=f"gb{half}{j}", src="w_in", kc=8, cw=256, row0=0,
                          col0=COL["gb"] + half * 512 + j * 256, bias=True))
        P.append(dict(name=f"ap{half}", src="w_attn_proj", kc=3, cw=512, row0=0, col0=half * 512, bias=False))
        for j in range(4):
            P.append(dict(name=f"rp{half}{j}", src="w_ret_proj", kc=16, cw=128, row0=0,
                          col0=half * 512 + j * 128, bias=False))
    for j in range(4):
        P.append(dict(name=f"wo{j}", src="w_out", kc=8, cw=256, row0=0, col0=j * 256, bias=False))
    for j in range(11):
        P.append(dict(name=f"fg{j}", src="w_ffn_gate", kc=8, cw=256, row0=0, col0=j * 256, bias=False))
        P.append(dict(name=f"fu{j}", src="w_ffn_up", kc=8, cw=256, row0=0, col0=j * 256, bias=False))
        P.append(dict(name=f"fd{j}", src="w_ffn_down", kc=2, cw=1024, row0=j * 256, col0=0, bias=False))
    for c in P:
        c["kk"] = c["kc"] + (1 if c["bias"] else 0)
        assert c["kk"] * c["cw"] <= SLOT_ELEMS, c
    return P


def build_program(S, depth, alpha, chunk_dec):
    NT = S // 128
    nc = bass.Bass("TRN2", target_bir_lowering=False)
    es = contextlib.ExitStack()

    def din(name, shape, dt=F32):
        return nc.dram_tensor(name, list(shape), dt, kind="ExternalInput").ap()

    x_in = din("x", [S, D])
    rel_bias = din("rel_bias", [NUM_BUCKETS, 18])
    Wd = dict(w_in=din("w_in", [depth, D, IN_COLS]), w_attn_proj=din("w_attn_proj", [depth, 384, D]),
              w_ret_proj=din("w_ret_proj", [depth, 2048, D]), w_out=din("w_out", [depth, D, D]),
              w_ffn_gate=din("w_ffn_gate", [depth, D, DFF]), w_ffn_up=din("w_ffn_up", [depth, D, DFF]),
              w_ffn_down=din("w_ffn_down", [depth, DFF, D]))
    b_in = din("b_in", [depth, IN_COLS])
    ln_in = dict(ln1_g=din("ln1_g", [depth, D]), ln1_b=din("ln1_b", [depth, D]),
                 ln2_g=din("ln2_g", [depth, D]), ln2_b=din("ln2_b", [depth, D]))
    c_cos = din("cos_t", [128, S])
    c_sin = din("sin_t", [128, S])
    c_GQ = din("GQ", [128, 512])
    c_DM = din("DM", [128, 512])
    c_GK = din("GK", [128, 4])
    c_ident = din("ident", [128, 128])
    c_J = din("J", [128, 128])
    c_OH = din("OH", [NUM_BUCKETS, LF])
    c_VALID = din("VALID", [18, LF])
    c_zeros = din("zeros", [128, 256])
    y_out = nc.dram_tensor("y", [S, D], F32, kind="ExternalOutput").ap()
    xmid_d = [nc.dram_tensor(f"xl{l}", [S, D], F32, kind="Internal").ap() for l in range(depth - 1)]
    Fd = nc.dram_tensor("Fd", [18, LF], BF16, kind="Internal")
    plan = chunk_plan()
    wsc = [[nc.dram_tensor(f"ws{l}_{c['name']}", [128, c["kk"] * c["cw"]], BF16, kind="Internal").ap()
            for c in plan] for l in range(depth)]

    with es:
        tk = Tracker(nc, es)

        def sb(name, shape, dt):
            return es.enter_context(nc.sbuf_tensor("sb_" + name, list(shape), dt))

        slots = [sb(f"slot{i}", [128, SLOT_ELEMS], BF16) for i in range(NSLOT)]
        slot_b = [Buf(f"slot{i}") for i in range(NSLOT)]
        slot_sem = [tk.dsem(f"d_slot{i}") for i in range(NSLOT)]
        Et = [sb(f"Et{g}", [128, 6, NBLK[g], 128], BF16) for g in range(3)]
        Et_b = Buf("Et")
        Kr = [sb(f"Kr{g}", [128, 3, NBLK[g], 128], BF16) for g in range(3)]
        Vr = [sb(f"Vr{g}", [128, NBLK[g], 6, 65], BF16) for g in range(3)]
        Kr_b = [[Buf(f"Kr{g}_{s}") for s in range(NBLK[g])] for g in range(3)]
        Vr_b = [[Buf(f"Vr{g}_{s}") for s in range(NBLK[g])] for g in range(3)]
        St_f = sb("St_f", [128, 8, 512], F32)
        St_b = sb("St_b", [128, 8, 512], BF16)
        St_fb = [Buf(f"Stf{i}") for i in range(8)]
        St_bb = [Buf(f"Stb{i}") for i in range(8)]
        X = sb("X", [128, D], F32); X_b = Buf("X")
        M = sb("M", [128, D], F32); M_b = Buf("M")
        O = sb("O", [128, D], F32); O_b = Buf("O")
        xb = sb("xb", [128, D], BF16); xb_b = Buf("xb")
        xb2 = sb("xb2", [128, D], BF16); xb2_b = Buf("xb2")
        xT = sb("xT", [128, 8, 128], BF16); xT_b = Buf("xT")
        xmT = sb("xmT", [128, 8, 128], BF16); xmT_b = Buf("xmT")
        qaT = sb("qaT", [128, 9, 128], BF16); qaT_b = Buf("qaT")
        ta = sb("ta", [128, 512], F32); ta_b = Buf("ta")
        tb = sb("tb", [128, 512], F32); tb_b = Buf("tb")
        qrT = sb("qrT", [128, 8, 128], BF16); qrT_b = [Buf("qrT0"), Buf("qrT1")]
        krT = sb("krT", [128, 8, 128], BF16); krT_b = [Buf("krT0"), Buf("krT1")]
        qtT = sb("qtT", [128, 8, 128], BF16); qtT_b = [Buf("qtT0"), Buf("qtT1")]
        ktm = sb("ktm", [128, 1024], BF16); ktm_b = [Buf("ktm0"), Buf("ktm1")]
        vr = sb("vr", [128, 2048], BF16); vr_b = [Buf(f"vr{i}") for i in range(8)]
        sg = [sb(f"sg{i}", [128, 512], F32) for i in range(2)]; sg_b = [Buf("sg0"), Buf("sg1")]
        th = sb("th", [128, 512], F32); th_b = Buf("th")
        pexp2 = [sb(f"pexp{i}", [128, 512], BF16) for i in range(2)]; pexp2_b = [Buf("pexp0"), Buf("pexp1")]
        pexp = pexp2[0]; pexp_b = pexp2_b[0]
        PT = [sb(f"PT{i}", [128, 512], BF16) for i in range(2)]; PT_b = [Buf("PT0"), Buf("PT1")]
        scT = sb("scT", [128, 512], BF16); scT_b = Buf("scT")
        gn = sb("gn", [128, 512], F32); gn_b = Buf("gn")
        yb = [sb(f"yb{i}", [128, 512], BF16) for i in range(2)]; yb_b = [Buf("yb0"), Buf("yb1")]
        ybT = sb("ybT", [128, 16, 128], BF16); ybT_b = [Buf(f"ybT{i}") for i in range(4)]
        ya = sb("ya", [128, 6, 64], BF16); ya_b = Buf("ya")
        yaT = sb("yaT", [128, 3, 128], BF16); yaT_b = Buf("yaT")
        rL = sb("rL", [128, 8], F32); rL_b = Buf("rL")
        sgA = sb("sgA", [128, 512], F32); sgA_b = Buf("sgA")
        sgB = sb("sgB", [128, 512], F32); sgB_b = Buf("sgB")
        mT = sb("mT", [128, 8, 128], BF16); mT_b = [Buf("mT0"), Buf("mT1")]
        hT = [sb(f"hT{i}", [128, 2, 128], BF16) for i in range(2)]; hT_b = [Buf("hT0"), Buf("hT1")]
        cs = sb("cs", [128, 2, 128], F32); cs_b = Buf("cs")
        lnbc = sb("lnbc", [128, 2, D], F32); lnbc_b = Buf("lnbc")
        ident = sb("ident", [128, 128], BF16)
        Jm = sb("Jm", [128, 128], BF16)
        GQ = sb("GQ", [128, 512], F32)
        DM = sb("DM", [128, 512], F32)
        GK = sb("GK", [128, 4], F32)
        ones = sb("ones", [128, 128], BF16)
        nhalf = sb("nhalf", [128, 1], F32)
        stats = sb("stats", [128, 2, 6], F32); stats_b = Buf("stats")
        mv = sb("mv", [128, 2], F32); mv_b = Buf("mv")
        rstd = sb("rstd", [128, 1], F32); rstd_b = Buf("rstd")
        nb = sb("nb", [128, 1], F32); nb_b = Buf("nb")
        const_b = Buf("const")
        PS = [es.enter_context(nc.psum_tensor(f"ps{i}", [128, 512], F32)) for i in range(8)]
        PS_b = [Buf(f"ps{i}") for i in range(8)]
        rot_state = [0]
        rot_mode = [None]
        rot_sub = {"A": 0, "B": 0}

        def rot():
            m = rot_mode[0]
            if m is None:
                i = rot_state[0]
                rot_state[0] = (i + 1) % 5
                return i
            if m == "A":
                i = rot_sub[m]
                rot_sub[m] = (i + 1) % 3
                return i
            i = rot_sub[m]
            rot_sub[m] = (i + 1) % 2
            return 3 + i
        L0, L1, L2 = 5, 6, 7

        d_misc = tk.dsem("d_misc")
        d_x = tk.dsem("d_x")
        d_cs = tk.dsem("d_cs")
        d_ln = tk.dsem("d_ln")
        d_st = tk.dsem("d_st")
        d_F = tk.dsem("d_F")
        d_H = tk.dsem("d_H")
        NCG = 6
        d_cv = [[tk.dsem(f"d_cv{l}_{k}") for k in range(NCG)] for l in range(depth)]

        rb = sb("rb", [NUM_BUCKETS, 18], F32)
        d_misc2 = tk.dsem("d_misc2")
        tk.dma("pool", d_misc2, ident[:], c_ident, W=[const_b])
        tk.dma("pool", d_misc2, Jm[:], c_J, W=[const_b])
        tk.dma("sp", d_misc, GQ[:], c_GQ, W=[const_b])
        tk.dma("sp", d_misc, DM[:], c_DM, W=[const_b])
        tk.dma("sp", d_misc, GK[:], c_GK, W=[const_b])
        tk.dma("sp", d_misc, rb[:], rel_bias, W=[const_b])
        for e_ in ("pe", "act", "dve", "pool", "sp"):
            tk.drain(e_, d_misc)
            tk.drain(e_, d_misc2)
        tk.op("pool", lambda e: e.memset(ones[:], 1.0), W=[const_b])
        tk.op("pool", lambda e: e.memset(nhalf[:], -0.5), W=[const_b])
        for g in range(3):
            tk.op("pool", lambda e, g=g: e.memset(Vr[g][:], 1.0), W=Vr_b[g])

        def cgroup(ci):
            return min(NCG - 1, ci * NCG // len(plan))
        conv_todo = {l: list(range(len(plan))) for l in range(depth)}

        def emit_conv(l, n):
            for _ in range(n):
                if not conv_todo[l]:
                    return
                ci = conv_todo[l].pop(0)
                c = plan[ci]
                ds = d_cv[l][cgroup(ci)]
                kc, cw = c["kc"], c["cw"]
                src = Wd[c["src"]][l, c["row0"]:c["row0"] + kc * 128, c["col0"]:c["col0"] + cw]
                src = src.rearrange("(k p) c -> p k c", p=128)
                dst = wsc[l][ci][:, 0:kc * cw].rearrange("p (k c) -> p k c", k=kc)
                if tk.dry:
                    continue
                ins = nc.gpsimd.dma_start(out=dst, in_=src)
                ins.then_inc(ds.sem, 16)
                ds.cnt += 16
                if c["bias"]:
                    bsrc = b_in[l:l + 1, c["col0"]:c["col0"] + cw]
                    ins = nc.gpsimd.dma_start(out=wsc[l][ci][0:1, kc * cw:(kc + 1) * cw], in_=bsrc)
                    ins.then_inc(ds.sem, 16)
                    ds.cnt += 16
                    ins = nc.gpsimd.dma_start(out=wsc[l][ci][1:128, kc * cw:(kc + 1) * cw], in_=c_zeros[0:127, 0:cw])
                    ins.then_inc(ds.sem, 16)
                    ds.cnt += 16
        emit_conv(0, len(plan))

        d_oh = tk.dsem("d_oh")
        d_vl = tk.dsem("d_vl")
        Fd_b = Buf("Fd")
        nchk = (LF + 511) // 512
        for j in range(nchk):
            c0 = j * 512
            w = min(512, LF - c0)
            tk.dma("sp", d_oh, ta[0:NUM_BUCKETS, 0:w], c_OH[:, c0:c0 + w], W=[ta_b])
            tk.dma("sp", d_vl, tb[0:18, 0:w], c_VALID[:, c0:c0 + w], W=[tb_b])
            p = rot()
            tk.op("pe", lambda e: e.matmul(PS[p][0:18, 0:w], lhsT=rb[:, :], rhs=ta[0:NUM_BUCKETS, 0:w],
                                           start=True, stop=True), R=[ta_b, const_b], W=[PS_b[p]])
            tk.op("act", lambda e: e.activation(out=gn[0:18, 0:w], in_=PS[p][0:18, 0:w], func=AF.Exp),
                  R=[PS_b[p]], W=[gn_b])
            tk.op("dve", lambda e: e.tensor_tensor(out=pexp[0:18, 0:w], in0=gn[0:18, 0:w], in1=tb[0:18, 0:w], op=ALU.mult),
                  R=[gn_b, tb_b], W=[pexp_b])
            tk.dma("sp", d_F, Fd.ap()[:, c0:c0 + w], pexp[0:18, 0:w], R=[pexp_b], W=[Fd_b])
        Hs = St_b[:].rearrange("p a (b c) -> p (a b) c", c=128)
        for g in range(3):
            no = NBLK[g]
            for h in range(6):
                hh = 6 * g + h
                srcap = bass.AP(Fd, hh * LF, [[1, 128], [128, no], [1, 128]])
                tk.dma("sp", d_H, Hs[:, 0:no, :], srcap, R=[Fd_b], W=St_bb)
                o0 = 0
                while o0 < no:
                    n_ = min(4, no - o0)
                    p = rot()
                    tk.op("pe", lambda e: e.matmul(PS[p][:, 0:n_ * 128], lhsT=Jm[:], rhs=Hs[:, o0:o0 + n_, :],
                                                   start=True, stop=True), R=St_bb + [const_b], W=[PS_b[p]])
                    tk.op("dve", lambda e: e.tensor_copy(
                        Et[g][:, h, o0:o0 + n_, :], PS[p][:, 0:n_ * 128].rearrange("p (a b) -> p a b", b=128)),
                        R=[PS_b[p]], W=[Et_b])
                    o0 += n_

        stream = []
        name2ci = {c["name"]: i for i, c in enumerate(plan)}
        issue_ptr = [0]
        use_ptr = [0]
        cv_drained = set()

        def issue_next():
            k = issue_ptr[0]
            if k >= len(stream):
                return
            l, t, ci = stream[k]
            c = plan[ci]
            s = k % NSLOT
            key = (l, cgroup(ci))
            if key not in cv_drained:
                tk.drain("sp", d_cv[l][cgroup(ci)])
                cv_drained.add(key)
            n = c["kk"] * c["cw"]
            tk.dma("sp", slot_sem[s], slots[s][:, 0:n], wsc[l][ci][:, 0:n], W=[slot_b[s]])
            issue_ptr[0] += 1

        def next_chunk(l, t, name):
            if tk.dry:
                k = len(stream)
                stream.append((l, t, name2ci[name]))
            else:
                k = use_ptr[0]
                assert stream[k] == (l, t, name2ci[name]), (stream[k], l, t, name)
                use_ptr[0] += 1
                while issue_ptr[0] < min(len(stream), k + NSLOT):
                    issue_next()
            s = k % NSLOT
            c = plan[stream[k][2]]
            view = slots[s][:, 0:c["kk"] * c["cw"]].rearrange("p (k c) -> p k c", k=c["kk"])
            return view, slot_b[s], c

        def mm(out, lhsT, rhs, start, stop, R, W, inc=None):
            tk.op("pe", lambda e: e.matmul(out, lhsT=lhsT, rhs=rhs, start=start, stop=stop), R=R, W=W,
                  inc=stop if inc is None else inc)

        def fm_group(pbank, col, wv, wb, c, cb, act, act_b, bias=True):
            kc = c["kc"]
            for k in range(kc):
                mm(PS[pbank][:, col:col + 128], wv[:, k, cb * 128:(cb + 1) * 128], act[:, k, :],
                   k == 0, (k == kc - 1) and not bias, R=[wb] + act_b, W=[PS_b[pbank]])
            if bias:
                mm(PS[pbank][:, col:col + 128], wv[:, kc, cb * 128:(cb + 1) * 128], ones[:, :],
                   False, True, R=[wb, const_b], W=[PS_b[pbank]])

        def tm_group(pbank, col, wv, wb, c, act, act_b, bias=True):
            kc, cw = c["kc"], c["cw"]
            for k in range(kc):
                mm(PS[pbank][:, col:col + cw], act[:, k, :], wv[:, k, :], k == 0, (k == kc - 1) and not bias,
                   R=[wb] + act_b, W=[PS_b[pbank]])
            if bias:
                mm(PS[pbank][:, col:col + cw], ones[:, :], wv[:, kc, :], False, True,
                   R=[wb, const_b], W=[PS_b[pbank]])

        def transpose_to(dst, dst_b, src, src_b, nblk, evac_eng):
            j0 = 0
            while j0 < nblk:
                n_ = min(4, nblk - j0)
                p = rot()
                pv = PS[p][:].bitcast(BF16)
                for j in range(n_):
                    tk.op("pe", lambda e, j=j: e.transpose(pv[:, j * 128:(j + 1) * 128],
                                                           src[:, (j0 + j) * 128:(j0 + j + 1) * 128], ident[:]),
                          R=src_b + [const_b], W=[PS_b[p]])
                if evac_eng == "act":
                    tk.op("act", lambda e: e.copy(dst[:, j0:j0 + n_, :],
                                                  pv[:, 0:n_ * 128].rearrange("p (a b) -> p a b", b=128)),
                          R=[PS_b[p]], W=dst_b)
                else:
                    tk.op("dve", lambda e: e.tensor_copy(dst[:, j0:j0 + n_, :],
                                                         pv[:, 0:n_ * 128].rearrange("p (a b) -> p a b", b=128)),
                          R=[PS_b[p]], W=dst_b)
                j0 += n_

        def layer_norm(buf, buf_b, eps):
            for j in range(2):
                tk.op("dve", lambda e, j=j: e.bn_stats(stats[:, j, :], buf[:, j * 512:(j + 1) * 512]),
                      R=[buf_b], W=[stats_b])
            tk.op("dve", lambda e: e.bn_aggr(mv[:], stats[:].rearrange("p a b -> p (a b)")), R=[stats_b], W=[mv_b])
            tk.op("pool", lambda e: e.tensor_scalar(rstd[:], mv[:, 1:2], eps, None, ALU.add), R=[mv_b], W=[rstd_b])
            tk.op("pool", lambda e: e.tensor_tensor(out=rstd[:], in0=rstd[:], in1=nhalf[:], op=ALU.pow),
                  R=[rstd_b, const_b], W=[rstd_b])
            tk.op("dve", lambda e: e.tensor_scalar(buf[:], buf[:], mv[:, 0:1], rstd[:], ALU.subtract, ALU.mult),
                  R=[buf_b, mv_b, rstd_b], W=[buf_b])
            tk.op("dve", lambda e: e.tensor_tensor(out=buf[:], in0=buf[:], in1=lnbc[:, 0, :], op=ALU.mult),
                  R=[buf_b, lnbc_b], W=[buf_b])
            tk.op("dve", lambda e: e.tensor_tensor(out=buf[:], in0=buf[:], in1=lnbc[:, 1, :], op=ALU.add),
                  R=[buf_b, lnbc_b], W=[buf_b])

        def emit_all():
            for l in range(depth):
                src_x = x_in if l == 0 else xmid_d[l - 1]
                dst_x = y_out if l == depth - 1 else xmid_d[l]
                if l > 0:
                    tk.drain("sp", d_st)
                for i in range(8):
                    tk.op("pool", lambda e, i=i: e.memset(St_f[:, i, :], 0.0), W=[St_fb[i]])
                    tk.op("pool", lambda e, i=i: e.memset(St_b[:, i, :], 0.0), W=[St_bb[i]])
                tk.dma("sp", d_x, X[:], src_x[0:128, :], W=[X_b])
                def front(t):
                    t0 = t * 128
                    tk.op("act", lambda e: e.copy(xb2[:], X[:]), R=[X_b], W=[xb2_b])
                    transpose_to(xT, [xT_b], xb2, [xb2_b], 8, "act")
                    tk.dma("sp", d_cs, cs[:, 0, :], c_cos[:, t0:t0 + 128], W=[cs_b])
                    tk.dma("sp", d_cs, cs[:, 1, :], c_sin[:, t0:t0 + 128], W=[cs_b])

                    blk = 0
                    for j in range(5):
                        wv, wb, c = next_chunk(l, t, f"qa{j}")
                        nb_ = c["cw"] // 128
                        p = rot()
                        for cb in range(nb_):
                            fm_group(p, cb * 128, wv, wb, c, cb, xT, [xT_b])
                        tk.op("act", lambda e, p=p, nb_=nb_, blk=blk: e.copy(
                            qaT[:, blk:blk + nb_, :], PS[p][:, 0:nb_ * 128].rearrange("p (a b) -> p a b", b=128)),
                            R=[PS_b[p]], W=[qaT_b])
                        blk += nb_
                        yield
                    blk = 0
                    for j in range(5):
                        wv, wb, c = next_chunk(l, t, f"ka{j}")
                        nb_ = c["cw"] // 128
                        p = rot()
                        for cb in range(nb_):
                            fm_group(p, cb * 128, wv, wb, c, cb, xT, [xT_b])
                        for cb in range(nb_):
                            g, hp = divmod(blk + cb, 3)
                            s = t % NBLK[g]
                            tk.op("dve", lambda e, p=p, cb=cb, g=g, hp=hp, s=s: e.tensor_copy(
                                Kr[g][:, hp, s, :], PS[p][:, cb * 128:(cb + 1) * 128]),
                                R=[PS_b[p]], W=[Kr_b[g][s]])
                        blk += nb_
                        yield
                    for j in range(6):
                        wv, wb, c = next_chunk(l, t, f"va{j}")
                        g, hh = divmod(j, 2)
                        s = t % NBLK[g]
                        p = rot()
                        tm_group(p, 0, wv, wb, c, xT, [xT_b])
                        tk.op("act", lambda e, p=p, g=g, hh=hh, s=s: e.copy(
                            Vr[g][:, s, 3 * hh:3 * hh + 3, 0:64], PS[p][:, 0:192].rearrange("p (a b) -> p a b", b=64)),
                            R=[PS_b[p]], W=[Vr_b[g][s]])
                        yield
                def attn_core(t):
                    jobs = []
                    for h in range(6):
                        for g in range(3):
                            nvis = min(t + 1, NBLK[g])
                            o0 = 0
                            while o0 < nvis:
                                n_ = min(4, nvis - o0)
                                jobs.append((h, g, o0, n_))
                                o0 += n_
                    first_of_head = {}
                    last_of_head = {}
                    for idx, (h, g, o0, n_) in enumerate(jobs):
                        first_of_head.setdefault(h, idx)
                        last_of_head[h] = idx
                    Uv = PS[L2][:, 0:390].rearrange("p (a b) -> p a b", b=65)

                    def emit_scores(idx):
                        h, g, o0, n_ = jobs[idx]
                        hp, hs = divmod(h, 2)
                        p = rot()
                        for j in range(n_):
                            s = (t - (o0 + j)) % NBLK[g]
                            mm(PS[p][:, j * 128:(j + 1) * 128],
                               Kr[g][hs * 64:(hs + 1) * 64, hp, s, :], qaT[hs * 64:(hs + 1) * 64, 3 * g + hp, :],
                               True, True, R=[Kr_b[g][s], qaT_b], W=[PS_b[p]])
                        return p

                    def emit_soft(idx, p):
                        h, g, o0, n_ = jobs[idx]
                        w = n_ * 128
                        q = idx % 2
                        tk.op("act", lambda e: e.activation(out=pexp2[q][:, 0:w], in_=PS[p][:, 0:w], func=AF.Exp, scale=0.125),
                              R=[PS_b[p]], W=[pexp2_b[q]])
                        tk.op("dve", lambda e: e.tensor_tensor(
                            out=PT[q][:, 0:w], in0=pexp2[q][:, 0:w],
                            in1=Et[g][:, h, o0:o0 + n_, :].rearrange("p a b -> p (a b)"), op=ALU.mult),
                            R=[pexp2_b[q], Et_b], W=[PT_b[q]])

                    def emit_pv(idx):
                        h, g, o0, n_ = jobs[idx]
                        q = idx % 2
                        for j in range(n_):
                            s = (t - (o0 + j)) % NBLK[g]
                            first = (idx == first_of_head[h]) and j == 0
                            last = (idx == last_of_head[h]) and j == n_ - 1
                            mm(Uv[:, h, :], PT[q][:, j * 128:(j + 1) * 128], Vr[g][:, s, h, :], first, last,
                               R=[PT_b[q], Vr_b[g][s]], W=[PS_b[L2]], inc=last or (j == n_ - 1))
                            if not last and j == n_ - 1:
                                pass
                    pbank = {0: emit_scores(0)}
                    if len(jobs) > 1:
                        pbank[1] = emit_scores(1)
                    for idx in range(len(jobs)):
                        emit_soft(idx, pbank.pop(idx))
                        if idx + 2 < len(jobs):
                            pbank[idx + 2] = emit_scores(idx + 2)
                        emit_pv(idx)
                        yield
                    tk.op("dve", lambda e: e.reciprocal(rL[:, 0:6], Uv[:, :, 64:65].rearrange("p a b -> p (a b)")),
                          R=[PS_b[L2]], W=[rL_b])
                    tk.op("dve", lambda e: e.tensor_tensor(out=ya[:], in0=Uv[:, :, 0:64],
                                                           in1=rL[:, 0:6].unsqueeze(2).to_broadcast([128, 6, 64]), op=ALU.mult),
                          R=[PS_b[L2], rL_b], W=[ya_b])
                    transpose_to(yaT, [yaT_b], ya[:].rearrange("p a b -> p (a b)"), [ya_b], 3, "act")

                deferred = []

                def ret_proj(t):
                    def rotary(tag, dstT, dstT_b):
                        for bank in range(2):
                            p = rot()
                            for hh in range(2):
                                wv, wb, c = next_chunk(l, t, f"{tag}{bank * 2 + hh}")
                                for cb in range(2):
                                    fm_group(p, (hh * 2 + cb) * 128, wv, wb, c, cb, xT, [xT_b])
                                yield
                            pv = PS[p][:].rearrange("p (h f t) -> p h f t", h=2, f=2)
                            q1 = pv[:, :, 0, :]
                            q2 = pv[:, :, 1, :]
                            cosb = cs[:, 0:1, :].to_broadcast([128, 2, 128])
                            sinb = cs[:, 1:2, :].to_broadcast([128, 2, 128])
                            tav = ta[:, 0:256].rearrange("p (h t) -> p h t", h=2)
                            tbv = tb[:, 0:256].rearrange("p (h t) -> p h t", h=2)
                            dv = dstT[:, bank * 4:(bank + 1) * 4, :].rearrange("p (h f) t -> p h f t", f=2)
                            tk.op("dve", lambda e: e.tensor_tensor(out=tav, in0=q1, in1=cosb, op=ALU.mult),
                                  R=[PS_b[p], cs_b], W=[ta_b])
                            tk.op("dve", lambda e: e.tensor_tensor(out=tbv, in0=q2, in1=sinb, op=ALU.mult),
                                  R=[PS_b[p], cs_b], W=[tb_b])
                            tk.op("pool", lambda e: e.tensor_tensor(out=dv[:, :, 0, :], in0=tav, in1=tbv, op=ALU.subtract),
                                  R=[ta_b, tb_b], W=[dstT_b[bank]])
                            tk.op("dve", lambda e: e.tensor_tensor(out=tav, in0=q1, in1=sinb, op=ALU.mult),
                                  R=[PS_b[p], cs_b], W=[ta_b])
                            tk.op("dve", lambda e: e.tensor_tensor(out=tbv, in0=q2, in1=cosb, op=ALU.mult),
                                  R=[PS_b[p], cs_b], W=[tb_b])
                            tk.op("pool", lambda e: e.tensor_tensor(out=dv[:, :, 1, :], in0=tav, in1=tbv, op=ALU.add),
                                  R=[ta_b, tb_b], W=[dstT_b[bank]])
                    yield from rotary("qr", qrT, qrT_b)
                    for bank in range(2):
                        tk.op("pool", lambda e, bank=bank: e.tensor_tensor(
                            out=qtT[:, bank * 4:(bank + 1) * 4, :].rearrange("p (h f) t -> p h f t", f=2),
                            in0=qrT[:, bank * 4:(bank + 1) * 4, :].rearrange("p (h f) t -> p h f t", f=2),
                            in1=GQ[:, bank * 256:(bank + 1) * 256].rearrange("p (h t) -> p h t", h=2).unsqueeze(2).to_broadcast([128, 2, 2, 128]),
                            op=ALU.mult), R=[qrT_b[bank], const_b], W=[qtT_b[bank]])
                    yield from rotary("kr", krT, krT_b)
                    while deferred:
                        deferred.pop(0)()
                    for j in range(8):
                        wv, wb, c = next_chunk(l, t, f"vr{j}")
                        p = rot()
                        tm_group(p, 0, wv, wb, c, xT, [xT_b])
                        tk.op("act", lambda e, p=p, j=j: e.copy(vr[:, j * 256:(j + 1) * 256], PS[p][:, 0:256]),
                              R=[PS_b[p]], W=[vr_b[j]])
                        yield

                def ffn_steps(t):
                    def gu(j):
                        p = rot()
                        wv, wb, c = next_chunk(l, t, f"fg{j}")
                        for cb in range(2):
                            fm_group(p, cb * 128, wv, wb, c, cb, xmT, [xmT_b], bias=False)
                        wv, wb, c = next_chunk(l, t, f"fu{j}")
                        for cb in range(2):
                            fm_group(p, (2 + cb) * 128, wv, wb, c, cb, xmT, [xmT_b], bias=False)
                        tk.op("act", lambda e: e.activation(out=th[:, 0:256], in_=PS[p][:, 0:256], func=AF.Tanh, scale=0.5),
                              R=[PS_b[p]], W=[th_b])
                        tk.op("dve", lambda e: e.scalar_tensor_tensor(
                            out=th[:, 256:512], in0=th[:, 0:256], scalar=1.0, in1=PS[p][:, 0:256], op0=ALU.add, op1=ALU.mult),
                            R=[th_b, PS_b[p]], W=[th_b])
                        q = j % 2
                        tk.op("dve", lambda e: e.scalar_tensor_tensor(
                            out=hT[q][:].rearrange("p a b -> p (a b)"), in0=th[:, 256:512], scalar=0.5, in1=PS[p][:, 256:512],
                            op0=ALU.mult, op1=ALU.mult), R=[th_b, PS_b[p]], W=[hT_b[q]])

                    def down(j):
                        q = j % 2
                        wv, wb, c = next_chunk(l, t, f"fd{j}")
                        for hb, bank in enumerate((L0, L1)):
                            for k in range(2):
                                mm(PS[bank][:], hT[q][:, k, :], wv[:, k, hb * 512:(hb + 1) * 512],
                                   j == 0 and k == 0, j == 10 and k == 1, R=[hT_b[q], wb], W=[PS_b[bank]],
                                   inc=(k == 1 and hb == 1) or (j == 10 and k == 1))
                    gu(0)
                    yield
                    for j in range(11):
                        if j + 1 < 11:
                            gu(j + 1)
                        down(j)
                        yield

                def run(gen, mode=None):
                    rot_mode[0] = mode
                    try:
                        next(gen)
                        ok = True
                    except StopIteration:
                        ok = False
                    rot_mode[0] = None
                    return ok

                def chain(*gens):
                    for g_ in gens:
                        yield from g_

                for t in range(NT):
                    t0 = t * 128
                    if l + 1 < depth:
                        emit_conv(l + 1, len(plan) if t == NT - 1 else -(-len(plan) // NT))
                    if t == 0:
                        for _ in front(0):
                            pass
                        for _ in attn_core(0):
                            pass
                    tk.op("pool", lambda e: e.tensor_scalar(M[:], X[:], alpha, 0.0, ALU.mult, ALU.add), R=[X_b], W=[M_b])
                    if t + 1 < NT:
                        tk.dma("sp", d_x, X[:], src_x[t0 + 128:t0 + 256, :], W=[X_b])
                    for _ in ret_proj(t):
                        pass
                    for bank in range(2):
                        p = rot()
                        pv = PS[p][:].bitcast(BF16)
                        for j in range(4):
                            blk_ = bank * 4 + j
                            tk.op("pe", lambda e, j=j, blk_=blk_: e.transpose(pv[:, j * 128:(j + 1) * 128], krT[:, blk_, :], ident[:]),
                                  R=[krT_b[bank], const_b], W=[PS_b[p]])
                        tk.op("dve", lambda e, bank=bank, pv=pv: e.tensor_tensor(
                            out=ktm[:, bank * 512:(bank + 1) * 512].rearrange("p (h d) -> p h d", h=2),
                            in0=pv[:, 0:512].rearrange("p (h d) -> p h d", h=2),
                            in1=GK[:, bank * 2:bank * 2 + 2].unsqueeze(2).to_broadcast([128, 2, 256]), op=ALU.mult),
                            R=[PS_b[p], const_b], W=[ktm_b[bank]])
                    p = rot()
                    for h in range(4):
                        for f in range(2):
                            mm(PS[p][:, h * 128:(h + 1) * 128], krT[:, h * 2 + f, :], qrT[:, h * 2 + f, :], f == 0, f == 1,
                               R=[krT_b[h // 2], qrT_b[h // 2]], W=[PS_b[p]])
                    tk.op("dve", lambda e, p=p: e.tensor_tensor(out=scT[:], in0=PS[p][:], in1=DM[:], op=ALU.mult),
                          R=[PS_b[p], const_b], W=[scT_b])
                    def ret_stage1(h):
                        pg = rot()
                        for j in range(2):
                            wv, wb, c = next_chunk(l, t, f"gr{h * 2 + j}")
                            tm_group(pg, j * 256, wv, wb, c, xT, [xT_b])
                        tk.op("act", lambda e, pg=pg: e.activation(out=th[:], in_=PS[pg][:], func=AF.Tanh, scale=0.5),
                              R=[PS_b[pg]], W=[th_b])
                        sq = h % 2
                        tk.op("dve", lambda e, pg=pg, sq=sq: e.scalar_tensor_tensor(
                            out=sg[sq][:], in0=th[:], scalar=1.0, in1=PS[pg][:], op0=ALU.add, op1=ALU.mult),
                            R=[th_b, PS_b[pg]], W=[sg_b[sq]])
                        po = rot()
                        mm(PS[po][:], scT[:, h * 128:(h + 1) * 128], vr[:, h * 512:(h + 1) * 512], True, False,
                           R=[scT_b, vr_b[2 * h], vr_b[2 * h + 1]], W=[PS_b[po]])
                        for f in range(2):
                            mm(PS[po][:], qtT[:, h * 2 + f, :], St_b[:, h * 2 + f, :], False, f == 1,
                               R=[qtT_b[h // 2], St_bb[h * 2 + f]], W=[PS_b[po]])
                        tk.op("dve", lambda e, po=po: e.bn_stats(stats[:, 0, :], PS[po][:]), R=[PS_b[po]], W=[stats_b])
                        tk.op("dve", lambda e: e.bn_aggr(mv[:], stats[:, 0, :]), R=[stats_b], W=[mv_b])
                        tk.op("pool", lambda e: e.tensor_scalar(rstd[:], mv[:, 1:2], GN_EPS, None, ALU.add), R=[mv_b], W=[rstd_b])
                        tk.op("pool", lambda e: e.tensor_tensor(out=rstd[:], in0=rstd[:], in1=nhalf[:], op=ALU.pow),
                              R=[rstd_b, const_b], W=[rstd_b])
                        tk.op("dve", lambda e, po=po: e.tensor_scalar(gn[:], PS[po][:], mv[:, 0:1], rstd[:], ALU.subtract, ALU.mult),
                              R=[PS_b[po], mv_b, rstd_b], W=[gn_b])
                        tk.op("dve", lambda e, sq=sq: e.scalar_tensor_tensor(
                            out=yb[sq][:], in0=gn[:], scalar=0.5, in1=sg[sq][:], op0=ALU.mult, op1=ALU.mult),
                            R=[gn_b, sg_b[sq]], W=[yb_b[sq]])

                    def ret_stage2(h):
                        sq = h % 2
                        p = rot()
                        pv = PS[p][:].bitcast(BF16)
                        for j in range(4):
                            tk.op("pe", lambda e, j=j, sq=sq, pv=pv: e.transpose(pv[:, j * 128:(j + 1) * 128],
                                                                                 yb[sq][:, j * 128:(j + 1) * 128], ident[:]),
                                  R=[yb_b[sq], const_b], W=[PS_b[p]])
                        tk.op("act", lambda e, h=h, pv=pv: e.copy(ybT[:, 4 * h:4 * h + 4, :],
                                                                  pv[:, 0:512].rearrange("p (a b) -> p a b", b=128)),
                              R=[PS_b[p]], W=[ybT_b[h]])
                    ret_stage1(0)
                    for h in range(4):
                        if h + 1 < 4:
                            ret_stage1(h + 1)
                        ret_stage2(h)
                    for h in range(4):
                        for f in range(2):
                            i8 = h * 2 + f
                            p = rot()
                            mm(PS[p][:], ktm[:, i8 * 128:(i8 + 1) * 128], vr[:, h * 512:(h + 1) * 512], True, True,
                               R=[ktm_b[h // 2], vr_b[2 * h], vr_b[2 * h + 1]], W=[PS_b[p]])
                            tk.op("dve", lambda e, p=p, i8=i8, h=h: e.scalar_tensor_tensor(
                                out=St_f[:, i8, :], in0=St_f[:, i8, :], scalar=chunk_dec[h], in1=PS[p][:],
                                op0=ALU.mult, op1=ALU.add), R=[St_fb[i8], PS_b[p]], W=[St_fb[i8]])
                            tk.op("act", lambda e, i8=i8: e.copy(St_b[:, i8, :], St_f[:, i8, :]),
                                  R=[St_fb[i8]], W=[St_bb[i8]])

                    for half in range(2):
                        pA = rot()
                        for j in range(2):
                            wv, wb, c = next_chunk(l, t, f"ga{half}{j}")
                            for cb in range(2):
                                fm_group(pA, (j * 2 + cb) * 128, wv, wb, c, cb, xT, [xT_b])
                        tk.op("act", lambda e, pA=pA: e.activation(out=sgA[:], in_=PS[pA][:], func=AF.Tanh, scale=0.5),
                              R=[PS_b[pA]], W=[sgA_b])
                        tk.op("pool", lambda e: e.tensor_scalar(sgA[:], sgA[:], 0.5, 0.5, ALU.mult, ALU.add),
                              R=[sgA_b], W=[sgA_b])
                        pB = rot()
                        for j in range(2):
                            wv, wb, c = next_chunk(l, t, f"gb{half}{j}")
                            for cb in range(2):
                                fm_group(pB, (j * 2 + cb) * 128, wv, wb, c, cb, xT, [xT_b])
                        tk.op("act", lambda e, pB=pB: e.activation(out=sgB[:], in_=PS[pB][:], func=AF.Tanh, scale=0.5),
                              R=[PS_b[pB]], W=[sgB_b])
                        tk.op("pool", lambda e: e.tensor_scalar(sgB[:], sgB[:], 0.5, 0.5, ALU.mult, ALU.add),
                              R=[sgB_b], W=[sgB_b])
                        wv, wb, c = next_chunk(l, t, f"ap{half}")
                        pa = rot()
                        for cb in range(4):
                            fm_group(pa, cb * 128, wv, wb, c, cb, yaT, [yaT_b], bias=False)
                        tk.op("dve", lambda e, pa=pa: e.tensor_tensor(out=ta[:], in0=PS[pa][:], in1=sgA[:], op=ALU.mult),
                              R=[PS_b[pa], sgA_b], W=[ta_b])
                        pb_ = rot()
                        for j in range(4):
                            wv, wb, c = next_chunk(l, t, f"rp{half}{j}")
                            fm_group(pb_, j * 128, wv, wb, c, 0, ybT, ybT_b, bias=False)
                        tk.op("dve", lambda e, pb_=pb_: e.tensor_tensor(out=tb[:], in0=PS[pb_][:], in1=sgB[:], op=ALU.mult),
                              R=[PS_b[pb_], sgB_b], W=[tb_b])
                        tk.op("pool", lambda e, half=half: e.tensor_tensor(
                            out=mT[:, half * 4:(half + 1) * 4, :].rearrange("p a b -> p (a b)"), in0=ta[:], in1=tb[:], op=ALU.add),
                            R=[ta_b, tb_b], W=[mT_b[half]])

                    tk.dma("sp", d_ln, lnbc[:, 0, :], ln_in["ln1_g"][l:l + 1, :].to_broadcast([128, D]), W=[lnbc_b])
                    tk.dma("sp", d_ln, lnbc[:, 1, :], ln_in["ln1_b"][l:l + 1, :].to_broadcast([128, D]), W=[lnbc_b])
                    for j in range(4):
                        wv, wb, c = next_chunk(l, t, f"wo{j}")
                        bank = L0 if j < 2 else L1
                        tm_group(bank, (j % 2) * 256, wv, wb, c, mT, mT_b, bias=False)
                    for j, bank in enumerate((L0, L1)):
                        tk.op("dve", lambda e, j=j, bank=bank: e.tensor_tensor(
                            out=M[:, j * 512:(j + 1) * 512], in0=M[:, j * 512:(j + 1) * 512], in1=PS[bank][:], op=ALU.add),
                            R=[M_b, PS_b[bank]], W=[M_b])
                    layer_norm(M, M_b, LN_EPS)
                    gA = chain(front(t + 1), attn_core(t + 1)) if t + 1 < NT else iter(())
                    a_live = True
                    for _ in range(16):
                        a_live = a_live and run(gA, "A")
                    tk.op("act", lambda e: e.copy(xb[:], M[:]), R=[M_b], W=[xb_b])
                    transpose_to(xmT, [xmT_b], xb, [xb_b], 8, "dve")

                    tk.dma("sp", d_ln, lnbc[:, 0, :], ln_in["ln2_g"][l:l + 1, :].to_broadcast([128, D]), R=[M_b], W=[lnbc_b])
                    tk.dma("sp", d_ln, lnbc[:, 1, :], ln_in["ln2_b"][l:l + 1, :].to_broadcast([128, D]), R=[M_b], W=[lnbc_b])
                    gE = ffn_steps(t)
                    e_live = True
                    while e_live:
                        e_live = run(gE, "B")
                        for _ in range(4):
                            a_live = a_live and run(gA, "A")
                    while a_live:
                        a_live = run(gA, "A")
                    for j, bank in enumerate((L0, L1)):
                        tk.op("dve", lambda e, j=j, bank=bank: e.scalar_tensor_tensor(
                            out=O[:, j * 512:(j + 1) * 512], in0=M[:, j * 512:(j + 1) * 512], scalar=alpha, in1=PS[bank][:],
                            op0=ALU.mult, op1=ALU.add), R=[M_b, PS_b[bank]], W=[O_b])
                    def _ln2_store(t0=t0):
                        layer_norm(O, O_b, LN_EPS)
                        tk.dma("sp", d_st, dst_x[t0:t0 + 128, :], O[:], R=[O_b])
                    deferred.append(_ln2_store)
                    if t + 1 >= NT:
                        while deferred:
                            deferred.pop(0)()

        tk.dry = True
        emit_all()
        tk.dry = False
        conv_todo.update({l: ([] if l == 0 else list(range(len(plan)))) for l in range(depth)})
        rot_state[0] = 0
        rot_sub.update({"A": 0, "B": 0})
        for _ in range(NSLOT - 1):
            issue_next()
        emit_all()
        tk.drain("sp", d_st)
    return nc


_PROG_CACHE = {}


def _run(inputs, S, depth, n_cores):
    alpha = float((2 * depth) ** 0.25)
    consts, chunk_dec = make_consts(S)
    key = (S, depth)
    if key not in _PROG_CACHE:
        _PROG_CACHE[key] = build_program(S, depth, alpha, chunk_dec)
    nc = _PROG_CACHE[key]
    x = np.ascontiguousarray(np.asarray(inputs["x"], dtype=np.float32))
    shared = {k: np.ascontiguousarray(np.asarray(v, dtype=np.float32)) for k, v in inputs.items() if k != "x"}
    shared.update(consts)
    in_maps = []
    for c in range(n_cores):
        m = dict(shared)
        m["x"] = x[c]
        in_maps.append(m)
    res = run_bass_kernel_spmd(nc, in_maps, core_ids=list(range(n_cores)))
    return np.stack([np.asarray(r["y"], dtype=np.float32) for r in res.results], axis=0)


def kernel(**inputs):
    x = np.asarray(inputs["x"])
    B, S, _ = x.shape
    depth = int(np.asarray(inputs["w_in"]).shape[0])
    return _run(inputs, S, depth, B)
```

```python
import contextlib
import numpy as np
import concourse.bass as bass
import concourse.mybir as mybir
from concourse.bass_utils import run_bass_kernel_spmd

F32 = mybir.dt.float32
BF16 = mybir.dt.bfloat16
AF = mybir.ActivationFunctionType
ALU = mybir.AluOpType

D = 1024
HD = 64
GROUPS = ((128, 1), (512, 4), (2048, 16))
NBLK = (2, 5, 17)
RH = 4
RDK = 256
RDV = 512
DFF = 2816
IN_COLS = 11648
NUM_BUCKETS = 32
MAX_DISTANCE = 2048
LF = 2304
COL = dict(qa=0, ka=1152, va=2304, qr=3456, kr=4480, vr=5504, gr=7552, ga=9600, gb=10624)
SLOT_ELEMS = 2304
NSLOT = 7
LN_EPS = 1e-5
GN_EPS = 1e-5


def _t5_bucket(dist):
    max_exact = NUM_BUCKETS // 2
    large = max_exact + (np.log(np.maximum(dist, max_exact) / max_exact)
                         / np.log(MAX_DISTANCE / max_exact)
                         * (NUM_BUCKETS - max_exact)).astype(np.int32)
    large = np.minimum(large, NUM_BUCKETS - 1)
    return np.where(dist < max_exact, dist, large).astype(np.int32)


def make_consts(S):
    half = RDK // 2
    pos = np.arange(S, dtype=np.float32)
    inv_freq = (10000.0 ** (-np.arange(half, dtype=np.float32) / half)).astype(np.float32)
    ang = (pos[None, :] * inv_freq[:, None]).astype(np.float32)
    cos_t = np.cos(ang).astype(np.float32)
    sin_t = np.sin(ang).astype(np.float32)
    log_g = np.log(1.0 - 2.0 ** (-5.0 - np.arange(RH, dtype=np.float64)))
    n = np.arange(128, dtype=np.float64)
    gq = np.exp(log_g[:, None] * (n[None, :] + 1.0))
    GQ = np.broadcast_to(gq.reshape(1, 4 * 128), (128, 512)).astype(np.float32).copy()
    diff = n[None, :] - n[:, None]
    DM = np.zeros((128, 4, 128), np.float64)
    for h in range(RH):
        DM[:, h, :] = np.where(diff >= 0, np.exp(log_g[h] * np.maximum(diff, 0.0)), 0.0) / 16.0
    DM = DM.reshape(128, 512).astype(np.float32)
    GK = (np.exp(log_g[None, :] * (127.0 - n[:, None])) / 16.0).astype(np.float32)
    chunk_dec = [float(np.exp(log_g[h] * 128.0)) for h in range(RH)]
    ident = np.eye(128, dtype=np.float32)
    J = ident[::-1].copy()
    delta = np.arange(LF) - 127
    bk = _t5_bucket(np.clip(delta, 0, MAX_DISTANCE))
    OH = np.zeros((NUM_BUCKETS, LF), np.float32)
    OH[bk, np.arange(LF)] = 1.0
    VALID = np.zeros((18, LF), np.float32)
    for g, (win, dil) in enumerate(GROUPS):
        v = ((delta >= 0) & (delta % dil == 0) & (delta <= win)).astype(np.float32)
        VALID[6 * g:6 * g + 6, :] = v[None, :]
    return dict(cos_t=cos_t, sin_t=sin_t, GQ=GQ, DM=DM, GK=GK, ident=ident, J=J, OH=OH,
                VALID=VALID, zeros=np.zeros((128, 256), np.float32)), chunk_dec


class Buf:
    __slots__ = ("w", "rs", "name")

    def __init__(self, name=""):
        self.w = None
        self.rs = {}
        self.name = name


class _Src:
    def __init__(self, name, sem, step):
        self.name, self.sem, self.step, self.cnt = name, sem, step, 0


class Tracker:
    def __init__(self, nc, es):
        self.nc = nc
        self.es = es
        self.src = {}
        self.handles = {"pe": nc.tensor, "act": nc.scalar, "dve": nc.vector,
                        "pool": nc.gpsimd, "sp": nc.sync}
        for e in ("pe", "act", "dve", "pool"):
            self.src[e] = _Src(e, es.enter_context(nc.semaphore("s_" + e)), 1)
        self.seen = {e: {} for e in self.handles}
        self.dry = False

    def dsem(self, name):
        s = _Src(name, self.es.enter_context(self.nc.semaphore(name)), 16)
        self.src[name] = s
        return s

    def _waits(self, eng, R, W):
        need = {}

        def req(ev, same_ok):
            if ev is None:
                return
            s, c = ev
            if s == eng and not same_ok:
                return
            if c > need.get(s, 0):
                need[s] = c
        for b in R:
            req(b.w, True)
        for b in W:
            req(b.w, False)
            for s, c in b.rs.items():
                req((s, c), False)
        h = self.handles[eng]
        seen = self.seen[eng]
        for s, c in need.items():
            if seen.get(s, 0) >= c:
                continue
            src = self.src[s]
            assert c <= src.cnt, (eng, s, c, src.cnt)
            h.wait_ge(src.sem, c)
            seen[s] = c

    def op(self, eng, fn, R=(), W=(), inc=True):
        if self.dry:
            return None
        self._waits(eng, R, W)
        src = self.src[eng]
        ins = fn(self.handles[eng])
        ticket = src.cnt + 1
        if inc:
            ins.then_inc(src.sem, 1)
            src.cnt += 1
        else:
            assert eng == "pe"
        for b in R:
            b.rs[eng] = ticket
        for b in W:
            b.w = (eng, ticket)
            b.rs = {}
        return ins

    def dma(self, eng, ds, out, in_, R=(), W=()):
        if self.dry:
            return
        self._waits(eng, R, W)
        ins = self.handles[eng].dma_start(out=out, in_=in_)
        ins.then_inc(ds.sem, 16)
        ds.cnt += 16
        for b in R:
            b.rs[ds.name] = ds.cnt
        for b in W:
            b.w = (ds.name, ds.cnt)
            b.rs = {}

    def drain(self, eng, ds):
        if self.dry:
            return
        if self.seen[eng].get(ds.name, 0) < ds.cnt:
            self.handles[eng].wait_ge(ds.sem, ds.cnt)
            self.seen[eng][ds.name] = ds.cnt


def chunk_plan():
    P = []

    def win(tag, col0, width, cw):
        c = 0
        j = 0
        while c < width:
            w = min(cw, width - c)
            P.append(dict(name=f"{tag}{j}", src="w_in", kc=8, cw=w, row0=0, col0=col0 + c, bias=True))
            c += w
            j += 1
    win("qa", COL["qa"], 1152, 256)
    win("ka", COL["ka"], 1152, 256)
    win("va", COL["va"], 1152, 192)
    win("qr", COL["qr"], 1024, 256)
    win("kr", COL["kr"], 1024, 256)
    win("vr", COL["vr"], 2048, 256)
    win("gr", COL["gr"], 2048, 256)
    for half in range(2):
        for j in range(2):
            P.append(dict(name=f"ga{half}{j}", src="w_in", kc=8, cw=256, row0=0,
                          col0=COL["ga"] + half * 512 + j * 256, bias=True))
        for j in range(2):
            P.append(dict(name=f"gb{half}{j}", src="w_in", kc=8, cw=256, row0=0,
                          col0=COL["gb"] + half * 512 + j * 256, bias=True))
        P.append(dict(name=f"ap{half}", src="w_attn_proj", kc=3, cw=512, row0=0, col0=half * 512, bias=False))
        for j in range(4):
            P.append(dict(name=f"rp{half}{j}", src="w_ret_proj", kc=16, cw=128, row0=0,
                          col0=half * 512 + j * 128, bias=False))
    for j in range(4):
        P.append(dict(name=f"wo{j}", src="w_out", kc=8, cw=256, row0=0, col0=j * 256, bias=False))
    for j in range(11):
        P.append(dict(name=f"fg{j}", src="w_ffn_gate", kc=8, cw=256, row0=0, col0=j * 256, bias=False))
        P.append(dict(name=f"fu{j}", src="w_ffn_up", kc=8, cw=256, row0=0, col0=j * 256, bias=False))
        P.append(dict(name=f"fd{j}", src="w_ffn_down", kc=2, cw=1024, row0=j * 256, col0=0, bias=False))
    for c in P:
        c["kk"] = c["kc"] + (1 if c["bias"] else 0)
        assert c["kk"] * c["cw"] <= SLOT_ELEMS, c
    return P


def build_program(S, depth, alpha, chunk_dec):
    NT = S // 128
    nc = bass.Bass("TRN2", target_bir_lowering=False)
    es = contextlib.ExitStack()

    def din(name, shape, dt=F32):
        return nc.dram_tensor(name, list(shape), dt, kind="ExternalInput").ap()

    x_in = din("x", [S, D])
    rel_bias = din("rel_bias", [NUM_BUCKETS, 18])
    Wd = dict(w_in=din("w_in", [depth, D, IN_COLS]), w_attn_proj=din("w_attn_proj", [depth, 384, D]),
              w_ret_proj=din("w_ret_proj", [depth, 2048, D]), w_out=din("w_out", [depth, D, D]),
              w_ffn_gate=din("w_ffn_gate", [depth, D, DFF]), w_ffn_up=din("w_ffn_up", [depth, D, DFF]),
              w_ffn_down=din("w_ffn_down", [depth, DFF, D]))
    b_in = din("b_in", [depth, IN_COLS])
    ln_in = dict(ln1_g=din("ln1_g", [depth, D]), ln1_b=din("ln1_b", [depth, D]),
                 ln2_g=din("ln2_g", [depth, D]), ln2_b=din("ln2_b", [depth, D]))
    c_cos = din("cos_t", [128, S])
    c_sin = din("sin_t", [128, S])
    c_GQ = din("GQ", [128, 512])
    c_DM = din("DM", [128, 512])
    c_GK = din("GK", [128, 4])
    c_ident = din("ident", [128, 128])
    c_J = din("J", [128, 128])
    c_OH = din("OH", [NUM_BUCKETS, LF])
    c_VALID = din("VALID", [18, LF])
    c_zeros = din("zeros", [128, 256])
    y_out = nc.dram_tensor("y", [S, D], F32, kind="ExternalOutput").ap()
    xmid_d = [nc.dram_tensor(f"xl{l}", [S, D], F32, kind="Internal").ap() for l in range(depth - 1)]
    Fd = nc.dram_tensor("Fd", [18, LF], BF16, kind="Internal")
    plan = chunk_plan()
    wsc = [[nc.dram_tensor(f"ws{l}_{c['name']}", [128, c["kk"] * c["cw"]], BF16, kind="Internal").ap()
            for c in plan] for l in range(depth)]

    with es:
        tk = Tracker(nc, es)

        def sb(name, shape, dt):
            return es.enter_context(nc.sbuf_tensor("sb_" + name, list(shape), dt))

        slots = [sb(f"slot{i}", [128, SLOT_ELEMS], BF16) for i in range(NSLOT)]
        slot_b = [Buf(f"slot{i}") for i in range(NSLOT)]
        slot_sem = [tk.dsem(f"d_slot{i}") for i in range(NSLOT)]
        Et = [sb(f"Et{g}", [128, 6, NBLK[g], 128], BF16) for g in range(3)]
        Et_b = Buf("Et")
        Kr = [sb(f"Kr{g}", [128, 3, NBLK[g], 128], BF16) for g in range(3)]
        Vr = [sb(f"Vr{g}", [128, NBLK[g], 6, 65], BF16) for g in range(3)]
        Kr_b = [[Buf(f"Kr{g}_{s}") for s in range(NBLK[g])] for g in range(3)]
        Vr_b = [[Buf(f"Vr{g}_{s}") for s in range(NBLK[g])] for g in range(3)]
        St_f = sb("St_f", [128, 8, 512], F32)
        St_b = sb("St_b", [128, 8, 512], BF16)
        St_fb = [Buf(f"Stf{i}") for i in range(8)]
        St_bb = [Buf(f"Stb{i}") for i in range(8)]
        X = sb("X", [128, D], F32); X_b = Buf("X")
        M = sb("M", [128, D], F32); M_b = Buf("M")
        xb = sb("xb", [128, D], BF16); xb_b = Buf("xb")
        xb2 = sb("xb2", [128, D], BF16); xb2_b = Buf("xb2")
        xT = sb("xT", [128, 8, 128], BF16); xT_b = Buf("xT")
        xmT = sb("xmT", [128, 8, 128], BF16); xmT_b = Buf("xmT")
        qaT = sb("qaT", [128, 9, 2, 128], BF16); qaT_b = Buf("qaT")
        ta = sb("ta", [128, 512], F32); ta_b = Buf("ta")
        tb = sb("tb", [128, 512], F32); tb_b = Buf("tb")
        qrT = sb("qrT", [128, 8, 128], BF16); qrT_b = [Buf("qrT0"), Buf("qrT1")]
        krT = sb("krT", [128, 8, 128], BF16); krT_b = [Buf("krT0"), Buf("krT1")]
        qtT = sb("qtT", [128, 8, 128], BF16); qtT_b = [Buf("qtT0"), Buf("qtT1")]
        ktm = sb("ktm", [128, 1024], BF16); ktm_b = [Buf("ktm0"), Buf("ktm1")]
        vr = sb("vr", [128, 2048], BF16); vr_b = [Buf(f"vr{i}") for i in range(8)]
        sg = [sb(f"sg{i}", [128, 512], F32) for i in range(2)]; sg_b = [Buf("sg0"), Buf("sg1")]
        th = sb("th", [128, 512], F32); th_b = Buf("th")
        pexp2 = [sb(f"pexp{i}", [128, 512], BF16) for i in range(2)]; pexp2_b = [Buf("pexp0"), Buf("pexp1")]
        pexp = pexp2[0]; pexp_b = pexp2_b[0]
        PT = [sb(f"PT{i}", [128, 512], BF16) for i in range(2)]; PT_b = [Buf("PT0"), Buf("PT1")]
        scT = sb("scT", [128, 512], BF16); scT_b = Buf("scT")
        gn = sb("gn", [128, 512], F32); gn_b = Buf("gn")
        yb = [sb(f"yb{i}", [128, 512], BF16) for i in range(2)]; yb_b = [Buf("yb0"), Buf("yb1")]
        ybT = sb("ybT", [128, 16, 128], BF16); ybT_b = [Buf(f"ybT{i}") for i in range(4)]
        ya = sb("ya", [128, 6, 64], BF16); ya_b = Buf("ya")
        yaT = sb("yaT", [128, 3, 128], BF16); yaT_b = Buf("yaT")
        rL = sb("rL", [128, 8], F32); rL_b = Buf("rL")
        sgA = sb("sgA", [128, 512], F32); sgA_b = Buf("sgA")
        sgB = sb("sgB", [128, 512], F32); sgB_b = Buf("sgB")
        mT = sb("mT", [128, 8, 128], BF16); mT_b = [Buf("mT0"), Buf("mT1")]
        hT = [sb(f"hT{i}", [128, 2, 128], BF16) for i in range(2)]; hT_b = [Buf("hT0"), Buf("hT1")]
        cs = sb("cs", [128, 2, 128], F32); cs_b = Buf("cs")
        lnbc = sb("lnbc", [128, 2, D], F32); lnbc_b = Buf("lnbc")
        ident = sb("ident", [128, 128], BF16)
        Jm = sb("Jm", [128, 128], BF16)
        GQ = sb("GQ", [128, 512], F32)
        DM = sb("DM", [128, 512], F32)
        GK = sb("GK", [128, 4], F32)
        ones = sb("ones", [128, 128], BF16)
        nhalf = sb("nhalf", [128, 1], F32)
        stats = sb("stats", [128, 2, 6], F32); stats_b = Buf("stats")
        mv = sb("mv", [128, 2], F32); mv_b = Buf("mv")
        rstd = sb("rstd", [128, 1], F32); rstd_b = Buf("rstd")
        nb = sb("nb", [128, 1], F32); nb_b = Buf("nb")
        const_b = Buf("const")
        PS = [es.enter_context(nc.psum_tensor(f"ps{i}", [128, 512], F32)) for i in range(8)]
        PS_b = [Buf(f"ps{i}") for i in range(8)]
        rot_state = [0]
        rot_mode = [None]
        rot_sub = {"A": 0, "B": 0}

        def rot():
            m = rot_mode[0]
            if m is None:
                i = rot_state[0]
                rot_state[0] = (i + 1) % 5
                return i
            if m == "A":
                i = rot_sub[m]
                rot_sub[m] = (i + 1) % 3
                return i
            i = rot_sub[m]
            rot_sub[m] = (i + 1) % 2
            return 3 + i
        L0, L1, L2 = 5, 6, 7

        d_misc = tk.dsem("d_misc")
        d_x = tk.dsem("d_x")
        d_cs = tk.dsem("d_cs")
        d_ln = tk.dsem("d_ln")
        d_st = tk.dsem("d_st")
        d_F = tk.dsem("d_F")
        d_H = tk.dsem("d_H")
        NCG = 6
        d_cv = [[tk.dsem(f"d_cv{l}_{k}") for k in range(NCG)] for l in range(depth)]

        rb = sb("rb", [NUM_BUCKETS, 18], F32)
        d_misc2 = tk.dsem("d_misc2")
        tk.dma("pool", d_misc2, ident[:], c_ident, W=[const_b])
        tk.dma("pool", d_misc2, Jm[:], c_J, W=[const_b])
        tk.dma("sp", d_misc, GQ[:], c_GQ, W=[const_b])
        tk.dma("sp", d_misc, DM[:], c_DM, W=[const_b])
        tk.dma("sp", d_misc, GK[:], c_GK, W=[const_b])
        tk.dma("sp", d_misc, rb[:], rel_bias, W=[const_b])
        for e_ in ("pe", "act", "dve", "pool", "sp"):
            tk.drain(e_, d_misc)
            tk.drain(e_, d_misc2)
        tk.op("pool", lambda e: e.memset(qaT[:], 0.0), W=[qaT_b])
        tk.op("pool", lambda e: e.memset(ones[:], 1.0), W=[const_b])
        tk.op("pool", lambda e: e.memset(nhalf[:], -0.5), W=[const_b])
        for g in range(3):
            tk.op("pool", lambda e, g=g: e.memset(Vr[g][:], 1.0), W=Vr_b[g])

        def cgroup(ci):
            return min(NCG - 1, ci * NCG // len(plan))
        conv_todo = {l: list(range(len(plan))) for l in range(depth)}

        def emit_conv(l, n):
            for _ in range(n):
                if not conv_todo[l]:
                    return
                ci = conv_todo[l].pop(0)
                c = plan[ci]
                ds = d_cv[l][cgroup(ci)]
                kc, cw = c["kc"], c["cw"]
                src = Wd[c["src"]][l, c["row0"]:c["row0"] + kc * 128, c["col0"]:c["col0"] + cw]
                src = src.rearrange("(k p) c -> p k c", p=128)
                dst = wsc[l][ci][:, 0:kc * cw].rearrange("p (k c) -> p k c", k=kc)
                if tk.dry:
                    continue
                ins = nc.gpsimd.dma_start(out=dst, in_=src)
                ins.then_inc(ds.sem, 16)
                ds.cnt += 16
                if c["bias"]:
                    bsrc = b_in[l:l + 1, c["col0"]:c["col0"] + cw]
                    ins = nc.gpsimd.dma_start(out=wsc[l][ci][0:1, kc * cw:(kc + 1) * cw], in_=bsrc)
                    ins.then_inc(ds.sem, 16)
                    ds.cnt += 16
                    ins = nc.gpsimd.dma_start(out=wsc[l][ci][1:128, kc * cw:(kc + 1) * cw], in_=c_zeros[0:127, 0:cw])
                    ins.then_inc(ds.sem, 16)
                    ds.cnt += 16
        emit_conv(0, len(plan))

        d_oh = tk.dsem("d_oh")
        d_vl = tk.dsem("d_vl")
        Fd_b = Buf("Fd")
        nchk = (LF + 511) // 512
        for j in range(nchk):
            c0 = j * 512
            w = min(512, LF - c0)
            tk.dma("sp", d_oh, ta[0:NUM_BUCKETS, 0:w], c_OH[:, c0:c0 + w], W=[ta_b])
            tk.dma("sp", d_vl, tb[0:18, 0:w], c_VALID[:, c0:c0 + w], W=[tb_b])
            p = rot()
            tk.op("pe", lambda e: e.matmul(PS[p][0:18, 0:w], lhsT=rb[:, :], rhs=ta[0:NUM_BUCKETS, 0:w],
                                           start=True, stop=True), R=[ta_b, const_b], W=[PS_b[p]])
            tk.op("act", lambda e: e.activation(out=gn[0:18, 0:w], in_=PS[p][0:18, 0:w], func=AF.Exp),
                  R=[PS_b[p]], W=[gn_b])
            tk.op("dve", lambda e: e.tensor_tensor(out=pexp[0:18, 0:w], in0=gn[0:18, 0:w], in1=tb[0:18, 0:w], op=ALU.mult),
                  R=[gn_b, tb_b], W=[pexp_b])
            tk.dma("sp", d_F, Fd.ap()[:, c0:c0 + w], pexp[0:18, 0:w], R=[pexp_b], W=[Fd_b])
        Hs = St_b[:].rearrange("p a (b c) -> p (a b) c", c=128)
        for g in range(3):
            no = NBLK[g]
            for h in range(6):
                hh = 6 * g + h
                srcap = bass.AP(Fd, hh * LF, [[1, 128], [128, no], [1, 128]])
                tk.dma("sp", d_H, Hs[:, 0:no, :], srcap, R=[Fd_b], W=St_bb)
                o0 = 0
                while o0 < no:
                    n_ = min(4, no - o0)
                    p = rot()
                    tk.op("pe", lambda e: e.matmul(PS[p][:, 0:n_ * 128], lhsT=Jm[:], rhs=Hs[:, o0:o0 + n_, :],
                                                   start=True, stop=True), R=St_bb + [const_b], W=[PS_b[p]])
                    tk.op("dve", lambda e: e.tensor_copy(
                        Et[g][:, h, o0:o0 + n_, :], PS[p][:, 0:n_ * 128].rearrange("p (a b) -> p a b", b=128)),
                        R=[PS_b[p]], W=[Et_b])
                    o0 += n_

        stream = []
        name2ci = {c["name"]: i for i, c in enumerate(plan)}
        issue_ptr = [0]
        use_ptr = [0]
        cv_drained = set()

        def issue_next():
            k = issue_ptr[0]
            if k >= len(stream):
                return
            l, t, ci = stream[k]
            c = plan[ci]
            s = k % NSLOT
            key = (l, cgroup(ci))
            if key not in cv_drained:
                tk.drain("sp", d_cv[l][cgroup(ci)])
                cv_drained.add(key)
            n = c["kk"] * c["cw"]
            tk.dma("sp", slot_sem[s], slots[s][:, 0:n], wsc[l][ci][:, 0:n], W=[slot_b[s]])
            issue_ptr[0] += 1

        def next_chunk(l, t, name):
            if tk.dry:
                k = len(stream)
                stream.append((l, t, name2ci[name]))
            else:
                k = use_ptr[0]
                assert stream[k] == (l, t, name2ci[name]), (stream[k], l, t, name)
                use_ptr[0] += 1
                while issue_ptr[0] < min(len(stream), k + NSLOT):
                    issue_next()
            s = k % NSLOT
            c = plan[stream[k][2]]
            view = slots[s][:, 0:c["kk"] * c["cw"]].rearrange("p (k c) -> p k c", k=c["kk"])
            return view, slot_b[s], c

        def mm(out, lhsT, rhs, start, stop, R, W, inc=None):
            tk.op("pe", lambda e: e.matmul(out, lhsT=lhsT, rhs=rhs, start=start, stop=stop), R=R, W=W,
                  inc=stop if inc is None else inc)

        def fm_group(pbank, col, wv, wb, c, cb, act, act_b, bias=True):
            kc = c["kc"]
            for k in range(kc):
                mm(PS[pbank][:, col:col + 128], wv[:, k, cb * 128:(cb + 1) * 128], act[:, k, :],
                   k == 0, (k == kc - 1) and not bias, R=[wb] + act_b, W=[PS_b[pbank]])
            if bias:
                mm(PS[pbank][:, col:col + 128], wv[:, kc, cb * 128:(cb + 1) * 128], ones[:, :],
                   False, True, R=[wb, const_b], W=[PS_b[pbank]])

        def tm_group(pbank, col, wv, wb, c, act, act_b, bias=True):
            kc, cw = c["kc"], c["cw"]
            for k in range(kc):
                mm(PS[pbank][:, col:col + cw], act[:, k, :], wv[:, k, :], k == 0, (k == kc - 1) and not bias,
                   R=[wb] + act_b, W=[PS_b[pbank]])
            if bias:
                mm(PS[pbank][:, col:col + cw], ones[:, :], wv[:, kc, :], False, True,
                   R=[wb, const_b], W=[PS_b[pbank]])

        def transpose_to(dst, dst_b, src, src_b, nblk, evac_eng):
            j0 = 0
            while j0 < nblk:
                n_ = min(4, nblk - j0)
                p = rot()
                pv = PS[p][:].bitcast(BF16)
                for j in range(n_):
                    tk.op("pe", lambda e, j=j: e.transpose(pv[:, j * 128:(j + 1) * 128],
                                                           src[:, (j0 + j) * 128:(j0 + j + 1) * 128], ident[:]),
                          R=src_b + [const_b], W=[PS_b[p]])
                if evac_eng == "act":
                    tk.op("act", lambda e: e.copy(dst[:, j0:j0 + n_, :],
                                                  pv[:, 0:n_ * 128].rearrange("p (a b) -> p a b", b=128)),
                          R=[PS_b[p]], W=dst_b)
                else:
                    tk.op("dve", lambda e: e.tensor_copy(dst[:, j0:j0 + n_, :],
                                                         pv[:, 0:n_ * 128].rearrange("p (a b) -> p a b", b=128)),
                          R=[PS_b[p]], W=dst_b)
                j0 += n_

        def layer_norm(buf, buf_b, eps):
            for j in range(2):
                tk.op("dve", lambda e, j=j: e.bn_stats(stats[:, j, :], buf[:, j * 512:(j + 1) * 512]),
                      R=[buf_b], W=[stats_b])
            tk.op("dve", lambda e: e.bn_aggr(mv[:], stats[:].rearrange("p a b -> p (a b)")), R=[stats_b], W=[mv_b])
            tk.op("pool", lambda e: e.tensor_scalar(rstd[:], mv[:, 1:2], eps, None, ALU.add), R=[mv_b], W=[rstd_b])
            tk.op("pool", lambda e: e.tensor_tensor(out=rstd[:], in0=rstd[:], in1=nhalf[:], op=ALU.pow),
                  R=[rstd_b, const_b], W=[rstd_b])
            tk.op("dve", lambda e: e.tensor_scalar(buf[:], buf[:], mv[:, 0:1], rstd[:], ALU.subtract, ALU.mult),
                  R=[buf_b, mv_b, rstd_b], W=[buf_b])
            tk.op("dve", lambda e: e.tensor_tensor(out=buf[:], in0=buf[:], in1=lnbc[:, 0, :], op=ALU.mult),
                  R=[buf_b, lnbc_b], W=[buf_b])
            tk.op("dve", lambda e: e.tensor_tensor(out=buf[:], in0=buf[:], in1=lnbc[:, 1, :], op=ALU.add),
                  R=[buf_b, lnbc_b], W=[buf_b])

        def emit_all():
            for l in range(depth):
                src_x = x_in if l == 0 else xmid_d[l - 1]
                dst_x = y_out if l == depth - 1 else xmid_d[l]
                if l > 0:
                    tk.drain("sp", d_st)
                for i in range(8):
                    tk.op("pool", lambda e, i=i: e.memset(St_f[:, i, :], 0.0), W=[St_fb[i]])
                    tk.op("pool", lambda e, i=i: e.memset(St_b[:, i, :], 0.0), W=[St_bb[i]])
                tk.dma("sp", d_x, X[:], src_x[0:128, :], W=[X_b])
                def front(t):
                    t0 = t * 128
                    tk.op("act", lambda e: e.copy(xb2[:], X[:]), R=[X_b], W=[xb2_b])
                    transpose_to(xT, [xT_b], xb2, [xb2_b], 8, "act")
                    tk.dma("sp", d_cs, cs[:, 0, :], c_cos[:, t0:t0 + 128], W=[cs_b])
                    tk.dma("sp", d_cs, cs[:, 1, :], c_sin[:, t0:t0 + 128], W=[cs_b])

                    blk = 0
                    for j in range(5):
                        wv, wb, c = next_chunk(l, t, f"qa{j}")
                        nb_ = c["cw"] // 128
                        p = rot()
                        for cb in range(nb_):
                            fm_group(p, cb * 128, wv, wb, c, cb, xT, [xT_b])
                        for hs_ in range(2):
                            tk.op("act", lambda e, p=p, nb_=nb_, blk=blk, hs_=hs_: e.copy(
                                qaT[hs_ * 64:(hs_ + 1) * 64, blk:blk + nb_, hs_, :],
                                PS[p][hs_ * 64:(hs_ + 1) * 64, 0:nb_ * 128].rearrange("p (a b) -> p a b", b=128)),
                                R=[PS_b[p]], W=[qaT_b])
                        blk += nb_
                        yield
                    blk = 0
                    for j in range(5):
                        wv, wb, c = next_chunk(l, t, f"ka{j}")
                        nb_ = c["cw"] // 128
                        p = rot()
                        for cb in range(nb_):
                            fm_group(p, cb * 128, wv, wb, c, cb, xT, [xT_b])
                        for cb in range(nb_):
                            g, hp = divmod(blk + cb, 3)
                            s = t % NBLK[g]
                            tk.op("dve", lambda e, p=p, cb=cb, g=g, hp=hp, s=s: e.tensor_copy(
                                Kr[g][:, hp, s, :], PS[p][:, cb * 128:(cb + 1) * 128]),
                                R=[PS_b[p]], W=[Kr_b[g][s]])
                        blk += nb_
                        yield
                    for j in range(6):
                        wv, wb, c = next_chunk(l, t, f"va{j}")
                        g, hh = divmod(j, 2)
                        s = t % NBLK[g]
                        p = rot()
                        tm_group(p, 0, wv, wb, c, xT, [xT_b])
                        tk.op("act", lambda e, p=p, g=g, hh=hh, s=s: e.copy(
                            Vr[g][:, s, 3 * hh:3 * hh + 3, 0:64], PS[p][:, 0:192].rearrange("p (a b) -> p a b", b=64)),
                            R=[PS_b[p]], W=[Vr_b[g][s]])
                        yield
                def attn_core(t):
                    jobs = []
                    for h in range(6):
                        for g in range(3):
                            nvis = min(t + 1, NBLK[g])
                            o0 = 0
                            while o0 < nvis:
                                n_ = min(4, nvis - o0)
                                jobs.append((h, g, o0, n_))
                                o0 += n_
                    first_of_head = {}
                    last_of_head = {}
                    for idx, (h, g, o0, n_) in enumerate(jobs):
                        first_of_head.setdefault(h, idx)
                        last_of_head[h] = idx
                    Uv = PS[L2][:, 0:390].rearrange("p (a b) -> p a b", b=65)

                    def emit_scores(idx):
                        h, g, o0, n_ = jobs[idx]
                        hp, hs = divmod(h, 2)
                        p = rot()
                        for j in range(n_):
                            s = (t - (o0 + j)) % NBLK[g]
                            mm(PS[p][:, j * 128:(j + 1) * 128],
                               Kr[g][:, hp, s, :], qaT[:, 3 * g + hp, hs, :],
                               True, True, R=[Kr_b[g][s], qaT_b], W=[PS_b[p]])
                        return p

                    def emit_soft(idx, p):
                        h, g, o0, n_ = jobs[idx]
                        w = n_ * 128
                        q = idx % 2
                        tk.op("act", lambda e: e.activation(out=pexp2[q][:, 0:w], in_=PS[p][:, 0:w], func=AF.Exp, scale=0.125),
                              R=[PS_b[p]], W=[pexp2_b[q]])
                        tk.op("dve", lambda e: e.tensor_tensor(
                            out=PT[q][:, 0:w], in0=pexp2[q][:, 0:w],
                            in1=Et[g][:, h, o0:o0 + n_, :].rearrange("p a b -> p (a b)"), op=ALU.mult),
                            R=[pexp2_b[q], Et_b], W=[PT_b[q]])

                    def emit_pv(idx):
                        h, g, o0, n_ = jobs[idx]
                        q = idx % 2
                        for j in range(n_):
                            s = (t - (o0 + j)) % NBLK[g]
                            first = (idx == first_of_head[h]) and j == 0
                            last = (idx == last_of_head[h]) and j == n_ - 1
                            mm(Uv[:, h, :], PT[q][:, j * 128:(j + 1) * 128], Vr[g][:, s, h, :], first, last,
                               R=[PT_b[q], Vr_b[g][s]], W=[PS_b[L2]], inc=last or (j == n_ - 1))
                            if not last and j == n_ - 1:
                                pass
                    pbank = {0: emit_scores(0)}
                    if len(jobs) > 1:
                        pbank[1] = emit_scores(1)
                    for idx in range(len(jobs)):
                        emit_soft(idx, pbank.pop(idx))
                        if idx + 2 < len(jobs):
                            pbank[idx + 2] = emit_scores(idx + 2)
                        emit_pv(idx)
                        yield
                    tk.op("dve", lambda e: e.reciprocal(rL[:, 0:6], Uv[:, :, 64:65].rearrange("p a b -> p (a b)")),
                          R=[PS_b[L2]], W=[rL_b])
                    tk.op("dve", lambda e: e.tensor_tensor(out=ya[:], in0=Uv[:, :, 0:64],
                                                           in1=rL[:, 0:6].unsqueeze(2).to_broadcast([128, 6, 64]), op=ALU.mult),
                          R=[PS_b[L2], rL_b], W=[ya_b])
                    transpose_to(yaT, [yaT_b], ya[:].rearrange("p a b -> p (a b)"), [ya_b], 3, "act")

                deferred = []

                def ret_proj(t):
                    def rotary(tag, dstT, dstT_b):
                        for bank in range(2):
                            p = rot()
                            for hh in range(2):
                                wv, wb, c = next_chunk(l, t, f"{tag}{bank * 2 + hh}")
                                for cb in range(2):
                                    fm_group(p, (hh * 2 + cb) * 128, wv, wb, c, cb, xT, [xT_b])
                                yield
                            pv = PS[p][:].rearrange("p (h f t) -> p h f t", h=2, f=2)
                            q1 = pv[:, :, 0, :]
                            q2 = pv[:, :, 1, :]
                            cosb = cs[:, 0:1, :].to_broadcast([128, 2, 128])
                            sinb = cs[:, 1:2, :].to_broadcast([128, 2, 128])
                            tav = ta[:, 0:256].rearrange("p (h t) -> p h t", h=2)
                            tbv = tb[:, 0:256].rearrange("p (h t) -> p h t", h=2)
                            dv = dstT[:, bank * 4:(bank + 1) * 4, :].rearrange("p (h f) t -> p h f t", f=2)
                            tk.op("dve", lambda e: e.tensor_tensor(out=tav, in0=q1, in1=cosb, op=ALU.mult),
                                  R=[PS_b[p], cs_b], W=[ta_b])
                            tk.op("dve", lambda e: e.tensor_tensor(out=tbv, in0=q2, in1=sinb, op=ALU.mult),
                                  R=[PS_b[p], cs_b], W=[tb_b])
                            tk.op("pool", lambda e: e.tensor_tensor(out=dv[:, :, 0, :], in0=tav, in1=tbv, op=ALU.subtract),
                                  R=[ta_b, tb_b], W=[dstT_b[bank]])
                            tk.op("dve", lambda e: e.tensor_tensor(out=tav, in0=q1, in1=sinb, op=ALU.mult),
                                  R=[PS_b[p], cs_b], W=[ta_b])
                            tk.op("dve", lambda e: e.tensor_tensor(out=tbv, in0=q2, in1=cosb, op=ALU.mult),
                                  R=[PS_b[p], cs_b], W=[tb_b])
                            tk.op("pool", lambda e: e.tensor_tensor(out=dv[:, :, 1, :], in0=tav, in1=tbv, op=ALU.add),
                                  R=[ta_b, tb_b], W=[dstT_b[bank]])
                    yield from rotary("qr", qrT, qrT_b)
                    for bank in range(2):
                        tk.op("pool", lambda e, bank=bank: e.tensor_tensor(
                            out=qtT[:, bank * 4:(bank + 1) * 4, :].rearrange("p (h f) t -> p h f t", f=2),
                            in0=qrT[:, bank * 4:(bank + 1) * 4, :].rearrange("p (h f) t -> p h f t", f=2),
                            in1=GQ[:, bank * 256:(bank + 1) * 256].rearrange("p (h t) -> p h t", h=2).unsqueeze(2).to_broadcast([128, 2, 2, 128]),
                            op=ALU.mult), R=[qrT_b[bank], const_b], W=[qtT_b[bank]])
                    yield from rotary("kr", krT, krT_b)
                    while deferred:
                        deferred.pop(0)()
                    tk.op("pool", lambda e: e.tensor_scalar(M[:], X[:], alpha, 0.0, ALU.mult, ALU.add), R=[X_b], W=[M_b])
                    if t + 1 < NT:
                        tk.dma("sp", d_x, X[:], src_x[(t + 1) * 128:(t + 2) * 128, :], W=[X_b])
                    for j in range(8):
                        wv, wb, c = next_chunk(l, t, f"vr{j}")
                        p = rot()
                        tm_group(p, 0, wv, wb, c, xT, [xT_b])
                        tk.op("act", lambda e, p=p, j=j: e.copy(vr[:, j * 256:(j + 1) * 256], PS[p][:, 0:256]),
                              R=[PS_b[p]], W=[vr_b[j]])
                        yield

                def ffn_steps(t):
                    def gu(j):
                        p = rot()
                        wv, wb, c = next_chunk(l, t, f"fg{j}")
                        for cb in range(2):
                            fm_group(p, cb * 128, wv, wb, c, cb, xmT, [xmT_b], bias=False)
                        wv, wb, c = next_chunk(l, t, f"fu{j}")
                        for cb in range(2):
                            fm_group(p, (2 + cb) * 128, wv, wb, c, cb, xmT, [xmT_b], bias=False)
                        tk.op("act", lambda e: e.activation(out=th[:, 0:256], in_=PS[p][:, 0:256], func=AF.Tanh, scale=0.5),
                              R=[PS_b[p]], W=[th_b])
                        tk.op("dve", lambda e: e.scalar_tensor_tensor(
                            out=th[:, 256:512], in0=th[:, 0:256], scalar=1.0, in1=PS[p][:, 0:256], op0=ALU.add, op1=ALU.mult),
                            R=[th_b, PS_b[p]], W=[th_b])
                        q = j % 2
                        tk.op("dve", lambda e: e.scalar_tensor_tensor(
                            out=hT[q][:].rearrange("p a b -> p (a b)"), in0=th[:, 256:512], scalar=0.5, in1=PS[p][:, 256:512],
                            op0=ALU.mult, op1=ALU.mult), R=[th_b, PS_b[p]], W=[hT_b[q]])

                    def down(j):
                        q = j % 2
                        wv, wb, c = next_chunk(l, t, f"fd{j}")
                        for hb, bank in enumerate((L0, L1)):
                            for k in range(2):
                                mm(PS[bank][:], hT[q][:, k, :], wv[:, k, hb * 512:(hb + 1) * 512],
                                   j == 0 and k == 0, j == 10 and k == 1, R=[hT_b[q], wb], W=[PS_b[bank]],
                                   inc=(k == 1 and hb == 1) or (j == 10 and k == 1))
                    gu(0)
                    yield
                    for j in range(11):
                        if j + 1 < 11:
                            gu(j + 1)
                        down(j)
                        yield

                def run(gen, mode=None):
                    rot_mode[0] = mode
                    try:
                        next(gen)
                        ok = True
                    except StopIteration:
                        ok = False
                    rot_mode[0] = None
                    return ok

                def chain(*gens):
                    for g_ in gens:
                        yield from g_

                for t in range(NT):
                    t0 = t * 128
                    if l + 1 < depth:
                        emit_conv(l + 1, len(plan) if t == NT - 1 else -(-len(plan) // NT))
                    if t == 0:
                        for _ in front(0):
                            pass
                        for _ in attn_core(0):
                            pass
                    for _ in ret_proj(t):
                        pass
                    for bank in range(2):
                        p = rot()
                        pv = PS[p][:].bitcast(BF16)
                        for j in range(4):
                            blk_ = bank * 4 + j
                            tk.op("pe", lambda e, j=j, blk_=blk_: e.transpose(pv[:, j * 128:(j + 1) * 128], krT[:, blk_, :], ident[:]),
                                  R=[krT_b[bank], const_b], W=[PS_b[p]])
                        tk.op("dve", lambda e, bank=bank, pv=pv: e.tensor_tensor(
                            out=ktm[:, bank * 512:(bank + 1) * 512].rearrange("p (h d) -> p h d", h=2),
                            in0=pv[:, 0:512].rearrange("p (h d) -> p h d", h=2),
                            in1=GK[:, bank * 2:bank * 2 + 2].unsqueeze(2).to_broadcast([128, 2, 256]), op=ALU.mult),
                            R=[PS_b[p], const_b], W=[ktm_b[bank]])
                    p = rot()
                    for h in range(4):
                        for f in range(2):
                            mm(PS[p][:, h * 128:(h + 1) * 128], krT[:, h * 2 + f, :], qrT[:, h * 2 + f, :], f == 0, f == 1,
                               R=[krT_b[h // 2], qrT_b[h // 2]], W=[PS_b[p]])
                    tk.op("dve", lambda e, p=p: e.tensor_tensor(out=scT[:], in0=PS[p][:], in1=DM[:], op=ALU.mult),
                          R=[PS_b[p], const_b], W=[scT_b])
                    def ret_stage1(h):
                        pg = rot()
                        for j in range(2):
                            wv, wb, c = next_chunk(l, t, f"gr{h * 2 + j}")
                            tm_group(pg, j * 256, wv, wb, c, xT, [xT_b])
                        tk.op("act", lambda e, pg=pg: e.activation(out=th[:], in_=PS[pg][:], func=AF.Tanh, scale=0.5),
                              R=[PS_b[pg]], W=[th_b])
                        sq = h % 2
                        tk.op("dve", lambda e, pg=pg, sq=sq: e.scalar_tensor_tensor(
                            out=sg[sq][:], in0=th[:], scalar=1.0, in1=PS[pg][:], op0=ALU.add, op1=ALU.mult),
                            R=[th_b, PS_b[pg]], W=[sg_b[sq]])
                        po = rot()
                        mm(PS[po][:], scT[:, h * 128:(h + 1) * 128], vr[:, h * 512:(h + 1) * 512], True, False,
                           R=[scT_b, vr_b[2 * h], vr_b[2 * h + 1]], W=[PS_b[po]])
                        for f in range(2):
                            mm(PS[po][:], qtT[:, h * 2 + f, :], St_b[:, h * 2 + f, :], False, f == 1,
                               R=[qtT_b[h // 2], St_bb[h * 2 + f]], W=[PS_b[po]])
                        tk.op("dve", lambda e, po=po: e.bn_stats(stats[:, 0, :], PS[po][:]), R=[PS_b[po]], W=[stats_b])
                        tk.op("dve", lambda e: e.bn_aggr(mv[:], stats[:, 0, :]), R=[stats_b], W=[mv_b])
                        tk.op("pool", lambda e: e.tensor_scalar(rstd[:], mv[:, 1:2], GN_EPS, None, ALU.add), R=[mv_b], W=[rstd_b])
                        tk.op("pool", lambda e: e.tensor_tensor(out=rstd[:], in0=rstd[:], in1=nhalf[:], op=ALU.pow),
                              R=[rstd_b, const_b], W=[rstd_b])
                        tk.op("dve", lambda e, po=po: e.tensor_scalar(gn[:], PS[po][:], mv[:, 0:1], rstd[:], ALU.subtract, ALU.mult),
                              R=[PS_b[po], mv_b, rstd_b], W=[gn_b])
                        tk.op("dve", lambda e, sq=sq: e.scalar_tensor_tensor(
                            out=yb[sq][:], in0=gn[:], scalar=0.5, in1=sg[sq][:], op0=ALU.mult, op1=ALU.mult),
                            R=[gn_b, sg_b[sq]], W=[yb_b[sq]])

                    def ret_stage2(h):
                        sq = h % 2
                        p = rot()
                        pv = PS[p][:].bitcast(BF16)
                        for j in range(4):
                            tk.op("pe", lambda e, j=j, sq=sq, pv=pv: e.transpose(pv[:, j * 128:(j + 1) * 128],
                                                                                 yb[sq][:, j * 128:(j + 1) * 128], ident[:]),
                                  R=[yb_b[sq], const_b], W=[PS_b[p]])
                        tk.op("act", lambda e, h=h, pv=pv: e.copy(ybT[:, 4 * h:4 * h + 4, :],
                                                                  pv[:, 0:512].rearrange("p (a b) -> p a b", b=128)),
                              R=[PS_b[p]], W=[ybT_b[h]])
                    ret_stage1(0)
                    for h in range(4):
                        if h + 1 < 4:
                            ret_stage1(h + 1)
                        ret_stage2(h)
                    for h in range(4):
                        for f in range(2):
                            i8 = h * 2 + f
                            p = rot()
                            mm(PS[p][:], ktm[:, i8 * 128:(i8 + 1) * 128], vr[:, h * 512:(h + 1) * 512], True, True,
                               R=[ktm_b[h // 2], vr_b[2 * h], vr_b[2 * h + 1]], W=[PS_b[p]])
                            tk.op("dve", lambda e, p=p, i8=i8, h=h: e.scalar_tensor_tensor(
                                out=St_f[:, i8, :], in0=St_f[:, i8, :], scalar=chunk_dec[h], in1=PS[p][:],
                                op0=ALU.mult, op1=ALU.add), R=[St_fb[i8], PS_b[p]], W=[St_fb[i8]])
                            tk.op("act", lambda e, i8=i8: e.copy(St_b[:, i8, :], St_f[:, i8, :]),
                                  R=[St_fb[i8]], W=[St_bb[i8]])

                    for half in range(2):
                        pA = rot()
                        for j in range(2):
                            wv, wb, c = next_chunk(l, t, f"ga{half}{j}")
                            for cb in range(2):
                                fm_group(pA, (j * 2 + cb) * 128, wv, wb, c, cb, xT, [xT_b])
                        tk.op("act", lambda e, pA=pA: e.activation(out=sgA[:], in_=PS[pA][:], func=AF.Tanh, scale=0.5),
                              R=[PS_b[pA]], W=[sgA_b])
                        tk.op("pool", lambda e: e.tensor_scalar(sgA[:], sgA[:], 0.5, 0.5, ALU.mult, ALU.add),
                              R=[sgA_b], W=[sgA_b])
                        pB = rot()
                        for j in range(2):
                            wv, wb, c = next_chunk(l, t, f"gb{half}{j}")
                            for cb in range(2):
                                fm_group(pB, (j * 2 + cb) * 128, wv, wb, c, cb, xT, [xT_b])
                        tk.op("act", lambda e, pB=pB: e.activation(out=sgB[:], in_=PS[pB][:], func=AF.Tanh, scale=0.5),
                              R=[PS_b[pB]], W=[sgB_b])
                        tk.op("pool", lambda e: e.tensor_scalar(sgB[:], sgB[:], 0.5, 0.5, ALU.mult, ALU.add),
                              R=[sgB_b], W=[sgB_b])
                        wv, wb, c = next_chunk(l, t, f"ap{half}")
                        pa = rot()
                        for cb in range(4):
                            fm_group(pa, cb * 128, wv, wb, c, cb, yaT, [yaT_b], bias=False)
                        tk.op("dve", lambda e, pa=pa: e.tensor_tensor(out=ta[:], in0=PS[pa][:], in1=sgA[:], op=ALU.mult),
                              R=[PS_b[pa], sgA_b], W=[ta_b])
                        pb_ = rot()
                        for j in range(4):
                            wv, wb, c = next_chunk(l, t, f"rp{half}{j}")
                            fm_group(pb_, j * 128, wv, wb, c, 0, ybT, ybT_b, bias=False)
                        tk.op("dve", lambda e, pb_=pb_: e.tensor_tensor(out=tb[:], in0=PS[pb_][:], in1=sgB[:], op=ALU.mult),
                              R=[PS_b[pb_], sgB_b], W=[tb_b])
                        tk.op("pool", lambda e, half=half: e.tensor_tensor(
                            out=mT[:, half * 4:(half + 1) * 4, :].rearrange("p a b -> p (a b)"), in0=ta[:], in1=tb[:], op=ALU.add),
                            R=[ta_b, tb_b], W=[mT_b[half]])

                    tk.dma("sp", d_ln, lnbc[:, 0, :], ln_in["ln1_g"][l:l + 1, :].to_broadcast([128, D]), W=[lnbc_b])
                    tk.dma("sp", d_ln, lnbc[:, 1, :], ln_in["ln1_b"][l:l + 1, :].to_broadcast([128, D]), W=[lnbc_b])
                    for j in range(4):
                        wv, wb, c = next_chunk(l, t, f"wo{j}")
                        bank = L0 if j < 2 else L1
                        tm_group(bank, (j % 2) * 256, wv, wb, c, mT, mT_b, bias=False)
                    for j, bank in enumerate((L0, L1)):
                        tk.op("dve", lambda e, j=j, bank=bank: e.tensor_tensor(
                            out=M[:, j * 512:(j + 1) * 512], in0=M[:, j * 512:(j + 1) * 512], in1=PS[bank][:], op=ALU.add),
                            R=[M_b, PS_b[bank]], W=[M_b])
                    layer_norm(M, M_b, LN_EPS)
                    gA = chain(front(t + 1), attn_core(t + 1)) if t + 1 < NT else iter(())
                    a_live = True
                    for _ in range(16):
                        a_live = a_live and run(gA, "A")
                    tk.op("act", lambda e: e.copy(xb[:], M[:]), R=[M_b], W=[xb_b])
                    transpose_to(xmT, [xmT_b], xb, [xb_b], 8, "dve")

                    tk.dma("sp", d_ln, lnbc[:, 0, :], ln_in["ln2_g"][l:l + 1, :].to_broadcast([128, D]), R=[M_b], W=[lnbc_b])
                    tk.dma("sp", d_ln, lnbc[:, 1, :], ln_in["ln2_b"][l:l + 1, :].to_broadcast([128, D]), R=[M_b], W=[lnbc_b])
                    gE = ffn_steps(t)
                    e_live = True
                    while e_live:
                        e_live = run(gE, "B")
                        for _ in range(4):
                            a_live = a_live and run(gA, "A")
                    while a_live:
                        a_live = run(gA, "A")
                    for j, bank in enumerate((L0, L1)):
                        tk.op("dve", lambda e, j=j, bank=bank: e.scalar_tensor_tensor(
                            out=M[:, j * 512:(j + 1) * 512], in0=M[:, j * 512:(j + 1) * 512], scalar=alpha, in1=PS[bank][:],
                            op0=ALU.mult, op1=ALU.add), R=[M_b, PS_b[bank]], W=[M_b])
                    def _ln2_store(t0=t0):
                        layer_norm(M, M_b, LN_EPS)
                        tk.dma("sp", d_st, dst_x[t0:t0 + 128, :], M[:], R=[M_b])
                    deferred.append(_ln2_store)
                    if t + 1 >= NT:
                        while deferred:
                            deferred.pop(0)()

        tk.dry = True
        emit_all()
        tk.dry = False
        conv_todo.update({l: ([] if l == 0 else list(range(len(plan)))) for l in range(depth)})
        rot_state[0] = 0
        rot_sub.update({"A": 0, "B": 0})
        for _ in range(NSLOT - 1):
            issue_next()
        emit_all()
        tk.drain("sp", d_st)
    return nc


_PROG_CACHE = {}


def _run(inputs, S, depth, n_cores):
    alpha = float((2 * depth) ** 0.25)
    consts, chunk_dec = make_consts(S)
    key = (S, depth)
    if key not in _PROG_CACHE:
        _PROG_CACHE[key] = build_program(S, depth, alpha, chunk_dec)
    nc = _PROG_CACHE[key]
    x = np.ascontiguousarray(np.asarray(inputs["x"], dtype=np.float32))
    shared = {k: np.ascontiguousarray(np.asarray(v, dtype=np.float32)) for k, v in inputs.items() if k != "x"}
    shared.update(consts)
    in_maps = []
    for c in range(n_cores):
        m = dict(shared)
        m["x"] = x[c]
        in_maps.append(m)
    res = run_bass_kernel_spmd(nc, in_maps, core_ids=list(range(n_cores)))
    return np.stack([np.asarray(r["y"], dtype=np.float32) for r in res.results], axis=0)


def kernel(**inputs):
    x = np.asarray(inputs["x"])
    B, S, _ = x.shape
    depth = int(np.asarray(inputs["w_in"]).shape[0])
    return _run(inputs, S, depth, B)
```

```python
import contextlib
import numpy as np
import concourse.bass as bass
import concourse.mybir as mybir
from concourse.bass_utils import run_bass_kernel_spmd

F32 = mybir.dt.float32
BF16 = mybir.dt.bfloat16
AF = mybir.ActivationFunctionType
ALU = mybir.AluOpType

D = 1024
HD = 64
GROUPS = ((128, 1), (512, 4), (2048, 16))
NBLK = (2, 5, 17)
RH = 4
RDK = 256
RDV = 512
DFF = 2816
IN_COLS = 11648
NUM_BUCKETS = 32
MAX_DISTANCE = 2048
LF = 2304
COL = dict(qa=0, ka=1152, va=2304, qr=3456, kr=4480, vr=5504, gr=7552, ga=9600, gb=10624)
SLOT_ELEMS = 2304
NSLOT = 7
LN_EPS = 1e-5
GN_EPS = 1e-5


def _t5_bucket(dist):
    max_exact = NUM_BUCKETS // 2
    large = max_exact + (np.log(np.maximum(dist, max_exact) / max_exact)
                         / np.log(MAX_DISTANCE / max_exact)
                         * (NUM_BUCKETS - max_exact)).astype(np.int32)
    large = np.minimum(large, NUM_BUCKETS - 1)
    return np.where(dist < max_exact, dist, large).astype(np.int32)


def make_consts(S):
    half = RDK // 2
    pos = np.arange(S, dtype=np.float32)
    inv_freq = (10000.0 ** (-np.arange(half, dtype=np.float32) / half)).astype(np.float32)
    ang = (pos[None, :] * inv_freq[:, None]).astype(np.float32)
    cos_t = np.cos(ang).astype(np.float32)
    sin_t = np.sin(ang).astype(np.float32)
    log_g = np.log(1.0 - 2.0 ** (-5.0 - np.arange(RH, dtype=np.float64)))
    n = np.arange(128, dtype=np.float64)
    gq = np.exp(log_g[:, None] * (n[None, :] + 1.0))
    GQ = np.broadcast_to(gq.reshape(1, 4 * 128), (128, 512)).astype(np.float32).copy()
    diff = n[None, :] - n[:, None]
    DM = np.zeros((128, 4, 128), np.float64)
    for h in range(RH):
        DM[:, h, :] = np.where(diff >= 0, np.exp(log_g[h] * np.maximum(diff, 0.0)), 0.0) / 16.0
    DM = DM.reshape(128, 512).astype(np.float32)
    GK = (np.exp(log_g[None, :] * (127.0 - n[:, None])) / 16.0).astype(np.float32)
    chunk_dec = [float(np.exp(log_g[h] * 128.0)) for h in range(RH)]
    ident = np.eye(128, dtype=np.float32)
    J = ident[::-1].copy()
    delta = np.arange(LF) - 127
    bk = _t5_bucket(np.clip(delta, 0, MAX_DISTANCE))
    OH = np.zeros((NUM_BUCKETS, LF), np.float32)
    OH[bk, np.arange(LF)] = 1.0
    VALID = np.zeros((18, LF), np.float32)
    for g, (win, dil) in enumerate(GROUPS):
        v = ((delta >= 0) & (delta % dil == 0) & (delta <= win)).astype(np.float32)
        VALID[6 * g:6 * g + 6, :] = v[None, :]
    return dict(cos_t=cos_t, sin_t=sin_t, GQ=GQ, DM=DM, GK=GK, ident=ident, J=J, OH=OH,
                VALID=VALID, zeros=np.zeros((128, 256), np.float32)), chunk_dec


class Buf:
    __slots__ = ("w", "rs", "name")

    def __init__(self, name=""):
        self.w = None
        self.rs = {}
        self.name = name


class _Src:
    def __init__(self, name, sem, step):
        self.name, self.sem, self.step, self.cnt = name, sem, step, 0


class Tracker:
    def __init__(self, nc, es):
        self.nc = nc
        self.es = es
        self.src = {}
        self.handles = {"pe": nc.tensor, "act": nc.scalar, "dve": nc.vector,
                        "pool": nc.gpsimd, "sp": nc.sync}
        for e in ("pe", "act", "dve", "pool"):
            self.src[e] = _Src(e, es.enter_context(nc.semaphore("s_" + e)), 1)
        self.seen = {e: {} for e in self.handles}
        self.dry = False

    def dsem(self, name):
        s = _Src(name, self.es.enter_context(self.nc.semaphore(name)), 16)
        self.src[name] = s
        return s

    def _waits(self, eng, R, W):
        need = {}

        def req(ev, same_ok):
            if ev is None:
                return
            s, c = ev
            if s == eng and not same_ok:
                return
            if c > need.get(s, 0):
                need[s] = c
        for b in R:
            req(b.w, True)
        for b in W:
            req(b.w, False)
            for s, c in b.rs.items():
                req((s, c), False)
        h = self.handles[eng]
        seen = self.seen[eng]
        for s, c in need.items():
            if seen.get(s, 0) >= c:
                continue
            src = self.src[s]
            assert c <= src.cnt, (eng, s, c, src.cnt)
            h.wait_ge(src.sem, c)
            seen[s] = c

    def op(self, eng, fn, R=(), W=(), inc=True):
        if self.dry:
            return None
        self._waits(eng, R, W)
        src = self.src[eng]
        ins = fn(self.handles[eng])
        ticket = src.cnt + 1
        if inc:
            ins.then_inc(src.sem, 1)
            src.cnt += 1
        else:
            assert eng == "pe"
        for b in R:
            b.rs[eng] = ticket
        for b in W:
            b.w = (eng, ticket)
            b.rs = {}
        return ins

    def dma(self, eng, ds, out, in_, R=(), W=()):
        if self.dry:
            return
        self._waits(eng, R, W)
        ins = self.handles[eng].dma_start(out=out, in_=in_)
        ins.then_inc(ds.sem, 16)
        ds.cnt += 16
        for b in R:
            b.rs[ds.name] = ds.cnt
        for b in W:
            b.w = (ds.name, ds.cnt)
            b.rs = {}

    def drain(self, eng, ds):
        if self.dry:
            return
        if self.seen[eng].get(ds.name, 0) < ds.cnt:
            self.handles[eng].wait_ge(ds.sem, ds.cnt)
            self.seen[eng][ds.name] = ds.cnt


def chunk_plan():
    P = []

    def win(tag, col0, width, cw):
        c = 0
        j = 0
        while c < width:
            w = min(cw, width - c)
            P.append(dict(name=f"{tag}{j}", src="w_in", kc=8, cw=w, row0=0, col0=col0 + c, bias=True))
            c += w
            j += 1
    win("qa", COL["qa"], 1152, 256)
    win("ka", COL["ka"], 1152, 256)
    win("va", COL["va"], 1152, 192)
    win("qr", COL["qr"], 1024, 256)
    win("kr", COL["kr"], 1024, 256)
    win("vr", COL["vr"], 2048, 256)
    win("gr", COL["gr"], 2048, 256)
    for half in range(2):
        for j in range(2):
            P.append(dict(name=f"ga{half}{j}", src="w_in", kc=8, cw=256, row0=0,
                          col0=COL["ga"] + half * 512 + j * 256, bias=True))
        for j in range(2):
            P.append(dict(name=f"gb{half}{j}", src="w_in", kc=8, cw=256, row0=0,
                          col0=COL["gb"] + half * 512 + j * 256, bias=True))
        P.append(dict(name=f"ap{half}", src="w_attn_proj", kc=3, cw=512, row0=0, col0=half * 512, bias=False))
        for j in range(4):
            P.append(dict(name=f"rp{half}{j}", src="w_ret_proj", kc=16, cw=128, row0=0,
                          col0=half * 512 + j * 128, bias=False))
    for j in range(4):
        P.append(dict(name=f"wo{j}", src="w_out", kc=8, cw=256, row0=0, col0=j * 256, bias=False))
    for j in range(11):
        P.append(dict(name=f"fg{j}", src="w_ffn_gate", kc=8, cw=256, row0=0, col0=j * 256, bias=False))
        P.append(dict(name=f"fu{j}", src="w_ffn_up", kc=8, cw=256, row0=0, col0=j * 256, bias=False))
        P.append(dict(name=f"fd{j}", src="w_ffn_down", kc=2, cw=1024, row0=j * 256, col0=0, bias=False))
    for c in P:
        c["kk"] = c["kc"] + (1 if c["bias"] else 0)
        assert c["kk"] * c["cw"] <= SLOT_ELEMS, c
    return P


def build_program(S, depth, alpha, chunk_dec):
    NT = S // 128
    nc = bass.Bass("TRN2", target_bir_lowering=False)
    es = contextlib.ExitStack()

    def din(name, shape, dt=F32):
        return nc.dram_tensor(name, list(shape), dt, kind="ExternalInput").ap()

    x_in = din("x", [S, D])
    rel_bias = din("rel_bias", [NUM_BUCKETS, 18])
    Wd = dict(w_in=din("w_in", [depth, D, IN_COLS]), w_attn_proj=din("w_attn_proj", [depth, 384, D]),
              w_ret_proj=din("w_ret_proj", [depth, 2048, D]), w_out=din("w_out", [depth, D, D]),
              w_ffn_gate=din("w_ffn_gate", [depth, D, DFF]), w_ffn_up=din("w_ffn_up", [depth, D, DFF]),
              w_ffn_down=din("w_ffn_down", [depth, DFF, D]))
    b_in = din("b_in", [depth, IN_COLS])
    ln_in = dict(ln1_g=din("ln1_g", [depth, D]), ln1_b=din("ln1_b", [depth, D]),
                 ln2_g=din("ln2_g", [depth, D]), ln2_b=din("ln2_b", [depth, D]))
    c_cos = din("cos_t", [128, S])
    c_sin = din("sin_t", [128, S])
    c_GQ = din("GQ", [128, 512])
    c_DM = din("DM", [128, 512])
    c_GK = din("GK", [128, 4])
    c_ident = din("ident", [128, 128])
    c_J = din("J", [128, 128])
    c_OH = din("OH", [NUM_BUCKETS, LF])
    c_VALID = din("VALID", [18, LF])
    c_zeros = din("zeros", [128, 256])
    y_out = nc.dram_tensor("y", [S, D], F32, kind="ExternalOutput").ap()
    xmid_d = [nc.dram_tensor(f"xl{l}", [S, D], F32, kind="Internal").ap() for l in range(depth - 1)]
    Fd = nc.dram_tensor("Fd", [18, LF], BF16, kind="Internal")
    plan = chunk_plan()
    wsc = [[nc.dram_tensor(f"ws{l}_{c['name']}", [128, c["kk"] * c["cw"]], BF16, kind="Internal").ap()
            for c in plan] for l in range(depth)]

    with es:
        tk = Tracker(nc, es)

        def sb(name, shape, dt):
            return es.enter_context(nc.sbuf_tensor("sb_" + name, list(shape), dt))

        slots = [sb(f"slot{i}", [128, SLOT_ELEMS], BF16) for i in range(NSLOT)]
        slot_b = [Buf(f"slot{i}") for i in range(NSLOT)]
        slot_sem = [tk.dsem(f"d_slot{i}") for i in range(NSLOT)]
        Et = [sb(f"Et{g}", [128, 6, NBLK[g], 128], BF16) for g in range(3)]
        Et_b = Buf("Et")
        Kr = [sb(f"Kr{g}", [128, 3, NBLK[g], 128], BF16) for g in range(3)]
        Vr = [sb(f"Vr{g}", [128, NBLK[g], 6, 65], BF16) for g in range(3)]
        Kr_b = [[Buf(f"Kr{g}_{s}") for s in range(NBLK[g])] for g in range(3)]
        Vr_b = [[Buf(f"Vr{g}_{s}") for s in range(NBLK[g])] for g in range(3)]
        St_f = sb("St_f", [128, 8, 512], F32)
        St_b = sb("St_b", [128, 8, 512], BF16)
        St_fb = [Buf(f"Stf{i}") for i in range(8)]
        St_bb = [Buf(f"Stb{i}") for i in range(8)]
        X = sb("X", [128, D], F32); X_b = Buf("X")
        M = sb("M", [128, D], F32); M_b = Buf("M")
        O = sb("O", [128, D], F32); O_b = Buf("O")
        xb = sb("xb", [128, D], BF16); xb_b = Buf("xb")
        xb2 = sb("xb2", [128, D], BF16); xb2_b = Buf("xb2")
        xT = sb("xT", [128, 8, 128], BF16); xT_b = Buf("xT")
        xmT = sb("xmT", [128, 8, 128], BF16); xmT_b = Buf("xmT")
        qaT = sb("qaT", [128, 9, 128], BF16); qaT_b = Buf("qaT")
        ta = sb("ta", [128, 512], F32); ta_b = Buf("ta")
        tb = sb("tb", [128, 512], F32); tb_b = Buf("tb")
        qrT = sb("qrT", [128, 8, 128], BF16); qrT_b = [Buf("qrT0"), Buf("qrT1")]
        krT = sb("krT", [128, 8, 128], BF16); krT_b = [Buf("krT0"), Buf("krT1")]
        qtT = sb("qtT", [128, 8, 128], BF16); qtT_b = [Buf("qtT0"), Buf("qtT1")]
        ktm = sb("ktm", [128, 1024], BF16); ktm_b = [Buf("ktm0"), Buf("ktm1")]
        vr = sb("vr", [128, 2048], BF16); vr_b = [Buf(f"vr{i}") for i in range(8)]
        sg = [sb(f"sg{i}", [128, 512], F32) for i in range(2)]; sg_b = [Buf("sg0"), Buf("sg1")]
        th = sb("th", [128, 512], F32); th_b = Buf("th")
        pexp2 = [sb(f"pexp{i}", [128, 512], BF16) for i in range(2)]; pexp2_b = [Buf("pexp0"), Buf("pexp1")]
        pexp = pexp2[0]; pexp_b = pexp2_b[0]
        PT = [sb(f"PT{i}", [128, 512], BF16) for i in range(2)]; PT_b = [Buf("PT0"), Buf("PT1")]
        scT = sb("scT", [128, 512], BF16); scT_b = Buf("scT")
        gn = sb("gn", [128, 512], F32); gn_b = Buf("gn")
        yb = [sb(f"yb{i}", [128, 512], BF16) for i in range(2)]; yb_b = [Buf("yb0"), Buf("yb1")]
        ybT = sb("ybT", [128, 16, 128], BF16); ybT_b = [Buf(f"ybT{i}") for i in range(4)]
        ya = sb("ya", [128, 6, 64], BF16); ya_b = Buf("ya")
        yaT = sb("yaT", [128, 3, 128], BF16); yaT_b = Buf("yaT")
        rL = sb("rL", [128, 8], F32); rL_b = Buf("rL")
        sgA = sb("sgA", [128, 512], F32); sgA_b = Buf("sgA")
        sgB = sb("sgB", [128, 512], F32); sgB_b = Buf("sgB")
        mT = sb("mT", [128, 8, 128], BF16); mT_b = [Buf("mT0"), Buf("mT1")]
        hT = [sb(f"hT{i}", [128, 2, 128], BF16) for i in range(2)]; hT_b = [Buf("hT0"), Buf("hT1")]
        cs = sb("cs", [128, 2, 128], F32); cs_b = Buf("cs")
        lnbc = sb("lnbc", [128, 2, D], F32); lnbc_b = Buf("lnbc")
        ident = sb("ident", [128, 128], BF16)
        Jm = sb("Jm", [128, 128], BF16)
        GQ = sb("GQ", [128, 512], F32)
        DM = sb("DM", [128, 512], F32)
        GK = sb("GK", [128, 4], F32)
        ones = sb("ones", [128, 128], BF16)
        nhalf = sb("nhalf", [128, 1], F32)
        stats = sb("stats", [128, 2, 6], F32); stats_b = Buf("stats")
        mv = sb("mv", [128, 2], F32); mv_b = Buf("mv")
        rstd = sb("rstd", [128, 1], F32); rstd_b = Buf("rstd")
        nb = sb("nb", [128, 1], F32); nb_b = Buf("nb")
        const_b = Buf("const")
        PS = [es.enter_context(nc.psum_tensor(f"ps{i}", [128, 512], F32)) for i in range(8)]
        PS_b = [Buf(f"ps{i}") for i in range(8)]
        rot_state = [0]
        rot_mode = [None]
        rot_sub = {"A": 0, "B": 0}

        def rot():
            m = rot_mode[0]
            if m is None:
                i = rot_state[0]
                rot_state[0] = (i + 1) % 5
                return i
            if m == "A":
                i = rot_sub[m]
                rot_sub[m] = (i + 1) % 3
                return i
            i = rot_sub[m]
            rot_sub[m] = (i + 1) % 2
            return 3 + i
        L0, L1, L2 = 5, 6, 7

        d_misc = tk.dsem("d_misc")
        d_x = tk.dsem("d_x")
        d_cs = tk.dsem("d_cs")
        d_ln = tk.dsem("d_ln")
        d_st = tk.dsem("d_st")
        d_F = tk.dsem("d_F")
        d_H = tk.dsem("d_H")
        NCG = 6
        d_cv = [[tk.dsem(f"d_cv{l}_{k}") for k in range(NCG)] for l in range(depth)]

        rb = sb("rb", [NUM_BUCKETS, 18], F32)
        d_misc2 = tk.dsem("d_misc2")
        tk.dma("pool", d_misc2, ident[:], c_ident, W=[const_b])
        tk.dma("pool", d_misc2, Jm[:], c_J, W=[const_b])
        tk.dma("sp", d_misc, GQ[:], c_GQ, W=[const_b])
        tk.dma("sp", d_misc, DM[:], c_DM, W=[const_b])
        tk.dma("sp", d_misc, GK[:], c_GK, W=[const_b])
        tk.dma("sp", d_misc, rb[:], rel_bias, W=[const_b])
        for e_ in ("pe", "act", "dve", "pool", "sp"):
            tk.drain(e_, d_misc)
            tk.drain(e_, d_misc2)
        tk.op("pool", lambda e: e.memset(ones[:], 1.0), W=[const_b])
        tk.op("pool", lambda e: e.memset(nhalf[:], -0.5), W=[const_b])
        for g in range(3):
            tk.op("pool", lambda e, g=g: e.memset(Vr[g][:], 1.0), W=Vr_b[g])

        def cgroup(ci):
            return min(NCG - 1, ci * NCG // len(plan))
        conv_todo = {l: list(range(len(plan))) for l in range(depth)}

        def emit_conv(l, n):
            for _ in range(n):
                if not conv_todo[l]:
                    return
                ci = conv_todo[l].pop(0)
                c = plan[ci]
                ds = d_cv[l][cgroup(ci)]
                kc, cw = c["kc"], c["cw"]
                src = Wd[c["src"]][l, c["row0"]:c["row0"] + kc * 128, c["col0"]:c["col0"] + cw]
                src = src.rearrange("(k p) c -> p k c", p=128)
                dst = wsc[l][ci][:, 0:kc * cw].rearrange("p (k c) -> p k c", k=kc)
                if tk.dry:
                    continue
                ins = nc.gpsimd.dma_start(out=dst, in_=src)
                ins.then_inc(ds.sem, 16)
                ds.cnt += 16
                if c["bias"]:
                    bsrc = b_in[l:l + 1, c["col0"]:c["col0"] + cw]
                    ins = nc.gpsimd.dma_start(out=wsc[l][ci][0:1, kc * cw:(kc + 1) * cw], in_=bsrc)
                    ins.then_inc(ds.sem, 16)
                    ds.cnt += 16
                    ins = nc.gpsimd.dma_start(out=wsc[l][ci][1:128, kc * cw:(kc + 1) * cw], in_=c_zeros[0:127, 0:cw])
                    ins.then_inc(ds.sem, 16)
                    ds.cnt += 16
        emit_conv(0, len(plan))

        d_oh = tk.dsem("d_oh")
        d_vl = tk.dsem("d_vl")
        Fd_b = Buf("Fd")
        nchk = (LF + 511) // 512
        for j in range(nchk):
            c0 = j * 512
            w = min(512, LF - c0)
            tk.dma("sp", d_oh, ta[0:NUM_BUCKETS, 0:w], c_OH[:, c0:c0 + w], W=[ta_b])
            tk.dma("sp", d_vl, tb[0:18, 0:w], c_VALID[:, c0:c0 + w], W=[tb_b])
            p = rot()
            tk.op("pe", lambda e: e.matmul(PS[p][0:18, 0:w], lhsT=rb[:, :], rhs=ta[0:NUM_BUCKETS, 0:w],
                                           start=True, stop=True), R=[ta_b, const_b], W=[PS_b[p]])
            tk.op("act", lambda e: e.activation(out=gn[0:18, 0:w], in_=PS[p][0:18, 0:w], func=AF.Exp),
                  R=[PS_b[p]], W=[gn_b])
            tk.op("dve", lambda e: e.tensor_tensor(out=pexp[0:18, 0:w], in0=gn[0:18, 0:w], in1=tb[0:18, 0:w], op=ALU.mult),
                  R=[gn_b, tb_b], W=[pexp_b])
            tk.dma("sp", d_F, Fd.ap()[:, c0:c0 + w], pexp[0:18, 0:w], R=[pexp_b], W=[Fd_b])
        Hs = St_b[:].rearrange("p a (b c) -> p (a b) c", c=128)
        for g in range(3):
            no = NBLK[g]
            for h in range(6):
                hh = 6 * g + h
                srcap = bass.AP(Fd, hh * LF, [[1, 128], [128, no], [1, 128]])
                tk.dma("sp", d_H, Hs[:, 0:no, :], srcap, R=[Fd_b], W=St_bb)
                o0 = 0
                while o0 < no:
                    n_ = min(4, no - o0)
                    p = rot()
                    tk.op("pe", lambda e: e.matmul(PS[p][:, 0:n_ * 128], lhsT=Jm[:], rhs=Hs[:, o0:o0 + n_, :],
                                                   start=True, stop=True), R=St_bb + [const_b], W=[PS_b[p]])
                    tk.op("dve", lambda e: e.tensor_copy(
                        Et[g][:, h, o0:o0 + n_, :], PS[p][:, 0:n_ * 128].rearrange("p (a b) -> p a b", b=128)),
                        R=[PS_b[p]], W=[Et_b])
                    o0 += n_

        stream = []
        name2ci = {c["name"]: i for i, c in enumerate(plan)}
        issue_ptr = [0]
        use_ptr = [0]
        cv_drained = set()

        def issue_next():
            k = issue_ptr[0]
            if k >= len(stream):
                return
            l, t, ci = stream[k]
            c = plan[ci]
            s = k % NSLOT
            key = (l, cgroup(ci))
            if key not in cv_drained:
                tk.drain("sp", d_cv[l][cgroup(ci)])
                cv_drained.add(key)
            n = c["kk"] * c["cw"]
            tk.dma("sp", slot_sem[s], slots[s][:, 0:n], wsc[l][ci][:, 0:n], W=[slot_b[s]])
            issue_ptr[0] += 1

        def next_chunk(l, t, name):
            if tk.dry:
                k = len(stream)
                stream.append((l, t, name2ci[name]))
            else:
                k = use_ptr[0]
                assert stream[k] == (l, t, name2ci[name]), (stream[k], l, t, name)
                use_ptr[0] += 1
                while issue_ptr[0] < min(len(stream), k + NSLOT):
                    issue_next()
            s = k % NSLOT
            c = plan[stream[k][2]]
            view = slots[s][:, 0:c["kk"] * c["cw"]].rearrange("p (k c) -> p k c", k=c["kk"])
            return view, slot_b[s], c

        def mm(out, lhsT, rhs, start, stop, R, W, inc=None):
            tk.op("pe", lambda e: e.matmul(out, lhsT=lhsT, rhs=rhs, start=start, stop=stop), R=R, W=W,
                  inc=stop if inc is None else inc)

        def fm_group(pbank, col, wv, wb, c, cb, act, act_b, bias=True):
            kc = c["kc"]
            for k in range(kc):
                mm(PS[pbank][:, col:col + 128], wv[:, k, cb * 128:(cb + 1) * 128], act[:, k, :],
                   k == 0, (k == kc - 1) and not bias, R=[wb] + act_b, W=[PS_b[pbank]])
            if bias:
                mm(PS[pbank][:, col:col + 128], wv[:, kc, cb * 128:(cb + 1) * 128], ones[:, :],
                   False, True, R=[wb, const_b], W=[PS_b[pbank]])

        def tm_group(pbank, col, wv, wb, c, act, act_b, bias=True):
            kc, cw = c["kc"], c["cw"]
            for k in range(kc):
                mm(PS[pbank][:, col:col + cw], act[:, k, :], wv[:, k, :], k == 0, (k == kc - 1) and not bias,
                   R=[wb] + act_b, W=[PS_b[pbank]])
            if bias:
                mm(PS[pbank][:, col:col + cw], ones[:, :], wv[:, kc, :], False, True,
                   R=[wb, const_b], W=[PS_b[pbank]])

        def transpose_to(dst, dst_b, src, src_b, nblk, evac_eng):
            j0 = 0
            while j0 < nblk:
                n_ = min(4, nblk - j0)
                p = rot()
                pv = PS[p][:].bitcast(BF16)
                for j in range(n_):
                    tk.op("pe", lambda e, j=j: e.transpose(pv[:, j * 128:(j + 1) * 128],
                                                           src[:, (j0 + j) * 128:(j0 + j + 1) * 128], ident[:]),
                          R=src_b + [const_b], W=[PS_b[p]])
                if evac_eng == "act":
                    tk.op("act", lambda e: e.copy(dst[:, j0:j0 + n_, :],
                                                  pv[:, 0:n_ * 128].rearrange("p (a b) -> p a b", b=128)),
                          R=[PS_b[p]], W=dst_b)
                else:
                    tk.op("dve", lambda e: e.tensor_copy(dst[:, j0:j0 + n_, :],
                                                         pv[:, 0:n_ * 128].rearrange("p (a b) -> p a b", b=128)),
                          R=[PS_b[p]], W=dst_b)
                j0 += n_

        def layer_norm(buf, buf_b, eps):
            for j in range(2):
                tk.op("dve", lambda e, j=j: e.bn_stats(stats[:, j, :], buf[:, j * 512:(j + 1) * 512]),
                      R=[buf_b], W=[stats_b])
            tk.op("dve", lambda e: e.bn_aggr(mv[:], stats[:].rearrange("p a b -> p (a b)")), R=[stats_b], W=[mv_b])
            tk.op("pool", lambda e: e.tensor_scalar(rstd[:], mv[:, 1:2], eps, None, ALU.add), R=[mv_b], W=[rstd_b])
            tk.op("pool", lambda e: e.tensor_tensor(out=rstd[:], in0=rstd[:], in1=nhalf[:], op=ALU.pow),
                  R=[rstd_b, const_b], W=[rstd_b])
            tk.op("dve", lambda e: e.tensor_scalar(buf[:], buf[:], mv[:, 0:1], rstd[:], ALU.subtract, ALU.mult),
                  R=[buf_b, mv_b, rstd_b], W=[buf_b])
            tk.op("dve", lambda e: e.tensor_tensor(out=buf[:], in0=buf[:], in1=lnbc[:, 0, :], op=ALU.mult),
                  R=[buf_b, lnbc_b], W=[buf_b])
            tk.op("dve", lambda e: e.tensor_tensor(out=buf[:], in0=buf[:], in1=lnbc[:, 1, :], op=ALU.add),
                  R=[buf_b, lnbc_b], W=[buf_b])

        def emit_all():
            for l in range(depth):
                src_x = x_in if l == 0 else xmid_d[l - 1]
                dst_x = y_out if l == depth - 1 else xmid_d[l]
                if l > 0:
                    tk.drain("sp", d_st)
                for i in range(8):
                    tk.op("pool", lambda e, i=i: e.memset(St_f[:, i, :], 0.0), W=[St_fb[i]])
                    tk.op("pool", lambda e, i=i: e.memset(St_b[:, i, :], 0.0), W=[St_bb[i]])
                tk.dma("sp", d_x, X[:], src_x[0:128, :], W=[X_b])
                def front(t):
                    t0 = t * 128
                    tk.op("act", lambda e: e.copy(xb2[:], X[:]), R=[X_b], W=[xb2_b])
                    transpose_to(xT, [xT_b], xb2, [xb2_b], 8, "act")
                    tk.dma("sp", d_cs, cs[:, 0, :], c_cos[:, t0:t0 + 128], W=[cs_b])
                    tk.dma("sp", d_cs, cs[:, 1, :], c_sin[:, t0:t0 + 128], W=[cs_b])

                    blk = 0
                    for j in range(5):
                        wv, wb, c = next_chunk(l, t, f"qa{j}")
                        nb_ = c["cw"] // 128
                        p = rot()
                        for cb in range(nb_):
                            fm_group(p, cb * 128, wv, wb, c, cb, xT, [xT_b])
                        tk.op("act", lambda e, p=p, nb_=nb_, blk=blk: e.copy(
                            qaT[:, blk:blk + nb_, :], PS[p][:, 0:nb_ * 128].rearrange("p (a b) -> p a b", b=128)),
                            R=[PS_b[p]], W=[qaT_b])
                        blk += nb_
                        yield
                    blk = 0
                    for j in range(5):
                        wv, wb, c = next_chunk(l, t, f"ka{j}")
                        nb_ = c["cw"] // 128
                        p = rot()
                        for cb in range(nb_):
                            fm_group(p, cb * 128, wv, wb, c, cb, xT, [xT_b])
                        for cb in range(nb_):
                            g, hp = divmod(blk + cb, 3)
                            s = t % NBLK[g]
                            tk.op("dve", lambda e, p=p, cb=cb, g=g, hp=hp, s=s: e.tensor_copy(
                                Kr[g][:, hp, s, :], PS[p][:, cb * 128:(cb + 1) * 128]),
                                R=[PS_b[p]], W=[Kr_b[g][s]])
                        blk += nb_
                        yield
                    for j in range(6):
                        wv, wb, c = next_chunk(l, t, f"va{j}")
                        g, hh = divmod(j, 2)
                        s = t % NBLK[g]
                        p = rot()
                        tm_group(p, 0, wv, wb, c, xT, [xT_b])
                        tk.op("act", lambda e, p=p, g=g, hh=hh, s=s: e.copy(
                            Vr[g][:, s, 3 * hh:3 * hh + 3, 0:64], PS[p][:, 0:192].rearrange("p (a b) -> p a b", b=64)),
                            R=[PS_b[p]], W=[Vr_b[g][s]])
                        yield
                def attn_core(t):
                    jobs = []
                    for h in range(6):
                        for g in range(3):
                            nvis = min(t + 1, NBLK[g])
                            o0 = 0
                            while o0 < nvis:
                                n_ = min(4, nvis - o0)
                                jobs.append((h, g, o0, n_))
                                o0 += n_
                    first_of_head = {}
                    last_of_head = {}
                    for idx, (h, g, o0, n_) in enumerate(jobs):
                        first_of_head.setdefault(h, idx)
                        last_of_head[h] = idx
                    Uv = PS[L2][:, 0:390].rearrange("p (a b) -> p a b", b=65)

                    def emit_scores(idx):
                        h, g, o0, n_ = jobs[idx]
                        hp, hs = divmod(h, 2)
                        p = rot()
                        for j in range(n_):
                            s = (t - (o0 + j)) % NBLK[g]
                            mm(PS[p][:, j * 128:(j + 1) * 128],
                               Kr[g][hs * 64:(hs + 1) * 64, hp, s, :], qaT[hs * 64:(hs + 1) * 64, 3 * g + hp, :],
                               True, True, R=[Kr_b[g][s], qaT_b], W=[PS_b[p]])
                        return p

                    def emit_soft(idx, p):
                        h, g, o0, n_ = jobs[idx]
                        w = n_ * 128
                        q = idx % 2
                        tk.op("act", lambda e: e.activation(out=pexp2[q][:, 0:w], in_=PS[p][:, 0:w], func=AF.Exp, scale=0.125),
                              R=[PS_b[p]], W=[pexp2_b[q]])
                        tk.op("dve", lambda e: e.tensor_tensor(
                            out=PT[q][:, 0:w], in0=pexp2[q][:, 0:w],
                            in1=Et[g][:, h, o0:o0 + n_, :].rearrange("p a b -> p (a b)"), op=ALU.mult),
                            R=[pexp2_b[q], Et_b], W=[PT_b[q]])

                    def emit_pv(idx):
                        h, g, o0, n_ = jobs[idx]
                        q = idx % 2
                        for j in range(n_):
                            s = (t - (o0 + j)) % NBLK[g]
                            first = (idx == first_of_head[h]) and j == 0
                            last = (idx == last_of_head[h]) and j == n_ - 1
                            mm(Uv[:, h, :], PT[q][:, j * 128:(j + 1) * 128], Vr[g][:, s, h, :], first, last,
                               R=[PT_b[q], Vr_b[g][s]], W=[PS_b[L2]], inc=last or (j == n_ - 1))
                            if not last and j == n_ - 1:
                                pass
                    pbank = {0: emit_scores(0)}
                    if len(jobs) > 1:
                        pbank[1] = emit_scores(1)
                    for idx in range(len(jobs)):
                        emit_soft(idx, pbank.pop(idx))
                        if idx + 2 < len(jobs):
                            pbank[idx + 2] = emit_scores(idx + 2)
                        emit_pv(idx)
                        yield
                    tk.op("dve", lambda e: e.reciprocal(rL[:, 0:6], Uv[:, :, 64:65].rearrange("p a b -> p (a b)")),
                          R=[PS_b[L2]], W=[rL_b])
                    tk.op("dve", lambda e: e.tensor_tensor(out=ya[:], in0=Uv[:, :, 0:64],
                                                           in1=rL[:, 0:6].unsqueeze(2).to_broadcast([128, 6, 64]), op=ALU.mult),
                          R=[PS_b[L2], rL_b], W=[ya_b])
                    transpose_to(yaT, [yaT_b], ya[:].rearrange("p a b -> p (a b)"), [ya_b], 3, "act")

                deferred = []

                def ret_proj(t):
                    def rotary(tag, dstT, dstT_b):
                        for bank in range(2):
                            p = rot()
                            for hh in range(2):
                                wv, wb, c = next_chunk(l, t, f"{tag}{bank * 2 + hh}")
                                for cb in range(2):
                                    fm_group(p, (hh * 2 + cb) * 128, wv, wb, c, cb, xT, [xT_b])
                                yield
                            pv = PS[p][:].rearrange("p (h f t) -> p h f t", h=2, f=2)
                            q1 = pv[:, :, 0, :]
                            q2 = pv[:, :, 1, :]
                            cosb = cs[:, 0:1, :].to_broadcast([128, 2, 128])
                            sinb = cs[:, 1:2, :].to_broadcast([128, 2, 128])
                            tav = ta[:, 0:256].rearrange("p (h t) -> p h t", h=2)
                            tbv = tb[:, 0:256].rearrange("p (h t) -> p h t", h=2)
                            dv = dstT[:, bank * 4:(bank + 1) * 4, :].rearrange("p (h f) t -> p h f t", f=2)
                            tk.op("dve", lambda e: e.tensor_tensor(out=tav, in0=q1, in1=cosb, op=ALU.mult),
                                  R=[PS_b[p], cs_b], W=[ta_b])
                            tk.op("dve", lambda e: e.tensor_tensor(out=tbv, in0=q2, in1=sinb, op=ALU.mult),
                                  R=[PS_b[p], cs_b], W=[tb_b])
                            tk.op("pool", lambda e: e.tensor_tensor(out=dv[:, :, 0, :], in0=tav, in1=tbv, op=ALU.subtract),
                                  R=[ta_b, tb_b], W=[dstT_b[bank]])
                            tk.op("dve", lambda e: e.tensor_tensor(out=tav, in0=q1, in1=sinb, op=ALU.mult),
                                  R=[PS_b[p], cs_b], W=[ta_b])
                            tk.op("dve", lambda e: e.tensor_tensor(out=tbv, in0=q2, in1=cosb, op=ALU.mult),
                                  R=[PS_b[p], cs_b], W=[tb_b])
                            tk.op("pool", lambda e: e.tensor_tensor(out=dv[:, :, 1, :], in0=tav, in1=tbv, op=ALU.add),
                                  R=[ta_b, tb_b], W=[dstT_b[bank]])
                    yield from rotary("qr", qrT, qrT_b)
                    for bank in range(2):
                        tk.op("pool", lambda e, bank=bank: e.tensor_tensor(
                            out=qtT[:, bank * 4:(bank + 1) * 4, :].rearrange("p (h f) t -> p h f t", f=2),
                            in0=qrT[:, bank * 4:(bank + 1) * 4, :].rearrange("p (h f) t -> p h f t", f=2),
                            in1=GQ[:, bank * 256:(bank + 1) * 256].rearrange("p (h t) -> p h t", h=2).unsqueeze(2).to_broadcast([128, 2, 2, 128]),
                            op=ALU.mult), R=[qrT_b[bank], const_b], W=[qtT_b[bank]])
                    yield from rotary("kr", krT, krT_b)
                    while deferred:
                        deferred.pop(0)()
                    for j in range(8):
                        wv, wb, c = next_chunk(l, t, f"vr{j}")
                        p = rot()
                        tm_group(p, 0, wv, wb, c, xT, [xT_b])
                        tk.op("act", lambda e, p=p, j=j: e.copy(vr[:, j * 256:(j + 1) * 256], PS[p][:, 0:256]),
                              R=[PS_b[p]], W=[vr_b[j]])
                        yield

                def ffn_steps(t):
                    def gu(j):
                        p = rot()
                        wv, wb, c = next_chunk(l, t, f"fg{j}")
                        for cb in range(2):
                            fm_group(p, cb * 128, wv, wb, c, cb, xmT, [xmT_b], bias=False)
                        wv, wb, c = next_chunk(l, t, f"fu{j}")
                        for cb in range(2):
                            fm_group(p, (2 + cb) * 128, wv, wb, c, cb, xmT, [xmT_b], bias=False)
                        tk.op("act", lambda e: e.activation(out=th[:, 0:256], in_=PS[p][:, 0:256], func=AF.Tanh, scale=0.5),
                              R=[PS_b[p]], W=[th_b])
                        tk.op("dve", lambda e: e.scalar_tensor_tensor(
                            out=th[:, 256:512], in0=th[:, 0:256], scalar=1.0, in1=PS[p][:, 0:256], op0=ALU.add, op1=ALU.mult),
                            R=[th_b, PS_b[p]], W=[th_b])
                        q = j % 2
                        tk.op("dve", lambda e: e.scalar_tensor_tensor(
                            out=hT[q][:].rearrange("p a b -> p (a b)"), in0=th[:, 256:512], scalar=0.5, in1=PS[p][:, 256:512],
                            op0=ALU.mult, op1=ALU.mult), R=[th_b, PS_b[p]], W=[hT_b[q]])

                    def down(j):
                        q = j % 2
                        wv, wb, c = next_chunk(l, t, f"fd{j}")
                        for hb, bank in enumerate((L0, L1)):
                            for k in range(2):
                                mm(PS[bank][:], hT[q][:, k, :], wv[:, k, hb * 512:(hb + 1) * 512],
                                   j == 0 and k == 0, j == 10 and k == 1, R=[hT_b[q], wb], W=[PS_b[bank]],
                                   inc=(k == 1 and hb == 1) or (j == 10 and k == 1))
                    gu(0)
                    yield
                    for j in range(11):
                        if j + 1 < 11:
                            gu(j + 1)
                        down(j)
                        yield

                def run(gen, mode=None):
                    rot_mode[0] = mode
                    try:
                        next(gen)
                        ok = True
                    except StopIteration:
                        ok = False
                    rot_mode[0] = None
                    return ok

                def chain(*gens):
                    for g_ in gens:
                        yield from g_

                for t in range(NT):
                    t0 = t * 128
                    if l + 1 < depth:
                        emit_conv(l + 1, len(plan) if t == NT - 1 else -(-len(plan) // NT))
                    if t == 0:
                        for _ in front(0):
                            pass
                        for _ in attn_core(0):
                            pass
                    tk.op("pool", lambda e: e.tensor_scalar(M[:], X[:], alpha, 0.0, ALU.mult, ALU.add), R=[X_b], W=[M_b])
                    if t + 1 < NT:
                        tk.dma("sp", d_x, X[:], src_x[t0 + 128:t0 + 256, :], W=[X_b])
                    for _ in ret_proj(t):
                        pass
                    for bank in range(2):
                        p = rot()
                        pv = PS[p][:].bitcast(BF16)
                        for j in range(4):
                            blk_ = bank * 4 + j
                            tk.op("pe", lambda e, j=j, blk_=blk_: e.transpose(pv[:, j * 128:(j + 1) * 128], krT[:, blk_, :], ident[:]),
                                  R=[krT_b[bank], const_b], W=[PS_b[p]])
                        tk.op("dve", lambda e, bank=bank, pv=pv: e.tensor_tensor(
                            out=ktm[:, bank * 512:(bank + 1) * 512].rearrange("p (h d) -> p h d", h=2),
                            in0=pv[:, 0:512].rearrange("p (h d) -> p h d", h=2),
                            in1=GK[:, bank * 2:bank * 2 + 2].unsqueeze(2).to_broadcast([128, 2, 256]), op=ALU.mult),
                            R=[PS_b[p], const_b], W=[ktm_b[bank]])
                    p = rot()
                    for h in range(4):
                        for f in range(2):
                            mm(PS[p][:, h * 128:(h + 1) * 128], krT[:, h * 2 + f, :], qrT[:, h * 2 + f, :], f == 0, f == 1,
                               R=[krT_b[h // 2], qrT_b[h // 2]], W=[PS_b[p]])
                    tk.op("dve", lambda e, p=p: e.tensor_tensor(out=scT[:], in0=PS[p][:], in1=DM[:], op=ALU.mult),
                          R=[PS_b[p], const_b], W=[scT_b])
                    def ret_stage1(h):
                        pg = rot()
                        for j in range(2):
                            wv, wb, c = next_chunk(l, t, f"gr{h * 2 + j}")
                            tm_group(pg, j * 256, wv, wb, c, xT, [xT_b])
                        tk.op("act", lambda e, pg=pg: e.activation(out=th[:], in_=PS[pg][:], func=AF.Tanh, scale=0.5),
                              R=[PS_b[pg]], W=[th_b])
                        sq = h % 2
                        tk.op("dve", lambda e, pg=pg, sq=sq: e.scalar_tensor_tensor(
                            out=sg[sq][:], in0=th[:], scalar=1.0, in1=PS[pg][:], op0=ALU.add, op1=ALU.mult),
                            R=[th_b, PS_b[pg]], W=[sg_b[sq]])
                        po = rot()
                        mm(PS[po][:], scT[:, h * 128:(h + 1) * 128], vr[:, h * 512:(h + 1) * 512], True, False,
                           R=[scT_b, vr_b[2 * h], vr_b[2 * h + 1]], W=[PS_b[po]])
                        for f in range(2):
                            mm(PS[po][:], qtT[:, h * 2 + f, :], St_b[:, h * 2 + f, :], False, f == 1,
                               R=[qtT_b[h // 2], St_bb[h * 2 + f]], W=[PS_b[po]])
                        tk.op("dve", lambda e, po=po: e.bn_stats(stats[:, 0, :], PS[po][:]), R=[PS_b[po]], W=[stats_b])
                        tk.op("dve", lambda e: e.bn_aggr(mv[:], stats[:, 0, :]), R=[stats_b], W=[mv_b])
                        tk.op("pool", lambda e: e.tensor_scalar(rstd[:], mv[:, 1:2], GN_EPS, None, ALU.add), R=[mv_b], W=[rstd_b])
                        tk.op("pool", lambda e: e.tensor_tensor(out=rstd[:], in0=rstd[:], in1=nhalf[:], op=ALU.pow),
                              R=[rstd_b, const_b], W=[rstd_b])
                        tk.op("dve", lambda e, po=po: e.tensor_scalar(gn[:], PS[po][:], mv[:, 0:1], rstd[:], ALU.subtract, ALU.mult),
                              R=[PS_b[po], mv_b, rstd_b], W=[gn_b])
                        tk.op("dve", lambda e, sq=sq: e.scalar_tensor_tensor(
                            out=yb[sq][:], in0=gn[:], scalar=0.5, in1=sg[sq][:], op0=ALU.mult, op1=ALU.mult),
                            R=[gn_b, sg_b[sq]], W=[yb_b[sq]])

                    def ret_stage2(h):
                        sq = h % 2
                        p = rot()
                        pv = PS[p][:].bitcast(BF16)
                        for j in range(4):
                            tk.op("pe", lambda e, j=j, sq=sq, pv=pv: e.transpose(pv[:, j * 128:(j + 1) * 128],
                                                                                 yb[sq][:, j * 128:(j + 1) * 128], ident[:]),
                                  R=[yb_b[sq], const_b], W=[PS_b[p]])
                        tk.op("act", lambda e, h=h, pv=pv: e.copy(ybT[:, 4 * h:4 * h + 4, :],
                                                                  pv[:, 0:512].rearrange("p (a b) -> p a b", b=128)),
                              R=[PS_b[p]], W=[ybT_b[h]])
                    ret_stage1(0)
                    for h in range(4):
                        if h + 1 < 4:
                            ret_stage1(h + 1)
                        ret_stage2(h)
                    for h in range(4):
                        for f in range(2):
                            i8 = h * 2 + f
                            p = rot()
                            mm(PS[p][:], ktm[:, i8 * 128:(i8 + 1) * 128], vr[:, h * 512:(h + 1) * 512], True, True,
                               R=[ktm_b[h // 2], vr_b[2 * h], vr_b[2 * h + 1]], W=[PS_b[p]])
                            tk.op("dve", lambda e, p=p, i8=i8, h=h: e.scalar_tensor_tensor(
                                out=St_f[:, i8, :], in0=St_f[:, i8, :], scalar=chunk_dec[h], in1=PS[p][:],
                                op0=ALU.mult, op1=ALU.add), R=[St_fb[i8], PS_b[p]], W=[St_fb[i8]])
                            tk.op("act", lambda e, i8=i8: e.copy(St_b[:, i8, :], St_f[:, i8, :]),
                                  R=[St_fb[i8]], W=[St_bb[i8]])

                    for half in range(2):
                        pA = rot()
                        for j in range(2):
                            wv, wb, c = next_chunk(l, t, f"ga{half}{j}")
                            for cb in range(2):
                                fm_group(pA, (j * 2 + cb) * 128, wv, wb, c, cb, xT, [xT_b])
                        tk.op("act", lambda e, pA=pA: e.activation(out=sgA[:], in_=PS[pA][:], func=AF.Tanh, scale=0.5),
                              R=[PS_b[pA]], W=[sgA_b])
                        tk.op("pool", lambda e: e.tensor_scalar(sgA[:], sgA[:], 0.5, 0.5, ALU.mult, ALU.add),
                              R=[sgA_b], W=[sgA_b])
                        pB = rot()
                        for j in range(2):
                            wv, wb, c = next_chunk(l, t, f"gb{half}{j}")
                            for cb in range(2):
                                fm_group(pB, (j * 2 + cb) * 128, wv, wb, c, cb, xT, [xT_b])
                        tk.op("act", lambda e, pB=pB: e.activation(out=sgB[:], in_=PS[pB][:], func=AF.Tanh, scale=0.5),
                              R=[PS_b[pB]], W=[sgB_b])
                        tk.op("pool", lambda e: e.tensor_scalar(sgB[:], sgB[:], 0.5, 0.5, ALU.mult, ALU.add),
                              R=[sgB_b], W=[sgB_b])
                        wv, wb, c = next_chunk(l, t, f"ap{half}")
                        pa = rot()
                        for cb in range(4):
                            fm_group(pa, cb * 128, wv, wb, c, cb, yaT, [yaT_b], bias=False)
                        tk.op("dve", lambda e, pa=pa: e.tensor_tensor(out=ta[:], in0=PS[pa][:], in1=sgA[:], op=ALU.mult),
                              R=[PS_b[pa], sgA_b], W=[ta_b])
                        pb_ = rot()
                        for j in range(4):
                            wv, wb, c = next_chunk(l, t, f"rp{half}{j}")
                            fm_group(pb_, j * 128, wv, wb, c, 0, ybT, ybT_b, bias=False)
                        tk.op("dve", lambda e, pb_=pb_: e.tensor_tensor(out=tb[:], in0=PS[pb_][:], in1=sgB[:], op=ALU.mult),
                              R=[PS_b[pb_], sgB_b], W=[tb_b])
                        tk.op("pool", lambda e, half=half: e.tensor_tensor(
                            out=mT[:, half * 4:(half + 1) * 4, :].rearrange("p a b -> p (a b)"), in0=ta[:], in1=tb[:], op=ALU.add),
                            R=[ta_b, tb_b], W=[mT_b[half]])

                    tk.dma("sp", d_ln, lnbc[:, 0, :], ln_in["ln1_g"][l:l + 1, :].to_broadcast([128, D]), W=[lnbc_b])
                    tk.dma("sp", d_ln, lnbc[:, 1, :], ln_in["ln1_b"][l:l + 1, :].to_broadcast([128, D]), W=[lnbc_b])
                    for j in range(4):
                        wv, wb, c = next_chunk(l, t, f"wo{j}")
                        bank = L0 if j < 2 else L1
                        tm_group(bank, (j % 2) * 256, wv, wb, c, mT, mT_b, bias=False)
                    for j, bank in enumerate((L0, L1)):
                        tk.op("dve", lambda e, j=j, bank=bank: e.tensor_tensor(
                            out=M[:, j * 512:(j + 1) * 512], in0=M[:, j * 512:(j + 1) * 512], in1=PS[bank][:], op=ALU.add),
                            R=[M_b, PS_b[bank]], W=[M_b])
                    layer_norm(M, M_b, LN_EPS)
                    gA = chain(front(t + 1), attn_core(t + 1)) if t + 1 < NT else iter(())
                    a_live = True
                    for _ in range(9):
                        a_live = a_live and run(gA, "A")
                    tk.op("act", lambda e: e.copy(xb[:], M[:]), R=[M_b], W=[xb_b])
                    transpose_to(xmT, [xmT_b], xb, [xb_b], 8, "dve")

                    tk.dma("sp", d_ln, lnbc[:, 0, :], ln_in["ln2_g"][l:l + 1, :].to_broadcast([128, D]), R=[M_b], W=[lnbc_b])
                    tk.dma("sp", d_ln, lnbc[:, 1, :], ln_in["ln2_b"][l:l + 1, :].to_broadcast([128, D]), R=[M_b], W=[lnbc_b])
                    gE = ffn_steps(t)
                    e_live = True
                    while e_live:
                        e_live = run(gE, "B")
                        for _ in range(5):
                            a_live = a_live and run(gA, "A")
                    while a_live:
                        a_live = run(gA, "A")
                    for j, bank in enumerate((L0, L1)):
                        tk.op("dve", lambda e, j=j, bank=bank: e.scalar_tensor_tensor(
                            out=O[:, j * 512:(j + 1) * 512], in0=M[:, j * 512:(j + 1) * 512], scalar=alpha, in1=PS[bank][:],
                            op0=ALU.mult, op1=ALU.add), R=[M_b, PS_b[bank]], W=[O_b])
                    def _ln2_store(t0=t0):
                        layer_norm(O, O_b, LN_EPS)
                        tk.dma("sp", d_st, dst_x[t0:t0 + 128, :], O[:], R=[O_b])
                    deferred.append(_ln2_store)
                    if t + 1 >= NT:
                        while deferred:
                            deferred.pop(0)()

        tk.dry = True
        emit_all()
        tk.dry = False
        conv_todo.update({l: ([] if l == 0 else list(range(len(plan)))) for l in range(depth)})
        rot_state[0] = 0
        rot_sub.update({"A": 0, "B": 0})
        for _ in range(NSLOT - 1):
            issue_next()
        emit_all()
        tk.drain("sp", d_st)
    return nc


_PROG_CACHE = {}


def _run(inputs, S, depth, n_cores):
    alpha = float((2 * depth) ** 0.25)
    consts, chunk_dec = make_consts(S)
    key = (S, depth)
    if key not in _PROG_CACHE:
        _PROG_CACHE[key] = build_program(S, depth, alpha, chunk_dec)
    nc = _PROG_CACHE[key]
    x = np.ascontiguousarray(np.asarray(inputs["x"], dtype=np.float32))
    shared = {k: np.ascontiguousarray(np.asarray(v, dtype=np.float32)) for k, v in inputs.items() if k != "x"}
    shared.update(consts)
    in_maps = []
    for c in range(n_cores):
        m = dict(shared)
        m["x"] = x[c]
        in_maps.append(m)
    res = run_bass_kernel_spmd(nc, in_maps, core_ids=list(range(n_cores)))
    return np.stack([np.asarray(r["y"], dtype=np.float32) for r in res.results], axis=0)


def kernel(**inputs):
    x = np.asarray(inputs["x"])
    B, S, _ = x.shape
    depth = int(np.asarray(inputs["w_in"]).shape[0])
    return _run(inputs, S, depth, B)
```
